# Optimizing a Trainium2 kernel written in Bass

```python
import math
import jax, jax.numpy as jnp
from jax import lax
import numpy as np

D_MODEL = 1024
BATCH = 2
SEQ = 16384
DEPTH = 2

N_EVEN = (DEPTH + 1) // 2
N_ODD = DEPTH // 2

M_WIDTH = D_MODEL // 2
M_HEADS = 4
M_HEAD_DIM = M_WIDTH // M_HEADS
M_CONV = 4
QKV_BLOCK = 4
M_CHUNK = 64
S_WIDTH = D_MODEL // 2
S_GROUP = 16
S_GROUPS = S_WIDTH // S_GROUP
S_STATE = 64
DT_MIN = 1e-3
DT_MAX = 1e-1
IN_EVEN = 2 * M_WIDTH + S_WIDTH
MIX_EVEN = M_WIDTH + S_WIDTH
H_EXPAND = 128
H_HEADS = D_MODEL // H_EXPAND
H_WIDTH = H_HEADS * H_EXPAND
H_CHUNK = 64
IN_ODD = 4 * H_WIDTH
D_FF = -(-8 * D_MODEL // (3 * 256)) * 256
EPS = 1e-6

kernel_name = 'hybrid_mlstm_s5_hgrn2_trunk'


def rmsnorm(x, g):
    xf = x.astype(jnp.float32)
    return xf * lax.rsqrt(jnp.mean(xf * xf, axis=-1, keepdims=True) + EPS) * g


def causal_dwconv(x, w, b):
    K, Cc = w.shape
    y = lax.conv_general_dilated(x, w[:, None, :].astype(x.dtype), window_strides=(1,), padding=[(K - 1, 0)],
                                 dimension_numbers=('NWC', 'WIO', 'NWC'), feature_group_count=Cc)
    return y + b


def swiglu(x, w1, w3, w2):
    return (jax.nn.silu(x @ w1) * (x @ w3)) @ w2


def mlstm_chunkwise(q, k, v, i_pre, f_pre):
    f32 = jnp.float32
    Bsz, H, L, Dh = q.shape
    C = M_CHUNK
    NC = L // C
    q, k, v = (t.astype(f32).reshape(Bsz, H, NC, C, Dh) for t in (q, k, v))
    log_f = jax.nn.log_sigmoid(f_pre.astype(f32)).reshape(Bsz, H, NC, C)
    log_i = i_pre.astype(f32).reshape(Bsz, H, NC, C)
    b = jnp.cumsum(log_f, axis=-1)
    b_last = b[..., -1]
    causal = jnp.tril(jnp.ones((C, C), dtype=bool))
    d_log = jnp.where(causal, b[..., :, None] - b[..., None, :] + log_i[..., None, :], -jnp.inf)
    m_intra = jnp.max(d_log, axis=-1)
    a = b_last[..., None] - b + log_i
    a_max = jnp.max(a, axis=-1)
    kw = k * jnp.exp(a - a_max[..., None])[..., None]
    U = jnp.einsum('bhcsk,bhcsv->bhckv', kw, v)
    un = jnp.sum(kw, axis=-2)

    def step(carry, inp):
        c_st, n_st, m_st = carry
        u_c, un_c, am_c, bl_c = inp
        m_new = jnp.maximum(bl_c + m_st, am_c)
        s_old = jnp.exp(bl_c + m_st - m_new)
        s_new = jnp.exp(am_c - m_new)
        c_new = s_old[..., None, None] * c_st + s_new[..., None, None] * u_c
        n_new = s_old[..., None] * n_st + s_new[..., None] * un_c
        return (c_new, n_new, m_new), (c_st, n_st, m_st)

    init = (jnp.zeros((Bsz, H, Dh, Dh), f32), jnp.zeros((Bsz, H, Dh), f32), jnp.zeros((Bsz, H), f32))
    xs = (jnp.moveaxis(U, 2, 0), jnp.moveaxis(un, 2, 0), jnp.moveaxis(a_max, 2, 0), jnp.moveaxis(b_last, 2, 0))
    _, (c_prev, n_prev, m_prev) = lax.scan(step, init, xs)
    c_prev = jnp.moveaxis(c_prev, 0, 2)
    n_prev = jnp.moveaxis(n_prev, 0, 2)
    m_prev = jnp.moveaxis(m_prev, 0, 2)
    inter_log = b + m_prev[..., None]
    m_t = jnp.maximum(inter_log, m_intra)
    s_inter = jnp.exp(inter_log - m_t)
    scores = jnp.einsum('bhctd,bhcsd->bhcts', q, k) * jnp.exp(d_log - m_t[..., None])
    num = jnp.einsum('bhcts,bhcsv->bhctv', scores, v) + s_inter[..., None] * jnp.einsum('bhctk,bhckv->bhctv', q, c_prev)
    den = jnp.sum(scores, axis=-1) + s_inter * jnp.einsum('bhctk,bhck->bhct', q, n_prev)
    h = num / jnp.maximum(jnp.abs(den), jnp.exp(-m_t))[..., None]
    return h.reshape(Bsz, H, L, Dh)


def _complex_affine_combine(e1, e2):
    a1r, a1i, b1r, b1i = e1
    a2r, a2i, b2r, b2i = e2
    return (a2r * a1r - a2i * a1i, a2r * a1i + a2i * a1r,
            a2r * b1r - a2i * b1i + b2r, a2r * b1i + a2i * b1r + b2i)


def s5_ssm(u, a_re, a_im, log_dt, b_re, b_im, c_re, c_im, d_skip):
    f32 = jnp.float32
    Bsz, L, _ = u.shape
    ug = u.astype(f32).reshape(Bsz, L, S_GROUPS, S_GROUP)
    a_re, a_im, b_re, b_im, c_re, c_im = (t.astype(f32) for t in (a_re, a_im, b_re, b_im, c_re, c_im))
    dt = jnp.exp(log_dt.astype(f32))[:, None]
    mag = jnp.exp(a_re * dt)
    ab_re = mag * jnp.cos(a_im * dt)
    ab_im = mag * jnp.sin(a_im * dt)
    inv = 1.0 / (a_re * a_re + a_im * a_im)
    g_re = ((ab_re - 1.0) * a_re + ab_im * a_im) * inv
    g_im = (ab_im * a_re - (ab_re - 1.0) * a_im) * inv
    bb_re = g_re[..., None] * b_re - g_im[..., None] * b_im
    bb_im = g_re[..., None] * b_im + g_im[..., None] * b_re
    bu_re = jnp.einsum('blgp,gnp->blgn', ug, bb_re)
    bu_im = jnp.einsum('blgp,gnp->blgn', ug, bb_im)
    shp = (1, L, S_GROUPS, S_STATE)
    a_seq_re = jnp.broadcast_to(ab_re, shp)
    a_seq_im = jnp.broadcast_to(ab_im, shp)
    _, _, x_re, x_im = lax.associative_scan(_complex_affine_combine, (a_seq_re, a_seq_im, bu_re, bu_im), axis=1)
    y = jnp.einsum('blgn,gpn->blgp', x_re, c_re) - jnp.einsum('blgn,gpn->blgp', x_im, c_im)
    y = y + d_skip.astype(f32).reshape(S_GROUPS, S_GROUP) * ug
    return y.reshape(Bsz, L, S_WIDTH)


def mixer_ab(x, w_in, conv_w, conv_b, wq, wk, wv, w_if, b_if, mh_gain, skip,
             a_re, a_im, log_dt, b_re, b_im, c_re, c_im, d_skip, w_glu, b_glu, w_out):
    Bsz, L, _ = x.shape
    proj = x @ w_in
    xm = proj[..., :M_WIDTH]
    zm = proj[..., M_WIDTH:2 * M_WIDTH]
    us = proj[..., 2 * M_WIDTH:]
    xc = jax.nn.silu(causal_dwconv(xm, conv_w, conv_b))
    nb = M_WIDTH // QKV_BLOCK
    xc_b = xc.reshape(Bsz, L, nb, QKV_BLOCK)
    xm_b = xm.reshape(Bsz, L, nb, QKV_BLOCK)
    q = jnp.einsum('blnd,nde->blne', xc_b, wq).reshape(Bsz, L, M_WIDTH)
    k = jnp.einsum('blnd,nde->blne', xc_b, wk).reshape(Bsz, L, M_WIDTH) * (M_HEAD_DIM ** -0.5)
    v = jnp.einsum('blnd,nde->blne', xm_b, wv).reshape(Bsz, L, M_WIDTH)
    gates = jnp.concatenate([q, k, v], axis=-1) @ w_if + b_if
    to_h = lambda t: t.reshape(Bsz, L, M_HEADS, M_HEAD_DIM).transpose(0, 2, 1, 3)
    h = mlstm_chunkwise(to_h(q), to_h(k), to_h(v),
                        gates[..., :M_HEADS].transpose(0, 2, 1), gates[..., M_HEADS:].transpose(0, 2, 1))
    mu = jnp.mean(h, axis=-1, keepdims=True)
    var = jnp.mean(jnp.square(h - mu), axis=-1, keepdims=True)
    hn = ((h - mu) * lax.rsqrt(var + EPS)).transpose(0, 2, 1, 3).reshape(Bsz, L, M_WIDTH) * mh_gain
    out_m = (hn + skip * xc) * jax.nn.silu(zm)
    y = jax.nn.gelu(s5_ssm(us, a_re, a_im, log_dt, b_re, b_im, c_re, c_im, d_skip))
    out_s = y * jax.nn.sigmoid(y @ w_glu + b_glu)
    return jnp.concatenate([out_m, out_s], axis=-1) @ w_out


def hgrn2_chunkwise(q, k, v, log_f):
    f32 = jnp.float32
    Bsz, H, L, E = q.shape
    C = H_CHUNK
    NC = L // C
    q, k, v, log_f = (t.astype(f32).reshape(Bsz, H, NC, C, E) for t in (q, k, v, log_f))
    b = jnp.cumsum(log_f, axis=3)
    b_last = b[:, :, :, -1:, :]
    q_inter = q * jnp.exp(b)
    k_state = k * jnp.exp(b_last - b)
    q_intra = q * jnp.exp(b - b_last)
    causal = jnp.tril(jnp.ones((C, C), dtype=bool))
    scores = jnp.where(causal, jnp.einsum('bhctk,bhcsk->bhcts', q_intra, k_state), 0.0)
    intra = jnp.einsum('bhcts,bhcsv->bhctv', scores, v)
    U = jnp.einsum('bhcsk,bhcsv->bhckv', k_state, v)
    decay = jnp.exp(b_last[:, :, :, 0, :])

    def step(s, inp):
        u_c, d_c = inp
        return d_c[..., None] * s + u_c, s

    _, s_prev = lax.scan(step, jnp.zeros((Bsz, H, E, E), f32), (jnp.moveaxis(U, 2, 0), jnp.moveaxis(decay, 2, 0)))
    s_prev = jnp.moveaxis(s_prev, 0, 2)
    inter = jnp.einsum('bhctk,bhckv->bhctv', q_inter, s_prev)
    return (intra + inter).reshape(Bsz, H, L, E)


def mixer_c(x, layer_idx, w_in, lb_raw, g_gain, w_out):
    Bsz, L, _ = x.shape
    proj = x @ w_in
    q = jax.nn.silu(proj[..., :H_WIDTH])
    f = proj[..., H_WIDTH:2 * H_WIDTH].astype(jnp.float32)
    i = proj[..., 2 * H_WIDTH:3 * H_WIDTH]
    g = proj[..., 3 * H_WIDTH:]
    lbs = jnp.cumsum(jax.nn.softmax(lb_raw.astype(jnp.float32), axis=0), axis=0)
    lb = lbs[layer_idx] - lbs[0]
    fg = lb + (1.0 - lb) * jax.nn.sigmoid(f)
    to_h = lambda t: t.reshape(Bsz, L, H_HEADS, H_EXPAND).transpose(0, 2, 1, 3)
    o = hgrn2_chunkwise(to_h(q), to_h(1.0 - fg), to_h(i), to_h(jnp.log(fg)))
    o = o * lax.rsqrt(jnp.mean(o * o, axis=-1, keepdims=True) + EPS)
    o = o.transpose(0, 2, 1, 3).reshape(Bsz, L, H_WIDTH) * g_gain * jax.nn.silu(g)
    return o @ w_out


def setup_inputs(seed: int = 0) -> dict:
    key = jax.random.key(seed)
    ks = iter(jax.random.split(key, 48))
    f32 = jnp.float32
    nrm = lambda shape, scale: scale * jax.random.normal(next(ks), shape, f32)
    NE, NO = N_EVEN, N_ODD
    nb = M_WIDTH // QKV_BLOCK
    b_if = jnp.concatenate([nrm((NE, M_HEADS), 0.1),
                            jnp.broadcast_to(jnp.linspace(3.0, 6.0, M_HEADS, dtype=f32), (NE, M_HEADS)) + nrm((NE, M_HEADS), 0.01)], axis=-1)
    return {
        'x': nrm((BATCH, SEQ, D_MODEL), 1.0),
        'norm_g': 1.0 + nrm((DEPTH, 4, D_MODEL), 0.05),
        'ab_w_in': nrm((NE, D_MODEL, IN_EVEN), D_MODEL ** -0.5),
        'ab_conv_w': nrm((NE, M_CONV, M_WIDTH), M_CONV ** -0.5),
        'ab_conv_b': nrm((NE, M_WIDTH), 0.01),
        'ab_wq': nrm((NE, nb, QKV_BLOCK, QKV_BLOCK), QKV_BLOCK ** -0.5),
        'ab_wk': nrm((NE, nb, QKV_BLOCK, QKV_BLOCK), QKV_BLOCK ** -0.5),
        'ab_wv': nrm((NE, nb, QKV_BLOCK, QKV_BLOCK), QKV_BLOCK ** -0.5),
        'ab_w_if': nrm((NE, 3 * M_WIDTH, 2 * M_HEADS), 0.1 * (3 * M_WIDTH) ** -0.5),
        'ab_b_if': b_if,
        'ab_mh_gain': 1.0 + nrm((NE, M_WIDTH), 0.05),
        'ab_skip': 1.0 + nrm((NE, M_WIDTH), 0.05),
        'ab_a_re': -0.5 + nrm((NE, S_GROUPS, S_STATE), 0.01),
        'ab_a_im': jnp.pi * jnp.arange(S_STATE, dtype=f32) + nrm((NE, S_GROUPS, S_STATE), 0.01),
        'ab_log_dt': jax.random.uniform(next(ks), (NE, S_GROUPS), f32, math.log(DT_MIN), math.log(DT_MAX)),
        'ab_b_re': nrm((NE, S_GROUPS, S_STATE, S_GROUP), (2 * S_GROUP) ** -0.5),
        'ab_b_im': nrm((NE, S_GROUPS, S_STATE, S_GROUP), (2 * S_GROUP) ** -0.5),
        'ab_c_re': nrm((NE, S_GROUPS, S_GROUP, S_STATE), S_STATE ** -0.5),
        'ab_c_im': nrm((NE, S_GROUPS, S_GROUP, S_STATE), S_STATE ** -0.5),
        'ab_d': nrm((NE, S_WIDTH), 1.0),
        'ab_w_glu': nrm((NE, S_WIDTH, S_WIDTH), S_WIDTH ** -0.5),
        'ab_b_glu': nrm((NE, S_WIDTH), 0.01),
        'ab_w_out': nrm((NE, MIX_EVEN, D_MODEL), MIX_EVEN ** -0.5),
        'c_w_in': nrm((NO, D_MODEL, IN_ODD), D_MODEL ** -0.5),
        'c_lb_raw': nrm((DEPTH, H_WIDTH), 0.1),
        'c_g_gain': 1.0 + nrm((NO, H_WIDTH), 0.05),
        'c_w_out': nrm((NO, H_WIDTH, D_MODEL), H_WIDTH ** -0.5),
        'ffn_w1': nrm((DEPTH, D_MODEL, D_FF), D_MODEL ** -0.5),
        'ffn_w3': nrm((DEPTH, D_MODEL, D_FF), D_MODEL ** -0.5),
        'ffn_w2': nrm((DEPTH, D_FF, D_MODEL), D_FF ** -0.5),
    }


def reference(x, norm_g, ab_w_in, ab_conv_w, ab_conv_b, ab_wq, ab_wk, ab_wv, ab_w_if, ab_b_if, ab_mh_gain, ab_skip,
              ab_a_re, ab_a_im, ab_log_dt, ab_b_re, ab_b_im, ab_c_re, ab_c_im, ab_d, ab_w_glu, ab_b_glu, ab_w_out,
              c_w_in, c_lb_raw, c_g_gain, c_w_out, ffn_w1, ffn_w3, ffn_w2):
    for layer in range(DEPTH):
        j = layer // 2
        h = rmsnorm(x, norm_g[layer, 0])
        if layer % 2 == 0:
            mix = mixer_ab(h, ab_w_in[j], ab_conv_w[j], ab_conv_b[j], ab_wq[j], ab_wk[j], ab_wv[j], ab_w_if[j],
                           ab_b_if[j], ab_mh_gain[j], ab_skip[j], ab_a_re[j], ab_a_im[j], ab_log_dt[j], ab_b_re[j],
                           ab_b_im[j], ab_c_re[j], ab_c_im[j], ab_d[j], ab_w_glu[j], ab_b_glu[j], ab_w_out[j])
        else:
            mix = mixer_c(h, layer, c_w_in[j], c_lb_raw, c_g_gain[j], c_w_out[j])
        x = x + rmsnorm(mix, norm_g[layer, 1])
        h = rmsnorm(x, norm_g[layer, 2])
        x = x + rmsnorm(swiglu(h, ffn_w1[layer], ffn_w3[layer], ffn_w2[layer]), norm_g[layer, 3])
    return x
```

```python
import contextlib
import numpy as np
import concourse.bass as bass
import concourse.mybir as mybir
from concourse.bass_utils import run_bass_kernel_spmd

F32 = mybir.dt.float32
BF16 = mybir.dt.bfloat16
I32 = mybir.dt.int32
AF = mybir.ActivationFunctionType
ALU = mybir.AluOpType
PI = float(np.pi)
EPS = 1e-6

D = 1024
DFF = 2816
NT = 256
NB = NT // 128
NC64 = NT // 64
SEQ = 16384
BATCH = 2

COMPUTE = ("pe", "act", "dve", "pool")
DEBUG_TAGS = False
STAGES = {"l0", "f0", "l1", "f1"}


class Buf:
    __slots__ = ("last_w", "readers", "serial")

    def __init__(self, serial=False):
        self.last_w = None
        self.readers = {}
        self.serial = serial


class Op:
    __slots__ = ("eng", "fn", "deps", "sig", "is_dma", "sem", "val", "prev_same_sem", "tag")

    def __init__(self, eng, fn, is_dma):
        self.tag = None
        if DEBUG_TAGS:
            import sys as _s
            f = _s._getframe(2)
            ls = []
            while f is not None and len(ls) < 5:
                ls.append(str(f.f_lineno))
                f = f.f_back
            self.tag = ">".join(ls)
        self.eng = eng
        self.fn = fn
        self.is_dma = is_dma
        self.deps = []
        self.sig = False
        self.sem = None
        self.val = 0
        self.prev_same_sem = None


class TL:
    __slots__ = ("a", "b")

    def __init__(self, a, b):
        self.a = a
        self.b = b


class Rot:
    def __init__(self, tiles):
        self.tiles = tiles
        self.i = 0

    def get(self):
        t = self.tiles[self.i % len(self.tiles)]
        self.i += 1
        return t


class Prog:
    def __init__(self, nc, n_dma_sems=8):
        self.nc = nc
        self.ops = {e: [] for e in ("sp", "act", "dve", "pool", "pe")}
        self.n_dma_sems = n_dma_sems
        self.dma_count = {"sp": 0, "act": 0, "pool": 0}
        self.dma_last = {}
        self.stack = contextlib.ExitStack()
        self.nalloc = 0

    def tile(self, shape, dtype, name=None):
        self.nalloc += 1
        t = self.stack.enter_context(self.nc.sbuf_tensor("sb_" + (name or f"t{self.nalloc}"), list(shape), dtype))
        return TL(t[:], Buf())

    def psum_bank(self):
        self.nalloc += 1
        t = self.stack.enter_context(self.nc.psum_tensor(f"ps{self.nalloc}", [128, 512], F32))
        return TL(t[:], Buf(serial=True))

    def _record(self, op, reads, writes):
        deps = {}
        for b in reads:
            if b.last_w is not None:
                deps[id(b.last_w)] = b.last_w
            if b.serial:
                for r in b.readers.values():
                    deps[id(r)] = r
        for b in writes:
            if b.last_w is not None:
                deps[id(b.last_w)] = b.last_w
            for r in b.readers.values():
                deps[id(r)] = r
        for d in deps.values():
            if d is op:
                continue
            if (not d.is_dma) and (not op.is_dma) and d.eng == op.eng and op.eng == "pe":
                continue
            op.deps.append(d)
            d.sig = True
        for b in writes:
            b.last_w = op
            b.readers = {}
        for b in reads:
            if b.last_w is op:
                continue
            key = ("dma", id(op)) if op.is_dma else op.eng
            b.readers[key] = op

    def op(self, eng, fn, reads=(), writes=()):
        o = Op(eng, fn, False)
        self._record(o, [t.b if isinstance(t, TL) else t for t in reads], [t.b if isinstance(t, TL) else t for t in writes])
        self.ops[eng].append(o)
        return o

    def dma(self, queue, fn, reads=(), writes=()):
        o = Op(queue, fn, True)
        k = (queue, self.dma_count[queue] % self.n_dma_sems)
        self.dma_count[queue] += 1
        o.sem = k
        o.prev_same_sem = self.dma_last.get(k)
        o.val = (o.prev_same_sem.val if o.prev_same_sem else 0) + 16
        self.dma_last[k] = o
        o.sig = True
        self._record(o, [t.b if isinstance(t, TL) else t for t in reads], [t.b if isinstance(t, TL) else t for t in writes])
        if o.prev_same_sem is not None:
            o.deps.append(o.prev_same_sem)
        self.ops[queue].append(o)
        return o

    def emit(self, final_wait_ops=()):
        nc = self.nc
        st = self.stack
        dma_sems = {(q, i): st.enter_context(nc.semaphore(f"s_dma_{q}{i}")) for q in ("sp", "pool") for i in range(self.n_dma_sems)}
        eng_sems = {}
        SEM_LIMIT = 20000
        self.sig_counts = {}
        for e in COMPUTE:
            cnt = 0
            epoch = 0
            total = 0
            for o in self.ops[e]:
                if o.is_dma:
                    continue
                if o.sig:
                    if cnt >= SEM_LIMIT:
                        epoch += 1
                        cnt = 0
                    cnt += 1
                    total += 1
                    key = (e, epoch)
                    if key not in eng_sems:
                        eng_sems[key] = st.enter_context(nc.semaphore(f"s_{e}{epoch}"))
                    o.sem = key
                    o.val = cnt
            self.sig_counts[e] = total
        block = st.enter_context(nc.Block())

        def run(ename, engobj):
            waited = {}
            for o in self.ops[ename]:
                for d in o.deps:
                    key = ("dma", d.sem) if d.is_dma else d.sem
                    if waited.get(key, 0) >= d.val:
                        continue
                    waited[key] = d.val
                    s = dma_sems[d.sem] if d.is_dma else eng_sems[d.sem]
                    engobj.wait_ge(s, d.val)
                ins = o.fn(engobj)
                if o.tag is not None:
                    ins.annotate(o.tag)
                if o.sig:
                    if o.is_dma:
                        ins.then_inc(dma_sems[o.sem], 16)
                    else:
                        ins.then_inc(eng_sems[o.sem], 1)
            if ename == "sp":
                for d in final_wait_ops:
                    s = dma_sems[d.sem] if d.is_dma else eng_sems[d.sem]
                    engobj.wait_ge(s, d.val)

        @block.sync
        def _(e):
            run("sp", e)

        @block.scalar
        def _(e):
            run("act", e)

        @block.vector
        def _(e):
            run("dve", e)

        @block.gpsimd
        def _(e):
            run("pool", e)

        @block.tensor
        def _(e):
            run("pe", e)

    def close(self):
        self.stack.close()


class Cols:
    def __init__(self):
        self.off = {}
        self.n = 0

    def add(self, name, w):
        self.off[name] = (self.n, w)
        self.n += w


PRM = Cols()
for _n, _w in [("ng", 64), ("convw", 16), ("convb", 4), ("wq", 16), ("wk", 16), ("wv", 16), ("wif", 96), ("bif", 2),
               ("mhg", 4), ("skip", 4), ("are", 16), ("aim", 16), ("ldt", 16), ("bre", 256), ("bim", 256),
               ("cre", 256), ("cim", 256), ("dsk", 4), ("bglu", 4), ("lbraw", 16), ("ggain", 8)]:
    PRM.add(_n, _w)

CST = Cols()
for _n, _w in [("ident", 128), ("ones", 128), ("onesdiv", 128), ("negmask", 128), ("mask64", 64), ("selneg", 512),
               ("r128", NT), ("r64", NT), ("tvec", 128), ("bdmask", 128), ("maskB", 32), ("maskC", 8)]:
    CST.add(_n, _w)

NSTATE = 4 * 129 + 32 + 8 * 128 + 12


def fm(v, nch):
    return np.ascontiguousarray(np.asarray(v, np.float32).reshape(nch, 128).T)


def pack_params(inp):
    prm = np.zeros((128, PRM.n), np.float32)

    def put(name, arr):
        o, w = PRM.off[name]
        arr = np.asarray(arr, np.float32)
        assert arr.shape[1] == w, (name, arr.shape, w)
        prm[:arr.shape[0], o:o + w] = arr

    ng = np.asarray(inp["norm_g"], np.float32)
    put("ng", np.concatenate([fm(ng[l, j], 8) for l in range(2) for j in range(4)], axis=1))
    cw = np.asarray(inp["ab_conv_w"], np.float32)[0]
    put("convw", cw.reshape(4, 4, 128).transpose(2, 1, 0).reshape(128, 16))
    put("convb", fm(inp["ab_conv_b"][0], 4))
    for nm, key in (("wq", "ab_wq"), ("wk", "ab_wk"), ("wv", "ab_wv")):
        w = np.asarray(inp[key], np.float32)[0]
        w = w.reshape(4, 32, 4, 4)
        put(nm, w.transpose(1, 2, 0, 3).reshape(128, 16))
    wif = np.asarray(inp["ab_w_if"], np.float32)[0]
    put("wif", wif.reshape(12, 128, 8).transpose(1, 0, 2).reshape(128, 96))
    bif = np.asarray(inp["ab_b_if"], np.float32)[0]
    put("bif", np.stack([bif[0:4], bif[4:8]], axis=1))
    put("mhg", fm(inp["ab_mh_gain"][0], 4))
    put("skip", fm(inp["ab_skip"][0], 4))

    def sm(a):
        return np.asarray(a, np.float32).reshape(16, 2, 64).transpose(1, 2, 0).reshape(128, 16)

    put("are", sm(inp["ab_a_re"][0]))
    put("aim", sm(inp["ab_a_im"][0]))
    put("ldt", sm(np.repeat(np.asarray(inp["ab_log_dt"], np.float32)[0][:, None], 64, axis=1)))
    for nm, key in (("bre", "ab_b_re"), ("bim", "ab_b_im")):
        b = np.asarray(inp[key], np.float32)[0]
        put(nm, b.reshape(16, 2, 64, 16).transpose(1, 2, 0, 3).reshape(128, 256))
    for nm, key in (("cre", "ab_c_re"), ("cim", "ab_c_im")):
        c = np.asarray(inp[key], np.float32)[0]
        put(nm, c.reshape(4, 128, 64).transpose(1, 0, 2).reshape(128, 256))
    put("dsk", fm(inp["ab_d"][0], 4))
    put("bglu", fm(inp["ab_b_glu"][0], 4))
    lb = np.asarray(inp["c_lb_raw"], np.float32)
    put("lbraw", np.concatenate([fm(lb[0], 8), fm(lb[1], 8)], axis=1))
    put("ggain", fm(inp["c_g_gain"][0], 8))
    return prm


def make_consts():
    c = np.zeros((128, CST.n), np.float32)

    def put(name, arr):
        o, w = CST.off[name]
        arr = np.asarray(arr, np.float32)
        assert arr.shape[1] == w
        c[:arr.shape[0], o:o + w] = arr

    put("ident", np.eye(128))
    put("ones", np.ones((128, 128)))
    put("onesdiv", np.full((128, 128), 1.0 / 128))
    s = np.arange(128)[:, None]
    t = np.arange(128)[None, :]
    put("negmask", np.where(s <= t, 0.0, -30000.0))
    put("mask64", (s[:64] <= t[:, :64]).astype(np.float32))
    sel = np.zeros((4, 4, 128), np.float32)
    for r in range(4):
        sel[r, r, :] = -1.0
    put("selneg", sel.reshape(4, 512))
    tt = np.arange(NT)
    put("r128", np.tile((tt % 128 != 0).astype(np.float32)[None, :], (128, 1)))
    put("r64", np.tile((tt % 64 != 0).astype(np.float32)[None, :], (128, 1)))
    put("tvec", np.tile(np.arange(1, 129, dtype=np.float32)[None, :], (128, 1)))
    p = np.arange(128)
    put("bdmask", (p[:, None] // 4 == p[None, :] // 4).astype(np.float32))
    mb = np.zeros((128, 4, 8), np.float32)
    for e in range(2):
        for v in range(4):
            mb[e * 64:(e + 1) * 64, v, 2 * v + e] = 1.0
    put("maskB", mb.reshape(128, 32))
    mc = np.zeros((128, 4, 2), np.float32)
    for gl in range(8):
        for v in range(4):
            for e in range(2):
                if gl == 2 * v + e:
                    mc[gl * 16:(gl + 1) * 16, v, e] = 1.0
    put("maskC", mc.reshape(128, 8))
    return c


def build_program(n_tiles, with_state=True):
    nc = bass.Bass("TRN2", target_bir_lowering=False)
    T = n_tiles * NT

    def din(name, shape):
        return nc.dram_tensor(name, list(shape), F32, kind="ExternalInput").ap()

    x_d = din("x", [T, D])
    prm_d = din("prm", [128, PRM.n])
    cst_d = din("cst", [128, CST.n])
    st_in = din("st_in", [128, NSTATE])
    w_in0 = din("ab_w_in", [D, 1536])
    w_glu = din("ab_w_glu", [512, 512])
    w_out0 = din("ab_w_out", [D, D])
    w_in1 = din("c_w_in", [D, 4096])
    w_out1 = din("c_w_out", [D, D])
    w1_d = din("ffn_w1", [2, D, DFF])
    w3_d = din("ffn_w3", [2, D, DFF])
    w2_d = din("ffn_w2", [2, DFF, D])
    y_d = nc.dram_tensor("y", [T, D], F32, kind="ExternalOutput").ap()
    st_out = nc.dram_tensor("st_out", [128, NSTATE], F32, kind="ExternalOutput").ap()

    P = Prog(nc)
    op = P.op

    prm = P.tile([128, PRM.n], F32, "prm")
    cst = P.tile([128, CST.n], F32, "cst")

    def pc(name, a=None, b=None):
        o, w = PRM.off[name]
        if a is None:
            return prm.a[:, o:o + w]
        return prm.a[:, o + a:o + b]

    def cc(name, a=None, b=None):
        o, w = CST.off[name]
        if a is None:
            return cst.a[:, o:o + w]
        return cst.a[:, o + a:o + b]

    ident_f = cc("ident")
    ident_bf = P.tile([128, 128], BF16, "ident_bf")
    ones_bf = P.tile([128, 128], BF16, "ones_bf")
    BD = P.tile([128, 3, 4, 128], BF16, "BD")
    wif_bf = P.tile([128, 96], BF16, "wif_bf")
    nbf = P.tile([4, 1], F32, "nbf")
    BBT = P.tile([128, 16, 2, 128], BF16, "BBT")
    CT = P.tile([128, 16, 2, 128], BF16, "CT")
    Ec = P.tile([128, 2048], F32, "Ec")
    Es = P.tile([128, 2048], F32, "Es")
    Rfull = P.tile([128, 2048], F32, "Rfull")
    rcol = P.tile([128, 16], F32, "rcol")
    lbt = P.tile([128, 8], F32, "lbt")
    omlt = P.tile([128, 8], F32, "omlt")
    Cn = P.tile([128, 4, 129], F32, "Cn")
    Cbf = P.tile([128, 4, 128], BF16, "Cbf")
    Nrep = P.tile([128, 4, 128], BF16, "Nrep")
    rx = P.tile([128, 2, 16], F32, "rx")
    S = P.tile([128, 8, 128], F32, "S")
    Sbf = P.tile([128, 8, 128], BF16, "Sbf")
    xio = P.tile([128, NB, D], F32, "xio")
    xT = P.tile([128, 8, NT], F32, "xT")
    hT = P.tile([128, 8, NT], BF16, "hT")
    mix = P.tile([128, 8, NT], F32, "mix")
    cat = P.tile([128, 8, NT], BF16, "cat")
    wbufs = Rot([P.tile([128, 5632], BF16, f"wb{i}") for i in range(2)])
    scrF = Rot([P.tile([128, NT], F32, f"sF{i}") for i in range(12)])
    scrB = Rot([P.tile([128, NT], BF16, f"sB{i}") for i in range(6)])
    scrS = Rot([P.tile([128, 129], BF16, f"sS{i}") for i in range(6)])
    rotP = Rot([P.psum_bank() for _ in range(6)])
    accP = Rot([P.psum_bank() for _ in range(2)])

    ARENA_BYTES = 64 * 1024
    arena = P.tile([128, ARENA_BYTES // 4], F32, "arena")
    phase_buf = Buf()

    class Carver:
        def __init__(self):
            self.off = 0

        def carve(self, shape, dtype):
            n = int(np.prod(shape[1:]))
            isz = 4 if dtype in (F32, I32) else 2
            nbytes = (n * isz + 31) // 32 * 32
            assert self.off + nbytes <= ARENA_BYTES, (self.off, nbytes)
            a = arena.a[:, self.off // 4:(self.off + nbytes) // 4]
            if dtype != F32:
                a = a.bitcast(dtype)
            a = a[:, 0:n]
            if len(shape) == 3:
                a = a.rearrange("p (a b) -> p a b", b=shape[2])
            elif len(shape) == 4:
                a = a.rearrange("p (a b c) -> p a b c", b=shape[2], c=shape[3])
            self.off += nbytes
            return TL(a, Buf())

    cv = Carver()
    xme = cv.carve([128, 4, NT + 8], F32)
    xc = cv.carve([128, 4, NT], F32)
    szm = cv.carve([128, 4, NT], F32)
    usf = cv.carve([128, 4, NT], F32)
    yf = cv.carve([128, 4, NT], F32)
    usb = cv.carve([128, 4, NT], BF16)
    xcb = cv.carve([128, 4, NT], BF16)
    xmb = cv.carve([128, 4, NT], BF16)
    qT = cv.carve([128, 4, NT], BF16)
    kT = cv.carve([128, 4, NT], BF16)
    vT = cv.carve([128, 4, NT], BF16)
    ygb = cv.carve([128, 4, NT], BF16)
    vtok = cv.carve([128, NB, 4, 129], BF16)
    Ig = cv.carve([128, NT], F32)
    Lf = cv.carve([128, NT], F32)
    Bn = cv.carve([128, NT], F32)
    Cm = cv.carve([128, NT], F32)
    Am = cv.carve([128, NT], F32)
    csT = cv.carve([128, NB, 4], F32)
    eaT = cv.carve([128, NB, 4], F32)
    scrW = Rot([cv.carve([128, 512], F32) for _ in range(8)])
    scrWB = Rot([cv.carve([128, 512], BF16) for _ in range(4)])
    l0_bytes = cv.off
    cv = Carver()
    qsl = cv.carve([128, 8, NT], F32)
    sgf = cv.carve([128, 8, NT], F32)
    vt64 = cv.carve([128, NC64, D], BF16)
    sgl = cv.carve([128, 8, NT], BF16)
    cv = Carver()
    hid = cv.carve([128, 22, NT], BF16)
    halo = P.tile([128, 4, 3], F32, "halo")
    dummy = P.tile([128, 1], F32, "dummy")

    def AR(t):
        return t

    def phase_barrier():
        op("pool", lambda e: e.memset(dummy.a, 0.0), reads=[], writes=[dummy, phase_buf])

    arena_ids = set()

    def mark(*tls):
        for t in tls:
            arena_ids.add(id(t.b))

    def aop(eng, fn, reads=(), writes=()):
        rs = list(reads)
        if any(id(t.b) in arena_ids for t in list(reads) + list(writes) if isinstance(t, TL)):
            rs.append(phase_buf)
        return P.op(eng, fn, reads=rs, writes=writes)

    op = aop
    mark(xme, xc, szm, usf, yf, usb, xcb, xmb, qT, kT, vT, ygb, vtok, Ig, Lf, Bn, Cm, Am, csT, eaT,
         qsl, sgf, vt64, sgl, hid, *scrW.tiles, *scrWB.tiles)

    cnt = {"evac": 0}

    def copy_any(out_ap, in_ap, reads, writes, engs=("act", "dve")):
        e = engs[cnt["evac"] % len(engs)]
        cnt["evac"] += 1
        if e == "act":
            op("act", lambda en: en.activation(out=out_ap, in_=in_ap, func=AF.Copy), reads=reads, writes=writes)
        elif e == "dve":
            op("dve", lambda en: en.tensor_copy(out=out_ap, in_=in_ap), reads=reads, writes=writes)
        else:
            op("pool", lambda en: en.tensor_copy(out=out_ap, in_=in_ap), reads=reads, writes=writes)

    def act(out_ap, in_ap, func, reads, writes, bias=None, scale=None):
        kw = {}
        if bias is not None:
            kw["bias"] = bias
        if scale is not None:
            kw["scale"] = scale
        op("act", lambda en: en.activation(out=out_ap, in_=in_ap, func=func, **kw), reads=reads, writes=writes)

    def tt(eng, out_ap, a_ap, b_ap, alu, reads, writes):
        op(eng, lambda en: en.tensor_tensor(out=out_ap, in0=a_ap, in1=b_ap, op=alu), reads=reads, writes=writes)

    def ts(eng, out_ap, a_ap, s1, s2, o0, o1, reads, writes):
        if o1 is None:
            op(eng, lambda en: en.tensor_scalar(out=out_ap, in0=a_ap, scalar1=s1, scalar2=None, op0=o0), reads=reads, writes=writes)
        else:
            op(eng, lambda en: en.tensor_scalar(out=out_ap, in0=a_ap, scalar1=s1, scalar2=s2, op0=o0, op1=o1), reads=reads, writes=writes)

    def stt(out_ap, a_ap, sc, b_ap, o0, o1, reads, writes):
        op("dve", lambda en: en.scalar_tensor_tensor(out=out_ap, in0=a_ap, scalar=sc, in1=b_ap, op0=o0, op1=o1), reads=reads, writes=writes)

    def mm(out_ap, lhsT, rhs, start, stop, reads, writes):
        op("pe", lambda en: en.matmul(out_ap, lhsT=lhsT, rhs=rhs, start=start, stop=stop), reads=reads, writes=writes)

    def tr(out_ap, in_ap, ident_ap, reads, writes):
        op("pe", lambda en: en.transpose(out_ap, in_ap, ident_ap), reads=reads, writes=writes)

    def recip(out_ap, in_ap, reads, writes):
        op("dve", lambda en: en.reciprocal(out=out_ap, in_=in_ap), reads=reads, writes=writes)

    P.dma("sp", lambda e: e.dma_start(out=prm.a, in_=prm_d), writes=[prm])
    P.dma("sp", lambda e: e.dma_start(out=cst.a, in_=cst_d), writes=[cst])
    op("act", lambda e: e.activation(out=ident_bf.a, in_=cc("ident"), func=AF.Copy), reads=[cst], writes=[ident_bf])
    op("act", lambda e: e.activation(out=ones_bf.a, in_=cc("ones"), func=AF.Copy), reads=[cst], writes=[ones_bf])
    op("act", lambda e: e.activation(out=wif_bf.a, in_=pc("wif"), func=AF.Copy), reads=[prm], writes=[wif_bf])
    ts("dve", nbf.a, pc("bif", 1, 2)[0:4, :], -1.0, None, ALU.mult, None, [prm], [nbf])
    for wi, nm in enumerate(("wq", "wk", "wv")):
        scale = float(128 ** -0.5) if nm == "wk" else 1.0
        for h in range(4):
            stt(BD.a[:, wi, h, :].rearrange("p (n e) -> p n e", e=4), cc("bdmask").rearrange("p (n e) -> p n e", e=4), scale,
                pc(nm, h * 4, h * 4 + 4).unsqueeze(1).broadcast_to([128, 32, 4]), ALU.mult, ALU.mult, [cst, prm], [BD])

    cv = Carver()
    sp = {n: cv.carve([128, 16], F32) for n in
          ["dt", "ard", "mag", "th", "cos", "sin", "abr", "abi", "den", "inv", "abr1", "t1", "t2", "gre", "gim"]}
    bbr = cv.carve([128, 256], F32)
    bbi = cv.carve([128, 256], F32)
    tmpA = cv.carve([128, 256], F32)
    ki = cv.carve([128, 128], I32)
    kf = cv.carve([128, 128], F32)
    ph = cv.carve([128, 128], F32)
    phi = cv.carve([128, 128], F32)
    Zs = Rot([cv.carve([128, 128], F32) for i in range(2)])
    mark(bbr, bbi, tmpA, ki, kf, ph, phi, *Zs.tiles, *sp.values())

    def sin_of(dst_ap, dst_tl, src_ap, src_tl, add, w):
        p_ = ph.a[:, 0:w]
        k_ = ki.a[:, 0:w]
        f_ = kf.a[:, 0:w]
        ts("dve", p_, src_ap, float(add), None, ALU.add, None, [src_tl], [ph])
        ts("dve", k_, p_, 1.0 / (2 * PI), None, ALU.mult, None, [ph], [ki])
        op("dve", lambda e: e.tensor_copy(out=f_, in_=k_), reads=[ki], writes=[kf])
        stt(p_, f_, -2 * PI, p_, ALU.mult, ALU.add, [kf, ph], [ph])
        ts("dve", f_, p_, PI, -2 * PI, ALU.is_gt, ALU.mult, [ph], [kf])
        tt("dve", p_, p_, f_, ALU.add, [ph, kf], [ph])
        ts("dve", f_, p_, -PI, 2 * PI, ALU.is_lt, ALU.mult, [ph], [kf])
        tt("dve", p_, p_, f_, ALU.add, [ph, kf], [ph])
        act(dst_ap, p_, AF.Sin, [ph], [dst_tl])

    act(sp["dt"].a, pc("ldt"), AF.Exp, [prm], [sp["dt"]])
    tt("dve", sp["ard"].a, pc("are"), sp["dt"].a, ALU.mult, [prm, sp["dt"]], [sp["ard"]])
    act(sp["mag"].a, sp["ard"].a, AF.Exp, [sp["ard"]], [sp["mag"]])
    tt("dve", sp["th"].a, pc("aim"), sp["dt"].a, ALU.mult, [prm, sp["dt"]], [sp["th"]])
    sin_of(sp["cos"].a, sp["cos"], sp["th"].a, sp["th"], PI / 2, 16)
    sin_of(sp["sin"].a, sp["sin"], sp["th"].a, sp["th"], 0.0, 16)
    tt("dve", sp["abr"].a, sp["mag"].a, sp["cos"].a, ALU.mult, [sp["mag"], sp["cos"]], [sp["abr"]])
    tt("dve", sp["abi"].a, sp["mag"].a, sp["sin"].a, ALU.mult, [sp["mag"], sp["sin"]], [sp["abi"]])
    tt("dve", sp["den"].a, pc("are"), pc("are"), ALU.mult, [prm], [sp["den"]])
    tt("dve", sp["t1"].a, pc("aim"), pc("aim"), ALU.mult, [prm], [sp["t1"]])
    tt("dve", sp["den"].a, sp["den"].a, sp["t1"].a, ALU.add, [sp["den"], sp["t1"]], [sp["den"]])
    recip(sp["inv"].a, sp["den"].a, [sp["den"]], [sp["inv"]])
    ts("dve", sp["abr1"].a, sp["abr"].a, -1.0, None, ALU.add, None, [sp["abr"]], [sp["abr1"]])
    tt("dve", sp["t1"].a, sp["abr1"].a, pc("are"), ALU.mult, [sp["abr1"], prm], [sp["t1"]])
    tt("dve", sp["t2"].a, sp["abi"].a, pc("aim"), ALU.mult, [sp["abi"], prm], [sp["t2"]])
    tt("dve", sp["t1"].a, sp["t1"].a, sp["t2"].a, ALU.add, [sp["t1"], sp["t2"]], [sp["t1"]])
    tt("dve", sp["gre"].a, sp["t1"].a, sp["inv"].a, ALU.mult, [sp["t1"], sp["inv"]], [sp["gre"]])
    tt("dve", sp["t1"].a, sp["abi"].a, pc("are"), ALU.mult, [sp["abi"], prm], [sp["t1"]])
    tt("dve", sp["t2"].a, sp["abr1"].a, pc("aim"), ALU.mult, [sp["abr1"], prm], [sp["t2"]])
    tt("dve", sp["t1"].a, sp["t1"].a, sp["t2"].a, ALU.subtract, [sp["t1"], sp["t2"]], [sp["t1"]])
    tt("dve", sp["gim"].a, sp["t1"].a, sp["inv"].a, ALU.mult, [sp["t1"], sp["inv"]], [sp["gim"]])

    def v3(ap2d):
        return ap2d.rearrange("p (m q) -> p m q", q=16)

    def bc_m(tl):
        return tl.a.unsqueeze(2).broadcast_to([128, 16, 16])

    tt("dve", v3(bbr.a), v3(pc("bre")), bc_m(sp["gre"]), ALU.mult, [prm, sp["gre"]], [bbr])
    tt("dve", v3(tmpA.a), v3(pc("bim")), bc_m(sp["gim"]), ALU.mult, [prm, sp["gim"]], [tmpA])
    tt("dve", bbr.a, bbr.a, tmpA.a, ALU.subtract, [bbr, tmpA], [bbr])
    tt("dve", v3(bbi.a), v3(pc("bim")), bc_m(sp["gre"]), ALU.mult, [prm, sp["gre"]], [bbi])
    tt("dve", v3(tmpA.a), v3(pc("bre")), bc_m(sp["gim"]), ALU.mult, [prm, sp["gim"]], [tmpA])
    tt("dve", bbi.a, bbi.a, tmpA.a, ALU.add, [bbi, tmpA], [bbi])
    op("dve", lambda e: e.tensor_copy(out=rcol.a, in_=sp["mag"].a), reads=[sp["mag"]], writes=[rcol])
    for m in range(16):
        ts("dve", phi.a, cc("tvec"), sp["th"].a[:, m:m + 1], None, ALU.mult, None, [cst, sp["th"]], [phi])
        sin_of(Ec.a[:, m * 128:(m + 1) * 128], Ec, phi.a, phi, PI / 2, 128)
        sin_of(Es.a[:, m * 128:(m + 1) * 128], Es, phi.a, phi, 0.0, 128)
        ts("dve", Rfull.a[:, m * 128:(m + 1) * 128], cc("ones"), sp["mag"].a[:, m:m + 1], None, ALU.mult, None, [cst, sp["mag"]], [Rfull])
    op("dve", lambda e: e.memset(Rfull.a.rearrange("p (m t) -> p m t", t=128)[:, :, 0:1], 0.0), reads=[], writes=[Rfull])
    for m in range(16):
        v = m % 4
        jj = m // 4
        for ri, bb in enumerate((bbr, bbi)):
            Z = Zs.get()
            tt("dve", Z.a.rearrange("p (g q) -> p g q", q=16), cc("maskB", v * 8, v * 8 + 8).unsqueeze(2).broadcast_to([128, 8, 16]),
               bb.a[:, m * 16:(m + 1) * 16].unsqueeze(1).broadcast_to([128, 8, 16]), ALU.mult, [cst, bb], [Z])
            bk = rotP.get()
            tr(bk.a[:, 0:128], Z.a, ident_f, [Z, cst], [bk])
            copy_any(BBT.a[:, m, ri, :], bk.a[:, 0:128], [bk], [BBT])
        for ri, cn in enumerate(("cre", "cim")):
            Z = Zs.get()
            sgn = -1.0 if ri == 1 else 1.0
            for e_ in range(2):
                ts("dve", Z.a[:, e_ * 64:(e_ + 1) * 64], pc(cn, jj * 64, jj * 64 + 64), cc("maskC", v * 2 + e_, v * 2 + e_ + 1), sgn,
                   ALU.mult, ALU.mult, [prm, cst], [Z])
            bk = rotP.get()
            tr(bk.a[:, 0:128], Z.a, ident_f, [Z, cst], [bk])
            copy_any(CT.a[:, m, ri, :], bk.a[:, 0:128], [bk], [CT])
    tt("dve", lbt.a, pc("lbraw", 8, 16), pc("lbraw", 0, 8), ALU.subtract, [prm], [lbt])
    act(lbt.a, lbt.a, AF.Sigmoid, [lbt], [lbt])
    ts("dve", omlt.a, lbt.a, -1.0, 1.0, ALU.mult, ALU.add, [lbt], [omlt])
    P.dma("sp", lambda e: e.dma_start(out=Cn.a.rearrange("p h v -> p (h v)"), in_=st_in[:, 0:516]), writes=[Cn])
    P.dma("sp", lambda e: e.dma_start(out=rx.a.rearrange("p a m -> p (a m)"), in_=st_in[:, 516:548]), writes=[rx])
    P.dma("sp", lambda e: e.dma_start(out=S.a.rearrange("p h v -> p (h v)"), in_=st_in[:, 548:1572]), writes=[S])
    P.dma("sp", lambda e: e.dma_start(out=halo.a.rearrange("p h k -> p (h k)"), in_=st_in[:, 1572:1584]), writes=[halo])
    op("act", lambda e: e.activation(out=Sbf.a, in_=S.a, func=AF.Copy), reads=[S], writes=[Sbf])
    op("act", lambda e: e.activation(out=Cbf.a, in_=Cn.a[:, :, 0:128], func=AF.Copy), reads=[Cn], writes=[Cbf])
    for h in range(4):
        act(Nrep.a[:, h, :], cc("ones"), AF.Copy, [cst, Cn], [Nrep], scale=Cn.a[:, h, 128:129])

    def rms_rstd(src_chunks, src_tls, nfeat):
        bk = rotP.get()
        n = len(src_chunks)
        for c, ap_ in enumerate(src_chunks):
            sq = scrB.get()
            act(sq.a, ap_, AF.Square, src_tls, [sq])
            mm(bk.a[:, 0:NT], ones_bf.a, sq.a, c == 0, c == n - 1, [ones_bf, sq], [bk])
        s = scrF.get()
        act(s.a, bk.a[:, 0:NT], AF.Sqrt, [bk], [s], bias=EPS, scale=1.0 / nfeat)
        recip(s.a, s.a, [s], [s])
        return s

    def prenorm(goff):
        r = rms_rstd([xT.a[:, c, :] for c in range(8)], [xT], D)
        for c in range(8):
            stt(hT.a[:, c, :], xT.a[:, c, :], pc("ng", goff + c, goff + c + 1), r.a, ALU.mult, ALU.mult, [xT, prm, r], [hT])

    def postnorm(goff):
        r = rms_rstd([mix.a[:, c, :] for c in range(8)], [mix], D)
        for c in range(8):
            t = scrF.get()
            stt(t.a, mix.a[:, c, :], pc("ng", goff + c, goff + c + 1), r.a, ALU.mult, ALU.mult, [mix, prm, r], [t])
            tt("pool", xT.a[:, c, :], xT.a[:, c, :], t.a, ALU.add, [xT, t], [xT])

    def proj(w_ap, K, ncols, rhs_fn, rhs_tls, consume):
        cw = 512 if K <= 8 else 256
        if ncols % cw:
            cw = 256
        assert ncols % cw == 0
        for s in range(ncols // cw):
            wb = wbufs.get()
            wv = wb.a[:, 0:K * cw].rearrange("p (k n) -> p k n", n=cw)
            src = w_ap[:, s * cw:(s + 1) * cw].rearrange("(k p) n -> p k n", p=128)
            P.dma("pool", lambda e, wv=wv, src=src: e.dma_start(out=wv, in_=src), writes=[wb])
            for e_ in range(cw // 128):
                bk = rotP.get()
                for k in range(K):
                    mm(bk.a[:, 0:NT], wv[:, k, e_ * 128:(e_ + 1) * 128], rhs_fn(k), k == 0, k == K - 1, [wb] + rhs_tls, [bk])
                consume(s * (cw // 128) + e_, bk)

    def proj_tok(w_ap, consume):
        cw = 512
        for s in range(w_ap.shape[1] // cw):
            wb = wbufs.get()
            wv = wb.a[:, 0:8 * cw].rearrange("p (k n) -> p k n", n=cw)
            src = w_ap[:, s * cw:(s + 1) * cw].rearrange("(k p) n -> p k n", p=128)
            P.dma("pool", lambda e, wv=wv, src=src: e.dma_start(out=wv, in_=src), writes=[wb])
            for c in range(NC64):
                bk = rotP.get()
                for k in range(8):
                    mm(bk.a[0:64, 0:512], hT.a[:, k, c * 64:(c + 1) * 64], wv[:, k, :], k == 0, k == 7, [wb, hT], [bk])
                consume(s, c, bk)

    def ffn(l):
        phase_barrier()
        prenorm((l * 4 + 2) * 8)

        def c1(e_, bk):
            act(hid.a[:, e_, :], bk.a[:, 0:NT], AF.Silu, [bk], [hid])

        proj(w1_d[l], 8, DFF, lambda k: hT.a[:, k, :], [hT], c1)

        def c3(e_, bk):
            tt("dve", hid.a[:, e_, :], hid.a[:, e_, :], bk.a[:, 0:NT], ALU.mult, [hid, bk], [hid])

        proj(w3_d[l], 8, DFF, lambda k: hT.a[:, k, :], [hT], c3)

        def c2(e_, bk):
            copy_any(mix.a[:, e_, :], bk.a[:, 0:NT], [bk], [mix])

        proj(w2_d[l], 22, D, lambda k: hid.a[:, k, :], [hid], c2)
        postnorm((l * 4 + 3) * 8)

    def layer0_mixer():
        phase_barrier()
        prenorm(0)
        for h in range(4):
            op("act", lambda e, h=h: e.activation(out=xme.a[:, h, 5:8], in_=halo.a[:, h, :], func=AF.Copy), reads=[halo], writes=[xme])

        def c_in(e_, bk):
            if e_ < 4:
                copy_any(xme.a[:, e_, 8:8 + NT], bk.a[:, 0:NT], [bk], [xme])
            elif e_ < 8:
                act(szm.a[:, e_ - 4, :], bk.a[:, 0:NT], AF.Silu, [bk], [szm])
            else:
                act(usf.a[:, e_ - 8, :], bk.a[:, 0:NT], AF.Copy, [bk], [usf])
                op("dve", lambda en: en.tensor_copy(out=usb.a[:, e_ - 8, :], in_=bk.a[:, 0:NT]), reads=[bk], writes=[usb])

        if "l0a0" in STAGES:
            return
        proj(w_in0, 8, 1536, lambda k: hT.a[:, k, :], [hT], c_in)
        if "l0a1" in STAGES:
            return
        for h in range(4):
            acc = scrF.get()
            ts("dve", acc.a, xme.a[:, h, 5:5 + NT], pc("convw", h * 4, h * 4 + 1), None, ALU.mult, None, [xme, prm], [acc])
            for k in range(1, 4):
                stt(acc.a, xme.a[:, h, 5 + k:5 + k + NT], pc("convw", h * 4 + k, h * 4 + k + 1), acc.a, ALU.mult, ALU.add, [xme, prm, acc], [acc])
            act(xc.a[:, h, :], acc.a, AF.Silu, [acc, prm], [xc], bias=pc("convb", h, h + 1))
            op("act", lambda e, h=h: e.activation(out=xcb.a[:, h, :], in_=xc.a[:, h, :], func=AF.Copy), reads=[xc], writes=[xcb])
            op("act", lambda e, h=h: e.activation(out=xmb.a[:, h, :], in_=xme.a[:, h, 8:8 + NT], func=AF.Copy), reads=[xme], writes=[xmb])
            op("act", lambda e, h=h: e.activation(out=halo.a[:, h, :], in_=xme.a[:, h, NT + 5:NT + 8], func=AF.Copy), reads=[xme], writes=[halo])
        if "l0a" in STAGES:
            return
        for h in range(4):
            for wi, (src, dst) in enumerate(((xcb, qT), (xcb, kT), (xmb, vT))):
                bk = rotP.get()
                mm(bk.a[:, 0:NT], BD.a[:, wi, h, :], src.a[:, h, :], True, True, [BD, src], [bk])
                copy_any(dst.a[:, h, :], bk.a[:, 0:NT], [bk], [dst])
        op("pool", lambda e: e.memset(vtok.a[:, :, :, 128:129], 1.0), reads=[], writes=[vtok])
        for nb in range(NB):
            bk = rotP.get()
            for h in range(4):
                mm(bk.a[:, h * 128:(h + 1) * 128], xmb.a[:, h, nb * 128:(nb + 1) * 128], BD.a[:, 2, h, :], True, True, [xmb, BD], [bk])
            copy_any(vtok.a[:, nb, :, 0:128], bk.a[:, 0:512].rearrange("p (h v) -> p h v", v=128), [bk], [vtok])
        if "l0b" in STAGES:
            return
        bi = rotP.get()
        bf_ = rotP.get()
        for c in range(12):
            src = (qT, kT, vT)[c // 4]
            mm(bi.a[0:4, 0:NT], wif_bf.a[:, c * 8:c * 8 + 4], src.a[:, c % 4, :], c == 0, c == 11, [wif_bf, src], [bi])
        for c in range(12):
            src = (qT, kT, vT)[c // 4]
            mm(bf_.a[0:4, 0:NT], wif_bf.a[:, c * 8 + 4:c * 8 + 8], src.a[:, c % 4, :], c == 0, c == 11, [wif_bf, src], [bf_])
        act(Ig.a[0:4, :], bi.a[0:4, 0:NT], AF.Identity, [bi, prm], [Ig], bias=pc("bif", 0, 1)[0:4, :])
        act(Lf.a[0:4, :], bf_.a[0:4, 0:NT], AF.Exp, [bf_, nbf], [Lf], bias=nbf.a, scale=-1.0)
        act(Lf.a[0:4, :], Lf.a[0:4, :], AF.Ln, [Lf], [Lf], bias=1.0)
        op("dve", lambda e: e.tensor_tensor_scan(out=Bn.a[0:4, :], data0=cc("r128")[0:4, :], data1=Lf.a[0:4, :], initial=0.0,
                                                 op0=ALU.mult, op1=ALU.add), reads=[cst, Lf], writes=[Bn])
        tt("dve", Cm.a[0:4, :], Ig.a[0:4, :], Bn.a[0:4, :], ALU.add, [Ig, Bn], [Cm])
        for c in range(NB):
            cs = slice(c * 128, (c + 1) * 128)
            ts("dve", Am.a[0:4, cs], Cm.a[0:4, cs], Bn.a[0:4, c * 128 + 127:c * 128 + 128], None, ALU.subtract, None, [Cm, Bn], [Am])
        for nb in range(NB):
            cs = slice(nb * 128, (nb + 1) * 128)
            bk = rotP.get()
            tr(bk.a[:, 0:4], Cm.a[0:4, cs], ident_f[0:4, 0:4], [Cm, cst], [bk])
            tr(bk.a[:, 4:8], Am.a[0:4, cs], ident_f[0:4, 0:4], [Am, cst], [bk])
            op("dve", lambda e, bk=bk, nb=nb: e.tensor_copy(out=csT.a[:, nb, :], in_=bk.a[:, 0:4]), reads=[bk], writes=[csT])
            act(eaT.a[:, nb, :], bk.a[:, 4:8], AF.Exp, [bk], [eaT])
        if "l0c" in STAGES:
            return
        for h in range(4):
            bB = rotP.get()
            mm(bB.a[:, 0:NT], cc("selneg", h * 128, h * 128 + 128)[0:4, :], Bn.a[0:4, :], True, True, [cst, Bn], [bB])
            ExpB = scrF.get()
            act(ExpB.a, bB.a[:, 0:NT], AF.Exp, [bB], [ExpB])
            qs = scrB.get()
            tt("dve", qs.a, qT.a[:, h, :], ExpB.a, ALU.mult, [qT, ExpB], [qs])
            bN = accP.get()
            bD = accP.get()
            for c in range(NB):
                cs = slice(c * 128, (c + 1) * 128)
                bE = rotP.get()
                mm(bE.a[:, 0:128], cc("selneg", h * 128, h * 128 + 128)[0:4, :], Bn.a[0:4, cs], True, False, [cst, Bn], [bE])
                mm(bE.a[:, 0:128], ident_f, cc("negmask"), False, True, [cst], [bE])
                E = scrF.get()
                act(E.a[:, 0:128], bE.a[:, 0:128], AF.Exp, [bE, csT], [E], bias=csT.a[:, c, h:h + 1])
                bS = rotP.get()
                mm(bS.a[:, 0:128], kT.a[:, h, cs], qT.a[:, h, cs], True, True, [kT, qT], [bS])
                PT = scrS.get()
                tt("dve", PT.a[:, 0:128], bS.a[:, 0:128], E.a[:, 0:128], ALU.mult, [bS, E], [PT])
                mm(bN.a[:, cs], vtok.a[:, c, h, 0:128], PT.a[:, 0:128], True, False, [vtok, PT], [bN])
                mm(bN.a[:, cs], Cbf.a[:, h, :], qs.a[:, cs], False, True, [Cbf, qs], [bN])
                mm(bD.a[:, cs], ones_bf.a, PT.a[:, 0:128], True, False, [ones_bf, PT], [bD])
                mm(bD.a[:, cs], Nrep.a[:, h, :], qs.a[:, cs], False, True, [Nrep, qs], [bD])
                bK = rotP.get()
                mm(bK.a[:, 0:128], xcb.a[:, h, cs], BD.a[:, 1, h, :], True, True, [xcb, BD], [bK])
                kw = scrS.get()
                act(kw.a[:, 0:128], bK.a[:, 0:128], AF.Copy, [bK, eaT], [kw], scale=eaT.a[:, c, h:h + 1])
                bU = rotP.get()
                mm(bU.a[:, 0:129], kw.a[:, 0:128], vtok.a[:, c, h, :], True, True, [kw, vtok], [bU])
                stt(Cn.a[:, h, :], Cn.a[:, h, :], ExpB.a[:, c * 128 + 127:c * 128 + 128], bU.a[:, 0:129], ALU.mult, ALU.add, [Cn, ExpB, bU], [Cn])
                act(Cbf.a[:, h, :], Cn.a[:, h, 0:128], AF.Copy, [Cn], [Cbf])
                act(Nrep.a[:, h, :], cc("ones"), AF.Copy, [cst, Cn], [Nrep], scale=Cn.a[:, h, 128:129])
            ad = scrF.get()
            act(ad.a, bD.a[:, 0:NT], AF.Abs, [bD], [ad])
            ts("dve", ad.a, ad.a, 1.0, None, ALU.max, None, [ad], [ad])
            recip(ad.a, ad.a, [ad], [ad])
            hb = scrF.get()
            tt("dve", hb.a, bN.a[:, 0:NT], ad.a, ALU.mult, [bN, ad], [hb])
            bM = rotP.get()
            mm(bM.a[:, 0:NT], cc("onesdiv"), hb.a, True, True, [cst, hb], [bM])
            dd = scrF.get()
            tt("dve", dd.a, hb.a, bM.a[:, 0:NT], ALU.subtract, [hb, bM], [dd])
            sq = scrF.get()
            tt("pool", sq.a, dd.a, dd.a, ALU.mult, [dd], [sq])
            bV = rotP.get()
            mm(bV.a[:, 0:NT], cc("onesdiv"), sq.a, True, True, [cst, sq], [bV])
            sd = scrF.get()
            act(sd.a, bV.a[:, 0:NT], AF.Sqrt, [bV], [sd], bias=EPS)
            recip(sd.a, sd.a, [sd], [sd])
            stt(dd.a, dd.a, pc("mhg", h, h + 1), sd.a, ALU.mult, ALU.mult, [dd, prm, sd], [dd])
            stt(dd.a, xc.a[:, h, :], pc("skip", h, h + 1), dd.a, ALU.mult, ALU.add, [xc, prm, dd], [dd])
            tt("dve", cat.a[:, h, :], dd.a, szm.a[:, h, :], ALU.mult, [dd, szm], [cat])
        if "l0d" in STAGES:
            return
        for jj in range(4):
            Ecq = Ec.a[:, jj * 512:(jj + 1) * 512]
            Esq = Es.a[:, jj * 512:(jj + 1) * 512]
            Rq = Rfull.a[:, jj * 512:(jj + 1) * 512]
            for sb in range(NB):
                cs = slice(sb * 128, (sb + 1) * 128)
                bR = rotP.get()
                bI = rotP.get()
                for mi in range(4):
                    m = jj * 4 + mi
                    mm(bR.a[:, mi * 128:(mi + 1) * 128], BBT.a[:, m, 0, :], usb.a[:, jj, cs], True, True, [BBT, usb], [bR])
                for mi in range(4):
                    m = jj * 4 + mi
                    mm(bI.a[:, mi * 128:(mi + 1) * 128], BBT.a[:, m, 1, :], usb.a[:, jj, cs], True, True, [BBT, usb], [bI])
                bre = scrW.get()
                bim = scrW.get()
                act(bre.a, bR.a, AF.Copy, [bR], [bre])
                act(bim.a, bI.a, AF.Copy, [bI], [bim])
                t1 = scrW.get()
                t2 = scrW.get()
                t3 = scrW.get()
                t4 = scrW.get()
                tt("dve", t1.a, Ecq, bre.a, ALU.mult, [Ec, bre], [t1])
                tt("dve", t2.a, Esq, bim.a, ALU.mult, [Es, bim], [t2])
                tt("dve", t1.a, t1.a, t2.a, ALU.add, [t1, t2], [t1])
                tt("pool", t3.a, Ecq, bim.a, ALU.mult, [Ec, bim], [t3])
                tt("pool", t4.a, Esq, bre.a, ALU.mult, [Es, bre], [t4])
                tt("pool", t3.a, t3.a, t4.a, ALU.subtract, [t3, t4], [t3])

                def v3w(t):
                    return t.a.rearrange("p (m t) -> p m t", t=128)

                tt("dve", v3w(t1)[:, :, 0:1], v3w(t1)[:, :, 0:1], rx.a[:, 0, jj * 4:(jj + 1) * 4].unsqueeze(2), ALU.add, [t1, rx], [t1])
                tt("pool", v3w(t3)[:, :, 0:1], v3w(t3)[:, :, 0:1], rx.a[:, 1, jj * 4:(jj + 1) * 4].unsqueeze(2), ALU.add, [t3, rx], [t3])
                zr = scrW.get()
                zi = scrW.get()
                op("dve", lambda e, zr=zr, t1=t1, Rq=Rq: e.tensor_tensor_scan(out=zr.a, data0=Rq, data1=t1.a, initial=0.0, op0=ALU.mult, op1=ALU.add),
                   reads=[Rfull, t1], writes=[zr])
                op("dve", lambda e, zi=zi, t3=t3, Rq=Rq: e.tensor_tensor_scan(out=zi.a, data0=Rq, data1=t3.a, initial=0.0, op0=ALU.mult, op1=ALU.add),
                   reads=[Rfull, t3], writes=[zi])
                u1 = scrW.get()
                u2 = scrW.get()
                u3 = scrW.get()
                u4 = scrW.get()
                tt("dve", u1.a, Ecq, zr.a, ALU.mult, [Ec, zr], [u1])
                tt("pool", u2.a, Esq, zi.a, ALU.mult, [Es, zi], [u2])
                tt("dve", u1.a, u1.a, u2.a, ALU.subtract, [u1, u2], [u1])
                tt("pool", u3.a, Ecq, zi.a, ALU.mult, [Ec, zi], [u3])
                tt("dve", u4.a, Esq, zr.a, ALU.mult, [Es, zr], [u4])
                tt("pool", u3.a, u3.a, u4.a, ALU.add, [u3, u4], [u3])
                tt("dve", rx.a[:, 0, jj * 4:(jj + 1) * 4].unsqueeze(2), v3w(u1)[:, :, 127:128], rcol.a[:, jj * 4:(jj + 1) * 4].unsqueeze(2), ALU.mult, [u1, rcol], [rx])
                tt("pool", rx.a[:, 1, jj * 4:(jj + 1) * 4].unsqueeze(2), v3w(u3)[:, :, 127:128], rcol.a[:, jj * 4:(jj + 1) * 4].unsqueeze(2), ALU.mult, [u3, rcol], [rx])
                xrb = scrWB.get()
                xib = scrWB.get()
                act(xrb.a, u1.a, AF.Copy, [u1], [xrb])
                act(xib.a, u3.a, AF.Copy, [u3], [xib])
                bY = rotP.get()
                for mi in range(4):
                    m = jj * 4 + mi
                    mm(bY.a[:, 0:128], CT.a[:, m, 0, :], xrb.a[:, mi * 128:(mi + 1) * 128], mi == 0, False, [CT, xrb], [bY])
                    mm(bY.a[:, 0:128], CT.a[:, m, 1, :], xib.a[:, mi * 128:(mi + 1) * 128], False, mi == 3, [CT, xib], [bY])
                stt(yf.a[:, jj, cs], usf.a[:, jj, cs], pc("dsk", jj, jj + 1), bY.a[:, 0:128], ALU.mult, ALU.add, [usf, prm, bY], [yf])
        if "l0e" in STAGES:
            return
        for jj in range(4):
            y_ = yf.a[:, jj, :]
            x2 = scrF.get()
            tt("pool", x2.a, y_, y_, ALU.mult, [yf], [x2])
            ts("dve", x2.a, x2.a, 0.044715, 1.0, ALU.mult, ALU.add, [x2], [x2])
            tt("dve", x2.a, x2.a, y_, ALU.mult, [x2, yf], [x2])
            act(x2.a, x2.a, AF.Sigmoid, [x2], [x2], scale=1.5957691216057308)
            tt("dve", y_, y_, x2.a, ALU.mult, [yf, x2], [yf])
            op("act", lambda e, jj=jj: e.activation(out=ygb.a[:, jj, :], in_=yf.a[:, jj, :], func=AF.Copy), reads=[yf], writes=[ygb])

        def c_glu(e_, bk):
            s = scrF.get()
            act(s.a, bk.a[:, 0:NT], AF.Sigmoid, [bk, prm], [s], bias=pc("bglu", e_, e_ + 1))
            tt("dve", cat.a[:, 4 + e_, :], yf.a[:, e_, :], s.a, ALU.mult, [yf, s], [cat])

        proj(w_glu, 4, 512, lambda k: ygb.a[:, k, :], [ygb], c_glu)

        def c_out(e_, bk):
            copy_any(mix.a[:, e_, :], bk.a[:, 0:NT], [bk], [mix])

        proj(w_out0, 8, D, lambda k: cat.a[:, k, :], [cat], c_out)
        postnorm(8)

    def layer1_mixer():
        phase_barrier()
        prenorm(32)

        def c_q(e_, bk):
            act(qsl.a[:, e_, :], bk.a[:, 0:NT], AF.Silu, [bk], [qsl])

        def c_f(e_, bk):
            act(sgf.a[:, e_, :], bk.a[:, 0:NT], AF.Sigmoid, [bk], [sgf])

        def c_g(e_, bk):
            act(sgl.a[:, e_, :], bk.a[:, 0:NT], AF.Silu, [bk], [sgl])

        def c_i(s, c, bk):
            copy_any(vt64.a[0:64, c, s * 512:(s + 1) * 512], bk.a[0:64, 0:512], [bk], [vt64])

        proj(w_in1[:, 0:1024], 8, 1024, lambda k: hT.a[:, k, :], [hT], c_q)
        proj(w_in1[:, 1024:2048], 8, 1024, lambda k: hT.a[:, k, :], [hT], c_f)
        proj_tok(w_in1[:, 2048:3072], c_i)
        proj(w_in1[:, 3072:4096], 8, 1024, lambda k: hT.a[:, k, :], [hT], c_g)
        for hd in range(8):
            fg = scrF.get()
            kk = scrF.get()
            lf = scrF.get()
            b_ = scrF.get()
            eb = scrF.get()
            enb = scrF.get()
            ed = scrF.get()
            ts("dve", fg.a, sgf.a[:, hd, :], omlt.a[:, hd:hd + 1], lbt.a[:, hd:hd + 1], ALU.mult, ALU.add, [sgf, omlt, lbt], [fg])
            ts("dve", kk.a, fg.a, -1.0, 1.0, ALU.mult, ALU.add, [fg], [kk])
            act(lf.a, fg.a, AF.Ln, [fg], [lf])
            op("dve", lambda e, b_=b_, lf=lf: e.tensor_tensor_scan(out=b_.a, data0=cc("r64"), data1=lf.a, initial=0.0, op0=ALU.mult, op1=ALU.add),
               reads=[cst, lf], writes=[b_])
            act(eb.a, b_.a, AF.Exp, [b_], [eb])
            act(enb.a, b_.a, AF.Exp, [b_], [enb], scale=-1.0)
            for c in range(NC64):
                cs = slice(c * 64, (c + 1) * 64)
                act(ed.a[:, cs], b_.a[:, cs], AF.Exp, [b_], [ed], scale=-1.0, bias=b_.a[:, c * 64 + 63:c * 64 + 64])
            qi = scrB.get()
            k2 = scrB.get()
            kst = scrB.get()
            tt("dve", qi.a, qsl.a[:, hd, :], eb.a, ALU.mult, [qsl, eb], [qi])
            tt("pool", k2.a, kk.a, enb.a, ALU.mult, [kk, enb], [k2])
            tt("pool", kst.a, kk.a, ed.a, ALU.mult, [kk, ed], [kst])
            bO = accP.get()
            for c in range(NC64):
                cs = slice(c * 64, (c + 1) * 64)
                hs = slice(hd * 128, (hd + 1) * 128)
                bS = rotP.get()
                mm(bS.a[0:64, 0:64], k2.a[:, cs], qi.a[:, cs], True, True, [k2, qi], [bS])
                PT = scrS.get()
                tt("dve", PT.a[0:64, 0:64], bS.a[0:64, 0:64], cc("mask64")[0:64, :], ALU.mult, [bS, cst], [PT])
                mm(bO.a[:, cs], vt64.a[0:64, c, hs], PT.a[0:64, 0:64], True, False, [vt64, PT], [bO])
                mm(bO.a[:, cs], Sbf.a[:, hd, :], qi.a[:, cs], False, True, [Sbf, qi], [bO])
                bT = rotP.get()
                tr(bT.a.bitcast(BF16)[0:64, 0:128], kst.a[:, cs], ident_bf.a, [kst, ident_bf], [bT])
                ktok = scrS.get()
                act(ktok.a[0:64, 0:128], bT.a.bitcast(BF16)[0:64, 0:128], AF.Copy, [bT], [ktok])
                bU = rotP.get()
                mm(bU.a[:, 0:128], ktok.a[0:64, 0:128], vt64.a[0:64, c, hs], True, True, [ktok, vt64], [bU])
                stt(S.a[:, hd, :], S.a[:, hd, :], eb.a[:, c * 64 + 63:c * 64 + 64], bU.a[:, 0:128], ALU.mult, ALU.add, [S, eb, bU], [S])
                act(Sbf.a[:, hd, :], S.a[:, hd, :], AF.Copy, [S], [Sbf])
            sq = scrB.get()
            act(sq.a, bO.a[:, 0:NT], AF.Square, [bO], [sq])
            bQ = rotP.get()
            mm(bQ.a[:, 0:NT], ones_bf.a, sq.a, True, True, [ones_bf, sq], [bQ])
            sd = scrF.get()
            act(sd.a, bQ.a[:, 0:NT], AF.Sqrt, [bQ], [sd], bias=EPS, scale=1.0 / 128)
            recip(sd.a, sd.a, [sd], [sd])
            on = scrF.get()
            tt("dve", on.a, bO.a[:, 0:NT], sd.a, ALU.mult, [bO, sd], [on])
            stt(cat.a[:, hd, :], on.a, pc("ggain", hd, hd + 1), sgl.a[:, hd, :], ALU.mult, ALU.mult, [on, prm, sgl], [cat])

        def c_out(e_, bk):
            copy_any(mix.a[:, e_, :], bk.a[:, 0:NT], [bk], [mix])

        proj(w_out1, 8, D, lambda k: cat.a[:, k, :], [cat], c_out)
        postnorm(40)

    stores = []
    for ti in range(n_tiles):
        src = x_d[ti * NT:(ti + 1) * NT, :].rearrange("(n p) d -> p n d", p=128)
        P.dma("sp", lambda e, src=src: e.dma_start(out=xio.a, in_=src), writes=[xio])
        for c in range(8):
            bk = rotP.get()
            for nb in range(NB):
                tr(bk.a[:, nb * 128:(nb + 1) * 128], xio.a[:, nb, c * 128:(c + 1) * 128], ident_f, [xio, cst], [bk])
            copy_any(xT.a[:, c, :], bk.a[:, 0:NT], [bk], [xT])
        if "l0" in STAGES:
            layer0_mixer()
        if "f0" in STAGES:
            ffn(0)
        if "l1" in STAGES:
            layer1_mixer()
        if "f1" in STAGES:
            ffn(1)
        for nb in range(NB):
            for c4 in range(2):
                bk = rotP.get()
                for c in range(4):
                    tr(bk.a[:, c * 128:(c + 1) * 128], xT.a[:, c4 * 4 + c, nb * 128:(nb + 1) * 128], ident_f, [xT, cst], [bk])
                copy_any(xio.a[:, nb, c4 * 512:(c4 + 1) * 512], bk.a[:, 0:512], [bk], [xio])
        dst = y_d[ti * NT:(ti + 1) * NT, :].rearrange("(n p) d -> p n d", p=128)
        stores.append(P.dma("sp", lambda e, dst=dst: e.dma_start(out=dst, in_=xio.a), reads=[xio]))

    stores.append(P.dma("sp", lambda e: e.dma_start(out=st_out[:, 0:516], in_=Cn.a.rearrange("p h v -> p (h v)")), reads=[Cn]))
    stores.append(P.dma("sp", lambda e: e.dma_start(out=st_out[:, 516:548], in_=rx.a.rearrange("p a m -> p (a m)")), reads=[rx]))
    stores.append(P.dma("sp", lambda e: e.dma_start(out=st_out[:, 548:1572], in_=S.a.rearrange("p h v -> p (h v)")), reads=[S]))
    stores.append(P.dma("sp", lambda e: e.dma_start(out=st_out[:, 1572:1584], in_=halo.a.rearrange("p h k -> p (h k)")), reads=[halo]))
    P.emit(final_wait_ops=stores[-12:])
    P.close()
    return nc


N_LAUNCH = 1
_CACHE = {}


def _get_prog(n_tiles):
    if n_tiles not in _CACHE:
        _CACHE[n_tiles] = build_program(n_tiles)
    return _CACHE[n_tiles]


def run_sequence_chunks(inputs, x_seqs, n_launch=1):
    L = x_seqs[0].shape[0]
    per = L // n_launch
    n_tiles = per // NT
    prm = pack_params(inputs)
    cst = make_consts()
    shared = {
        "prm": prm, "cst": cst,
        "ab_w_in": np.ascontiguousarray(inputs["ab_w_in"][0], np.float32),
        "ab_w_glu": np.ascontiguousarray(inputs["ab_w_glu"][0], np.float32),
        "ab_w_out": np.ascontiguousarray(inputs["ab_w_out"][0], np.float32),
        "c_w_in": np.ascontiguousarray(inputs["c_w_in"][0], np.float32),
        "c_w_out": np.ascontiguousarray(inputs["c_w_out"][0], np.float32),
        "ffn_w1": np.ascontiguousarray(inputs["ffn_w1"], np.float32),
        "ffn_w3": np.ascontiguousarray(inputs["ffn_w3"], np.float32),
        "ffn_w2": np.ascontiguousarray(inputs["ffn_w2"], np.float32),
    }
    ncore = len(x_seqs)
    states = [np.zeros((128, NSTATE), np.float32) for _ in range(ncore)]
    outs = [[] for _ in range(ncore)]
    for li in range(n_launch):
        nc = _get_prog(n_tiles)
        in_maps = []
        for c in range(ncore):
            xs = x_seqs[c % len(x_seqs)]
            m = dict(shared)
            m["x"] = np.ascontiguousarray(xs[li * per:(li + 1) * per], np.float32)
            m["st_in"] = states[c]
            in_maps.append(m)
        res = run_bass_kernel_spmd(nc, in_maps, core_ids=list(range(ncore)))
        for c in range(ncore):
            outs[c].append(np.asarray(res.results[c]["y"], np.float32))
            states[c] = np.asarray(res.results[c]["st_out"], np.float32)
    return [np.concatenate(outs[c], axis=0) for c in range(len(x_seqs))]


def kernel(**inputs):
    x = np.asarray(inputs["x"], np.float32)
    seqs = [x[b] for b in range(x.shape[0])]
    ys = run_sequence_chunks(inputs, seqs, n_launch=N_LAUNCH)
    return np.stack(ys, axis=0).astype(np.float32)
```

```python
import contextlib
import numpy as np
import concourse.bass as bass
import concourse.mybir as mybir
from concourse.bass_utils import run_bass_kernel_spmd

F32 = mybir.dt.float32
BF16 = mybir.dt.bfloat16
I32 = mybir.dt.int32
AF = mybir.ActivationFunctionType
ALU = mybir.AluOpType
PI = float(np.pi)
EPS = 1e-6

D = 1024
DFF = 2816
NT = 256
NB = NT // 128
NC64 = NT // 64
SEQ = 16384
BATCH = 2

COMPUTE = ("pe", "act", "dve", "pool")
DEBUG_TAGS = False
STAGES = {"l0", "f0", "l1", "f1"}


class Buf:
    __slots__ = ("last_w", "readers", "serial")

    def __init__(self, serial=False):
        self.last_w = None
        self.readers = {}
        self.serial = serial


class Op:
    __slots__ = ("eng", "fn", "deps", "sig", "is_dma", "sem", "val", "prev_same_sem", "tag")

    def __init__(self, eng, fn, is_dma):
        self.tag = None
        if DEBUG_TAGS:
            import sys as _s
            f = _s._getframe(2)
            ls = []
            while f is not None and len(ls) < 5:
                ls.append(str(f.f_lineno))
                f = f.f_back
            self.tag = ">".join(ls)
        self.eng = eng
        self.fn = fn
        self.is_dma = is_dma
        self.deps = []
        self.sig = False
        self.sem = None
        self.val = 0
        self.prev_same_sem = None


class TL:
    __slots__ = ("a", "b")

    def __init__(self, a, b):
        self.a = a
        self.b = b


class Rot:
    def __init__(self, tiles):
        self.tiles = tiles
        self.i = 0

    def get(self):
        t = self.tiles[self.i % len(self.tiles)]
        self.i += 1
        return t


class Prog:
    def __init__(self, nc, n_dma_sems=8):
        self.nc = nc
        self.ops = {e: [] for e in ("sp", "act", "dve", "pool", "pe")}
        self.n_dma_sems = n_dma_sems
        self.dma_count = {"sp": 0, "act": 0, "pool": 0}
        self.dma_last = {}
        self.stack = contextlib.ExitStack()
        self.nalloc = 0

    def tile(self, shape, dtype, name=None):
        self.nalloc += 1
        t = self.stack.enter_context(self.nc.sbuf_tensor("sb_" + (name or f"t{self.nalloc}"), list(shape), dtype))
        return TL(t[:], Buf())

    def psum_bank(self):
        self.nalloc += 1
        t = self.stack.enter_context(self.nc.psum_tensor(f"ps{self.nalloc}", [128, 512], F32))
        return TL(t[:], Buf(serial=True))

    def _record(self, op, reads, writes):
        deps = {}
        for b in reads:
            if b.last_w is not None:
                deps[id(b.last_w)] = b.last_w
            if b.serial:
                for r in b.readers.values():
                    deps[id(r)] = r
        for b in writes:
            if b.last_w is not None:
                deps[id(b.last_w)] = b.last_w
            for r in b.readers.values():
                deps[id(r)] = r
        for d in deps.values():
            if d is op:
                continue
            if (not d.is_dma) and (not op.is_dma) and d.eng == op.eng and op.eng == "pe":
                continue
            op.deps.append(d)
            d.sig = True
        for b in writes:
            b.last_w = op
            b.readers = {}
        for b in reads:
            if b.last_w is op:
                continue
            key = ("dma", id(op)) if op.is_dma else op.eng
            b.readers[key] = op

    def op(self, eng, fn, reads=(), writes=()):
        o = Op(eng, fn, False)
        self._record(o, [t.b if isinstance(t, TL) else t for t in reads], [t.b if isinstance(t, TL) else t for t in writes])
        self.ops[eng].append(o)
        return o

    def dma(self, queue, fn, reads=(), writes=()):
        o = Op(queue, fn, True)
        k = (queue, self.dma_count[queue] % self.n_dma_sems)
        self.dma_count[queue] += 1
        o.sem = k
        o.prev_same_sem = self.dma_last.get(k)
        o.val = (o.prev_same_sem.val if o.prev_same_sem else 0) + 16
        self.dma_last[k] = o
        o.sig = True
        self._record(o, [t.b if isinstance(t, TL) else t for t in reads], [t.b if isinstance(t, TL) else t for t in writes])
        if o.prev_same_sem is not None:
            o.deps.append(o.prev_same_sem)
        self.ops[queue].append(o)
        return o

    def emit(self, final_wait_ops=()):
        nc = self.nc
        st = self.stack
        dma_sems = {(q, i): st.enter_context(nc.semaphore(f"s_dma_{q}{i}")) for q in ("sp", "pool") for i in range(self.n_dma_sems)}
        eng_sems = {}
        SEM_LIMIT = 20000
        self.sig_counts = {}
        for e in COMPUTE:
            cnt = 0
            epoch = 0
            total = 0
            for o in self.ops[e]:
                if o.is_dma:
                    continue
                if o.sig:
                    if cnt >= SEM_LIMIT:
                        epoch += 1
                        cnt = 0
                    cnt += 1
                    total += 1
                    key = (e, epoch)
                    if key not in eng_sems:
                        eng_sems[key] = st.enter_context(nc.semaphore(f"s_{e}{epoch}"))
                    o.sem = key
                    o.val = cnt
            self.sig_counts[e] = total
        block = st.enter_context(nc.Block())

        def run(ename, engobj):
            waited = {}
            for o in self.ops[ename]:
                for d in o.deps:
                    key = ("dma", d.sem) if d.is_dma else d.sem
                    if waited.get(key, 0) >= d.val:
                        continue
                    waited[key] = d.val
                    s = dma_sems[d.sem] if d.is_dma else eng_sems[d.sem]
                    engobj.wait_ge(s, d.val)
                ins = o.fn(engobj)
                if o.tag is not None:
                    ins.annotate(o.tag)
                if o.sig:
                    if o.is_dma:
                        ins.then_inc(dma_sems[o.sem], 16)
                    else:
                        ins.then_inc(eng_sems[o.sem], 1)
            if ename == "sp":
                for d in final_wait_ops:
                    s = dma_sems[d.sem] if d.is_dma else eng_sems[d.sem]
                    engobj.wait_ge(s, d.val)

        @block.sync
        def _(e):
            run("sp", e)

        @block.scalar
        def _(e):
            run("act", e)

        @block.vector
        def _(e):
            run("dve", e)

        @block.gpsimd
        def _(e):
            run("pool", e)

        @block.tensor
        def _(e):
            run("pe", e)

    def close(self):
        self.stack.close()


class Cols:
    def __init__(self):
        self.off = {}
        self.n = 0

    def add(self, name, w):
        self.off[name] = (self.n, w)
        self.n += w


PRM = Cols()
for _n, _w in [("ng", 64), ("convw", 16), ("convb", 4), ("wq", 16), ("wk", 16), ("wv", 16), ("wif", 96), ("bif", 2),
               ("mhg", 4), ("skip", 4), ("are", 16), ("aim", 16), ("ldt", 16), ("bre", 256), ("bim", 256),
               ("cre", 256), ("cim", 256), ("dsk", 4), ("bglu", 4), ("lbraw", 16), ("ggain", 8)]:
    PRM.add(_n, _w)

CST = Cols()
for _n, _w in [("ident", 128), ("ones", 128), ("onesdiv", 128), ("negmask", 128), ("mask64", 64), ("selneg", 512),
               ("r128", NT), ("r64", NT), ("tvec", 128), ("bdmask", 128), ("maskB", 32), ("maskC", 8)]:
    CST.add(_n, _w)

NSTATE = 4 * 129 + 32 + 8 * 128 + 12


def fm(v, nch):
    return np.ascontiguousarray(np.asarray(v, np.float32).reshape(nch, 128).T)


def pack_params(inp):
    prm = np.zeros((128, PRM.n), np.float32)

    def put(name, arr):
        o, w = PRM.off[name]
        arr = np.asarray(arr, np.float32)
        assert arr.shape[1] == w, (name, arr.shape, w)
        prm[:arr.shape[0], o:o + w] = arr

    ng = np.asarray(inp["norm_g"], np.float32)
    put("ng", np.concatenate([fm(ng[l, j], 8) for l in range(2) for j in range(4)], axis=1))
    cw = np.asarray(inp["ab_conv_w"], np.float32)[0]
    put("convw", cw.reshape(4, 4, 128).transpose(2, 1, 0).reshape(128, 16))
    put("convb", fm(inp["ab_conv_b"][0], 4))
    for nm, key in (("wq", "ab_wq"), ("wk", "ab_wk"), ("wv", "ab_wv")):
        w = np.asarray(inp[key], np.float32)[0]
        w = w.reshape(4, 32, 4, 4)
        put(nm, w.transpose(1, 2, 0, 3).reshape(128, 16))
    wif = np.asarray(inp["ab_w_if"], np.float32)[0]
    put("wif", wif.reshape(12, 128, 8).transpose(1, 0, 2).reshape(128, 96))
    bif = np.asarray(inp["ab_b_if"], np.float32)[0]
    put("bif", np.stack([bif[0:4], bif[4:8]], axis=1))
    put("mhg", fm(inp["ab_mh_gain"][0], 4))
    put("skip", fm(inp["ab_skip"][0], 4))

    def sm(a):
        return np.asarray(a, np.float32).reshape(16, 2, 64).transpose(1, 2, 0).reshape(128, 16)

    put("are", sm(inp["ab_a_re"][0]))
    put("aim", sm(inp["ab_a_im"][0]))
    put("ldt", sm(np.repeat(np.asarray(inp["ab_log_dt"], np.float32)[0][:, None], 64, axis=1)))
    for nm, key in (("bre", "ab_b_re"), ("bim", "ab_b_im")):
        b = np.asarray(inp[key], np.float32)[0]
        put(nm, b.reshape(16, 2, 64, 16).transpose(1, 2, 0, 3).reshape(128, 256))
    for nm, key in (("cre", "ab_c_re"), ("cim", "ab_c_im")):
        c = np.asarray(inp[key], np.float32)[0]
        put(nm, c.reshape(4, 128, 64).transpose(1, 0, 2).reshape(128, 256))
    put("dsk", fm(inp["ab_d"][0], 4))
    put("bglu", fm(inp["ab_b_glu"][0], 4))
    lb = np.asarray(inp["c_lb_raw"], np.float32)
    put("lbraw", np.concatenate([fm(lb[0], 8), fm(lb[1], 8)], axis=1))
    put("ggain", fm(inp["c_g_gain"][0], 8))
    return prm


def make_consts():
    c = np.zeros((128, CST.n), np.float32)

    def put(name, arr):
        o, w = CST.off[name]
        arr = np.asarray(arr, np.float32)
        assert arr.shape[1] == w
        c[:arr.shape[0], o:o + w] = arr

    put("ident", np.eye(128))
    put("ones", np.ones((128, 128)))
    put("onesdiv", np.full((128, 128), 1.0 / 128))
    s = np.arange(128)[:, None]
    t = np.arange(128)[None, :]
    put("negmask", np.where(s <= t, 0.0, -30000.0))
    put("mask64", (s[:64] <= t[:, :64]).astype(np.float32))
    sel = np.zeros((4, 4, 128), np.float32)
    for r in range(4):
        sel[r, r, :] = -1.0
    put("selneg", sel.reshape(4, 512))
    tt = np.arange(NT)
    put("r128", np.tile((tt % 128 != 0).astype(np.float32)[None, :], (128, 1)))
    put("r64", np.tile((tt % 64 != 0).astype(np.float32)[None, :], (128, 1)))
    put("tvec", np.tile(np.arange(1, 129, dtype=np.float32)[None, :], (128, 1)))
    p = np.arange(128)
    put("bdmask", (p[:, None] // 4 == p[None, :] // 4).astype(np.float32))
    mb = np.zeros((128, 4, 8), np.float32)
    for e in range(2):
        for v in range(4):
            mb[e * 64:(e + 1) * 64, v, 2 * v + e] = 1.0
    put("maskB", mb.reshape(128, 32))
    mc = np.zeros((128, 4, 2), np.float32)
    for gl in range(8):
        for v in range(4):
            for e in range(2):
                if gl == 2 * v + e:
                    mc[gl * 16:(gl + 1) * 16, v, e] = 1.0
    put("maskC", mc.reshape(128, 8))
    return c


def build_program(n_tiles, with_state=True):
    nc = bass.Bass("TRN2", target_bir_lowering=False)
    T = n_tiles * NT

    def din(name, shape):
        return nc.dram_tensor(name, list(shape), F32, kind="ExternalInput").ap()

    x_d = din("x", [T, D])
    prm_d = din("prm", [128, PRM.n])
    cst_d = din("cst", [128, CST.n])
    st_in = din("st_in", [128, NSTATE])
    w_in0 = din("ab_w_in", [D, 1536])
    w_glu = din("ab_w_glu", [512, 512])
    w_out0 = din("ab_w_out", [D, D])
    w_in1 = din("c_w_in", [D, 4096])
    w_out1 = din("c_w_out", [D, D])
    w1_d = din("ffn_w1", [2, D, DFF])
    w3_d = din("ffn_w3", [2, D, DFF])
    w2_d = din("ffn_w2", [2, DFF, D])
    y_d = nc.dram_tensor("y", [T, D], F32, kind="ExternalOutput").ap()
    st_out = nc.dram_tensor("st_out", [128, NSTATE], F32, kind="ExternalOutput").ap()

    P = Prog(nc)
    op = P.op
    nc_dummy_holder = [P.tile([128, 1], F32, "dummy0")]

    wsc = {}
    conv_ops = []

    def strip_w(K, ncols):
        cw = 512 if K <= 8 else 128
        if ncols % cw:
            cw = 256
        return cw

    def make_scratch(name, w_ap, K, ncols):
        cw = strip_w(K, ncols)
        ns = ncols // cw
        t = nc.dram_tensor("sc_" + name, [ns, 128, K * cw], BF16).ap()
        for s_ in range(ns):
            src = w_ap[:, s_ * cw:(s_ + 1) * cw].rearrange("(k p) n -> p k n", p=128)
            dst = t[s_].rearrange("p (k n) -> p k n", n=cw)
            conv_ops.append(P.dma("pool", lambda e, src=src, dst=dst: e.dma_start(out=dst, in_=src)))
        wsc[name] = (t, K, cw, ns)

    make_scratch("w_in0", w_in0, 8, 1536)
    make_scratch("w_glu", w_glu, 4, 512)
    make_scratch("w_out0", w_out0, 8, D)
    for l in range(2):
        make_scratch(f"w1_{l}", w1_d[l], 8, DFF)
        make_scratch(f"w3_{l}", w3_d[l], 8, DFF)
        make_scratch(f"w2_{l}", w2_d[l], 22, D)
    make_scratch("w_q1", w_in1[:, 0:1024], 8, 1024)
    make_scratch("w_f1", w_in1[:, 1024:2048], 8, 1024)
    make_scratch("w_i1", w_in1[:, 2048:3072], 8, 1024)
    make_scratch("w_g1", w_in1[:, 3072:4096], 8, 1024)
    make_scratch("w_out1", w_out1, 8, D)
    wsc_buf = Buf()
    P.op("pool", lambda e: e.memset(nc_dummy_holder[0].a, 0.0), reads=[], writes=[wsc_buf])
    P.ops["pool"][-1].deps.extend(conv_ops)

    prm = P.tile([128, PRM.n], F32, "prm")
    cst = P.tile([128, CST.n], F32, "cst")

    def pc(name, a=None, b=None):
        o, w = PRM.off[name]
        if a is None:
            return prm.a[:, o:o + w]
        return prm.a[:, o + a:o + b]

    def cc(name, a=None, b=None):
        o, w = CST.off[name]
        if a is None:
            return cst.a[:, o:o + w]
        return cst.a[:, o + a:o + b]

    ident_f = cc("ident")
    ident_bf = P.tile([128, 128], BF16, "ident_bf")
    ones_bf = P.tile([128, 128], BF16, "ones_bf")
    BD = P.tile([128, 3, 4, 128], BF16, "BD")
    wif_bf = P.tile([128, 96], BF16, "wif_bf")
    nbf = P.tile([4, 1], F32, "nbf")
    BBT = P.tile([128, 16, 2, 128], BF16, "BBT")
    CT = P.tile([128, 16, 2, 128], BF16, "CT")
    Ec = P.tile([128, 2048], F32, "Ec")
    Es = P.tile([128, 2048], F32, "Es")
    Rfull = P.tile([128, 2048], F32, "Rfull")
    rcol = P.tile([128, 16], F32, "rcol")
    lbt = P.tile([128, 8], F32, "lbt")
    omlt = P.tile([128, 8], F32, "omlt")
    Cn = P.tile([128, 4, 129], F32, "Cn")
    Cbf = P.tile([128, 4, 128], BF16, "Cbf")
    Nrep = P.tile([128, 4, 128], BF16, "Nrep")
    rx = P.tile([128, 2, 16], F32, "rx")
    S = P.tile([128, 8, 128], F32, "S")
    Sbf = P.tile([128, 8, 128], BF16, "Sbf")
    xio = P.tile([128, NB, D], F32, "xio")
    xT = P.tile([128, 8, NT], F32, "xT")
    hT = P.tile([128, 8, NT], BF16, "hT")
    mix = P.tile([128, 8, NT], F32, "mix")
    cat = P.tile([128, 8, NT], BF16, "cat")
    wbufs = Rot([P.tile([128, 4096], BF16, f"wb{i}") for i in range(3)])
    scrF = Rot([P.tile([128, NT], F32, f"sF{i}") for i in range(12)])
    scrB = Rot([P.tile([128, NT], BF16, f"sB{i}") for i in range(6)])
    scrS = Rot([P.tile([128, 129], BF16, f"sS{i}") for i in range(6)])
    rotP = Rot([P.psum_bank() for _ in range(6)])
    accP = Rot([P.psum_bank() for _ in range(2)])

    ARENA_BYTES = 64 * 1024
    arena = P.tile([128, ARENA_BYTES // 4], F32, "arena")
    phase_buf = Buf()

    class Carver:
        def __init__(self):
            self.off = 0

        def carve(self, shape, dtype):
            n = int(np.prod(shape[1:]))
            isz = 4 if dtype in (F32, I32) else 2
            nbytes = (n * isz + 31) // 32 * 32
            assert self.off + nbytes <= ARENA_BYTES, (self.off, nbytes)
            a = arena.a[:, self.off // 4:(self.off + nbytes) // 4]
            if dtype != F32:
                a = a.bitcast(dtype)
            a = a[:, 0:n]
            if len(shape) == 3:
                a = a.rearrange("p (a b) -> p a b", b=shape[2])
            elif len(shape) == 4:
                a = a.rearrange("p (a b c) -> p a b c", b=shape[2], c=shape[3])
            self.off += nbytes
            return TL(a, Buf())

    cv = Carver()
    xme = cv.carve([128, 4, NT + 8], F32)
    xc = cv.carve([128, 4, NT], F32)
    szm = cv.carve([128, 4, NT], F32)
    usf = cv.carve([128, 4, NT], F32)
    yf = cv.carve([128, 4, NT], F32)
    usb = cv.carve([128, 4, NT], BF16)
    xcb = cv.carve([128, 4, NT], BF16)
    xmb = cv.carve([128, 4, NT], BF16)
    qT = cv.carve([128, 4, NT], BF16)
    kT = cv.carve([128, 4, NT], BF16)
    vT = cv.carve([128, 4, NT], BF16)
    ygb = cv.carve([128, 4, NT], BF16)
    vtok = cv.carve([128, NB, 4, 129], BF16)
    Ig = cv.carve([128, NT], F32)
    Lf = cv.carve([128, NT], F32)
    Bn = cv.carve([128, NT], F32)
    Cm = cv.carve([128, NT], F32)
    Am = cv.carve([128, NT], F32)
    csT = cv.carve([128, NB, 4], F32)
    eaT = cv.carve([128, NB, 4], F32)
    scrW = Rot([cv.carve([128, 512], F32) for _ in range(8)])
    scrWB = Rot([cv.carve([128, 512], BF16) for _ in range(4)])
    l0_bytes = cv.off
    cv = Carver()
    qsl = cv.carve([128, 8, NT], F32)
    sgf = cv.carve([128, 8, NT], F32)
    vt64 = cv.carve([128, NC64, D], BF16)
    sgl = cv.carve([128, 8, NT], BF16)
    cv = Carver()
    hid = cv.carve([128, 22, NT], BF16)
    halo = P.tile([128, 4, 3], F32, "halo")
    dummy = P.tile([128, 1], F32, "dummy")

    def AR(t):
        return t

    def phase_barrier():
        op("pool", lambda e: e.memset(dummy.a, 0.0), reads=[], writes=[dummy, phase_buf])

    arena_ids = set()

    def mark(*tls):
        for t in tls:
            arena_ids.add(id(t.b))

    def aop(eng, fn, reads=(), writes=()):
        rs = list(reads)
        if any(id(t.b) in arena_ids for t in list(reads) + list(writes) if isinstance(t, TL)):
            rs.append(phase_buf)
        return P.op(eng, fn, reads=rs, writes=writes)

    op = aop
    mark(xme, xc, szm, usf, yf, usb, xcb, xmb, qT, kT, vT, ygb, vtok, Ig, Lf, Bn, Cm, Am, csT, eaT,
         qsl, sgf, vt64, sgl, hid, *scrW.tiles, *scrWB.tiles)

    cnt = {"evac": 0}

    def copy_any(out_ap, in_ap, reads, writes, engs=("act", "dve")):
        e = engs[cnt["evac"] % len(engs)]
        cnt["evac"] += 1
        if e == "act":
            op("act", lambda en: en.activation(out=out_ap, in_=in_ap, func=AF.Copy), reads=reads, writes=writes)
        elif e == "dve":
            op("dve", lambda en: en.tensor_copy(out=out_ap, in_=in_ap), reads=reads, writes=writes)
        else:
            op("pool", lambda en: en.tensor_copy(out=out_ap, in_=in_ap), reads=reads, writes=writes)

    def act(out_ap, in_ap, func, reads, writes, bias=None, scale=None):
        kw = {}
        if bias is not None:
            kw["bias"] = bias
        if scale is not None:
            kw["scale"] = scale
        op("act", lambda en: en.activation(out=out_ap, in_=in_ap, func=func, **kw), reads=reads, writes=writes)

    def tt(eng, out_ap, a_ap, b_ap, alu, reads, writes):
        op(eng, lambda en: en.tensor_tensor(out=out_ap, in0=a_ap, in1=b_ap, op=alu), reads=reads, writes=writes)

    def ts(eng, out_ap, a_ap, s1, s2, o0, o1, reads, writes):
        if o1 is None:
            op(eng, lambda en: en.tensor_scalar(out=out_ap, in0=a_ap, scalar1=s1, scalar2=None, op0=o0), reads=reads, writes=writes)
        else:
            op(eng, lambda en: en.tensor_scalar(out=out_ap, in0=a_ap, scalar1=s1, scalar2=s2, op0=o0, op1=o1), reads=reads, writes=writes)

    def stt(out_ap, a_ap, sc, b_ap, o0, o1, reads, writes):
        op("dve", lambda en: en.scalar_tensor_tensor(out=out_ap, in0=a_ap, scalar=sc, in1=b_ap, op0=o0, op1=o1), reads=reads, writes=writes)

    def mm(out_ap, lhsT, rhs, start, stop, reads, writes):
        op("pe", lambda en: en.matmul(out_ap, lhsT=lhsT, rhs=rhs, start=start, stop=stop), reads=reads, writes=writes)

    def tr(out_ap, in_ap, ident_ap, reads, writes):
        op("pe", lambda en: en.transpose(out_ap, in_ap, ident_ap), reads=reads, writes=writes)

    def recip(out_ap, in_ap, reads, writes):
        op("dve", lambda en: en.reciprocal(out=out_ap, in_=in_ap), reads=reads, writes=writes)

    P.dma("sp", lambda e: e.dma_start(out=prm.a, in_=prm_d), writes=[prm])
    P.dma("sp", lambda e: e.dma_start(out=cst.a, in_=cst_d), writes=[cst])
    op("act", lambda e: e.activation(out=ident_bf.a, in_=cc("ident"), func=AF.Copy), reads=[cst], writes=[ident_bf])
    op("act", lambda e: e.activation(out=ones_bf.a, in_=cc("ones"), func=AF.Copy), reads=[cst], writes=[ones_bf])
    op("act", lambda e: e.activation(out=wif_bf.a, in_=pc("wif"), func=AF.Copy), reads=[prm], writes=[wif_bf])
    ts("dve", nbf.a, pc("bif", 1, 2)[0:4, :], -1.0, None, ALU.mult, None, [prm], [nbf])
    for wi, nm in enumerate(("wq", "wk", "wv")):
        scale = float(128 ** -0.5) if nm == "wk" else 1.0
        for h in range(4):
            stt(BD.a[:, wi, h, :].rearrange("p (n e) -> p n e", e=4), cc("bdmask").rearrange("p (n e) -> p n e", e=4), scale,
                pc(nm, h * 4, h * 4 + 4).unsqueeze(1).broadcast_to([128, 32, 4]), ALU.mult, ALU.mult, [cst, prm], [BD])

    cv = Carver()
    sp = {n: cv.carve([128, 16], F32) for n in
          ["dt", "ard", "mag", "th", "cos", "sin", "abr", "abi", "den", "inv", "abr1", "t1", "t2", "gre", "gim"]}
    bbr = cv.carve([128, 256], F32)
    bbi = cv.carve([128, 256], F32)
    tmpA = cv.carve([128, 256], F32)
    ki = cv.carve([128, 128], I32)
    kf = cv.carve([128, 128], F32)
    ph = cv.carve([128, 128], F32)
    phi = cv.carve([128, 128], F32)
    Zs = Rot([cv.carve([128, 128], F32) for i in range(2)])
    mark(bbr, bbi, tmpA, ki, kf, ph, phi, *Zs.tiles, *sp.values())

    def sin_of(dst_ap, dst_tl, src_ap, src_tl, add, w):
        p_ = ph.a[:, 0:w]
        k_ = ki.a[:, 0:w]
        f_ = kf.a[:, 0:w]
        ts("dve", p_, src_ap, float(add), None, ALU.add, None, [src_tl], [ph])
        ts("dve", k_, p_, 1.0 / (2 * PI), None, ALU.mult, None, [ph], [ki])
        op("dve", lambda e: e.tensor_copy(out=f_, in_=k_), reads=[ki], writes=[kf])
        stt(p_, f_, -2 * PI, p_, ALU.mult, ALU.add, [kf, ph], [ph])
        ts("dve", f_, p_, PI, -2 * PI, ALU.is_gt, ALU.mult, [ph], [kf])
        tt("dve", p_, p_, f_, ALU.add, [ph, kf], [ph])
        ts("dve", f_, p_, -PI, 2 * PI, ALU.is_lt, ALU.mult, [ph], [kf])
        tt("dve", p_, p_, f_, ALU.add, [ph, kf], [ph])
        act(dst_ap, p_, AF.Sin, [ph], [dst_tl])

    act(sp["dt"].a, pc("ldt"), AF.Exp, [prm], [sp["dt"]])
    tt("dve", sp["ard"].a, pc("are"), sp["dt"].a, ALU.mult, [prm, sp["dt"]], [sp["ard"]])
    act(sp["mag"].a, sp["ard"].a, AF.Exp, [sp["ard"]], [sp["mag"]])
    tt("dve", sp["th"].a, pc("aim"), sp["dt"].a, ALU.mult, [prm, sp["dt"]], [sp["th"]])
    sin_of(sp["cos"].a, sp["cos"], sp["th"].a, sp["th"], PI / 2, 16)
    sin_of(sp["sin"].a, sp["sin"], sp["th"].a, sp["th"], 0.0, 16)
    tt("dve", sp["abr"].a, sp["mag"].a, sp["cos"].a, ALU.mult, [sp["mag"], sp["cos"]], [sp["abr"]])
    tt("dve", sp["abi"].a, sp["mag"].a, sp["sin"].a, ALU.mult, [sp["mag"], sp["sin"]], [sp["abi"]])
    tt("dve", sp["den"].a, pc("are"), pc("are"), ALU.mult, [prm], [sp["den"]])
    tt("dve", sp["t1"].a, pc("aim"), pc("aim"), ALU.mult, [prm], [sp["t1"]])
    tt("dve", sp["den"].a, sp["den"].a, sp["t1"].a, ALU.add, [sp["den"], sp["t1"]], [sp["den"]])
    recip(sp["inv"].a, sp["den"].a, [sp["den"]], [sp["inv"]])
    ts("dve", sp["abr1"].a, sp["abr"].a, -1.0, None, ALU.add, None, [sp["abr"]], [sp["abr1"]])
    tt("dve", sp["t1"].a, sp["abr1"].a, pc("are"), ALU.mult, [sp["abr1"], prm], [sp["t1"]])
    tt("dve", sp["t2"].a, sp["abi"].a, pc("aim"), ALU.mult, [sp["abi"], prm], [sp["t2"]])
    tt("dve", sp["t1"].a, sp["t1"].a, sp["t2"].a, ALU.add, [sp["t1"], sp["t2"]], [sp["t1"]])
    tt("dve", sp["gre"].a, sp["t1"].a, sp["inv"].a, ALU.mult, [sp["t1"], sp["inv"]], [sp["gre"]])
    tt("dve", sp["t1"].a, sp["abi"].a, pc("are"), ALU.mult, [sp["abi"], prm], [sp["t1"]])
    tt("dve", sp["t2"].a, sp["abr1"].a, pc("aim"), ALU.mult, [sp["abr1"], prm], [sp["t2"]])
    tt("dve", sp["t1"].a, sp["t1"].a, sp["t2"].a, ALU.subtract, [sp["t1"], sp["t2"]], [sp["t1"]])
    tt("dve", sp["gim"].a, sp["t1"].a, sp["inv"].a, ALU.mult, [sp["t1"], sp["inv"]], [sp["gim"]])

    def v3(ap2d):
        return ap2d.rearrange("p (m q) -> p m q", q=16)

    def bc_m(tl):
        return tl.a.unsqueeze(2).broadcast_to([128, 16, 16])

    tt("dve", v3(bbr.a), v3(pc("bre")), bc_m(sp["gre"]), ALU.mult, [prm, sp["gre"]], [bbr])
    tt("dve", v3(tmpA.a), v3(pc("bim")), bc_m(sp["gim"]), ALU.mult, [prm, sp["gim"]], [tmpA])
    tt("dve", bbr.a, bbr.a, tmpA.a, ALU.subtract, [bbr, tmpA], [bbr])
    tt("dve", v3(bbi.a), v3(pc("bim")), bc_m(sp["gre"]), ALU.mult, [prm, sp["gre"]], [bbi])
    tt("dve", v3(tmpA.a), v3(pc("bre")), bc_m(sp["gim"]), ALU.mult, [prm, sp["gim"]], [tmpA])
    tt("dve", bbi.a, bbi.a, tmpA.a, ALU.add, [bbi, tmpA], [bbi])
    op("dve", lambda e: e.tensor_copy(out=rcol.a, in_=sp["mag"].a), reads=[sp["mag"]], writes=[rcol])
    for m in range(16):
        ts("dve", phi.a, cc("tvec"), sp["th"].a[:, m:m + 1], None, ALU.mult, None, [cst, sp["th"]], [phi])
        sin_of(Ec.a[:, m * 128:(m + 1) * 128], Ec, phi.a, phi, PI / 2, 128)
        sin_of(Es.a[:, m * 128:(m + 1) * 128], Es, phi.a, phi, 0.0, 128)
        ts("dve", Rfull.a[:, m * 128:(m + 1) * 128], cc("ones"), sp["mag"].a[:, m:m + 1], None, ALU.mult, None, [cst, sp["mag"]], [Rfull])
    op("dve", lambda e: e.memset(Rfull.a.rearrange("p (m t) -> p m t", t=128)[:, :, 0:1], 0.0), reads=[], writes=[Rfull])
    for m in range(16):
        v = m % 4
        jj = m // 4
        for ri, bb in enumerate((bbr, bbi)):
            Z = Zs.get()
            tt("dve", Z.a.rearrange("p (g q) -> p g q", q=16), cc("maskB", v * 8, v * 8 + 8).unsqueeze(2).broadcast_to([128, 8, 16]),
               bb.a[:, m * 16:(m + 1) * 16].unsqueeze(1).broadcast_to([128, 8, 16]), ALU.mult, [cst, bb], [Z])
            bk = rotP.get()
            tr(bk.a[:, 0:128], Z.a, ident_f, [Z, cst], [bk])
            copy_any(BBT.a[:, m, ri, :], bk.a[:, 0:128], [bk], [BBT])
        for ri, cn in enumerate(("cre", "cim")):
            Z = Zs.get()
            sgn = -1.0 if ri == 1 else 1.0
            for e_ in range(2):
                ts("dve", Z.a[:, e_ * 64:(e_ + 1) * 64], pc(cn, jj * 64, jj * 64 + 64), cc("maskC", v * 2 + e_, v * 2 + e_ + 1), sgn,
                   ALU.mult, ALU.mult, [prm, cst], [Z])
            bk = rotP.get()
            tr(bk.a[:, 0:128], Z.a, ident_f, [Z, cst], [bk])
            copy_any(CT.a[:, m, ri, :], bk.a[:, 0:128], [bk], [CT])
    tt("dve", lbt.a, pc("lbraw", 8, 16), pc("lbraw", 0, 8), ALU.subtract, [prm], [lbt])
    act(lbt.a, lbt.a, AF.Sigmoid, [lbt], [lbt])
    ts("dve", omlt.a, lbt.a, -1.0, 1.0, ALU.mult, ALU.add, [lbt], [omlt])
    P.dma("sp", lambda e: e.dma_start(out=Cn.a.rearrange("p h v -> p (h v)"), in_=st_in[:, 0:516]), writes=[Cn])
    P.dma("sp", lambda e: e.dma_start(out=rx.a.rearrange("p a m -> p (a m)"), in_=st_in[:, 516:548]), writes=[rx])
    P.dma("sp", lambda e: e.dma_start(out=S.a.rearrange("p h v -> p (h v)"), in_=st_in[:, 548:1572]), writes=[S])
    P.dma("sp", lambda e: e.dma_start(out=halo.a.rearrange("p h k -> p (h k)"), in_=st_in[:, 1572:1584]), writes=[halo])
    op("act", lambda e: e.activation(out=Sbf.a, in_=S.a, func=AF.Copy), reads=[S], writes=[Sbf])
    op("act", lambda e: e.activation(out=Cbf.a, in_=Cn.a[:, :, 0:128], func=AF.Copy), reads=[Cn], writes=[Cbf])
    for h in range(4):
        act(Nrep.a[:, h, :], cc("ones"), AF.Copy, [cst, Cn], [Nrep], scale=Cn.a[:, h, 128:129])

    def rms_rstd(src_chunks, src_tls, nfeat):
        bk = rotP.get()
        n = len(src_chunks)
        for c, ap_ in enumerate(src_chunks):
            sq = scrB.get()
            act(sq.a, ap_, AF.Square, src_tls, [sq])
            mm(bk.a[:, 0:NT], ones_bf.a, sq.a, c == 0, c == n - 1, [ones_bf, sq], [bk])
        s = scrF.get()
        act(s.a, bk.a[:, 0:NT], AF.Sqrt, [bk], [s], bias=EPS, scale=1.0 / nfeat)
        recip(s.a, s.a, [s], [s])
        return s

    def prenorm(goff):
        r = rms_rstd([xT.a[:, c, :] for c in range(8)], [xT], D)
        for c in range(8):
            stt(hT.a[:, c, :], xT.a[:, c, :], pc("ng", goff + c, goff + c + 1), r.a, ALU.mult, ALU.mult, [xT, prm, r], [hT])

    def postnorm(goff):
        r = rms_rstd([mix.a[:, c, :] for c in range(8)], [mix], D)
        for c in range(8):
            t = scrF.get()
            stt(t.a, mix.a[:, c, :], pc("ng", goff + c, goff + c + 1), r.a, ALU.mult, ALU.mult, [mix, prm, r], [t])
            tt("pool", xT.a[:, c, :], xT.a[:, c, :], t.a, ALU.add, [xT, t], [xT])

    def proj(wname, K, ncols, rhs_fn, rhs_tls, consume):
        t, K_, cw, ns = wsc[wname]
        assert K_ == K and ns * cw == ncols
        for s in range(ns):
            wb = wbufs.get()
            wv = wb.a[:, 0:K * cw].rearrange("p (k n) -> p k n", n=cw)
            src = t[s]
            P.dma("sp", lambda e, wb=wb, src=src, n=K * cw: e.dma_start(out=wb.a[:, 0:n], in_=src), reads=[wsc_buf], writes=[wb])
            for e_ in range(cw // 128):
                bk = rotP.get()
                for k in range(K):
                    mm(bk.a[:, 0:NT], wv[:, k, e_ * 128:(e_ + 1) * 128], rhs_fn(k), k == 0, k == K - 1, [wb] + rhs_tls, [bk])
                consume(s * (cw // 128) + e_, bk)

    def proj_tok(wname, consume):
        t, K_, cw, ns = wsc[wname]
        for s in range(ns):
            wb = wbufs.get()
            wv = wb.a[:, 0:8 * cw].rearrange("p (k n) -> p k n", n=cw)
            src = t[s]
            P.dma("sp", lambda e, wb=wb, src=src, n=8 * cw: e.dma_start(out=wb.a[:, 0:n], in_=src), reads=[wsc_buf], writes=[wb])
            for c in range(NC64):
                bk = rotP.get()
                for k in range(8):
                    mm(bk.a[0:64, 0:512], hT.a[:, k, c * 64:(c + 1) * 64], wv[:, k, :], k == 0, k == 7, [wb, hT], [bk])
                consume(s, c, bk)

    def ffn(l):
        phase_barrier()
        prenorm((l * 4 + 2) * 8)

        def c1(e_, bk):
            act(hid.a[:, e_, :], bk.a[:, 0:NT], AF.Silu, [bk], [hid])

        proj(f"w1_{l}", 8, DFF, lambda k: hT.a[:, k, :], [hT], c1)

        def c3(e_, bk):
            tt("dve", hid.a[:, e_, :], hid.a[:, e_, :], bk.a[:, 0:NT], ALU.mult, [hid, bk], [hid])

        proj(f"w3_{l}", 8, DFF, lambda k: hT.a[:, k, :], [hT], c3)

        def c2(e_, bk):
            copy_any(mix.a[:, e_, :], bk.a[:, 0:NT], [bk], [mix])

        proj(f"w2_{l}", 22, D, lambda k: hid.a[:, k, :], [hid], c2)
        postnorm((l * 4 + 3) * 8)

    def layer0_mixer():
        phase_barrier()
        prenorm(0)
        for h in range(4):
            op("act", lambda e, h=h: e.activation(out=xme.a[:, h, 5:8], in_=halo.a[:, h, :], func=AF.Copy), reads=[halo], writes=[xme])

        def c_in(e_, bk):
            if e_ < 4:
                copy_any(xme.a[:, e_, 8:8 + NT], bk.a[:, 0:NT], [bk], [xme])
            elif e_ < 8:
                act(szm.a[:, e_ - 4, :], bk.a[:, 0:NT], AF.Silu, [bk], [szm])
            else:
                act(usf.a[:, e_ - 8, :], bk.a[:, 0:NT], AF.Copy, [bk], [usf])
                op("dve", lambda en: en.tensor_copy(out=usb.a[:, e_ - 8, :], in_=bk.a[:, 0:NT]), reads=[bk], writes=[usb])

        if "l0a0" in STAGES:
            return
        proj("w_in0", 8, 1536, lambda k: hT.a[:, k, :], [hT], c_in)
        if "l0a1" in STAGES:
            return
        for h in range(4):
            acc = scrF.get()
            ts("dve", acc.a, xme.a[:, h, 5:5 + NT], pc("convw", h * 4, h * 4 + 1), None, ALU.mult, None, [xme, prm], [acc])
            for k in range(1, 4):
                stt(acc.a, xme.a[:, h, 5 + k:5 + k + NT], pc("convw", h * 4 + k, h * 4 + k + 1), acc.a, ALU.mult, ALU.add, [xme, prm, acc], [acc])
            act(xc.a[:, h, :], acc.a, AF.Silu, [acc, prm], [xc], bias=pc("convb", h, h + 1))
            op("act", lambda e, h=h: e.activation(out=xcb.a[:, h, :], in_=xc.a[:, h, :], func=AF.Copy), reads=[xc], writes=[xcb])
            op("act", lambda e, h=h: e.activation(out=xmb.a[:, h, :], in_=xme.a[:, h, 8:8 + NT], func=AF.Copy), reads=[xme], writes=[xmb])
            op("act", lambda e, h=h: e.activation(out=halo.a[:, h, :], in_=xme.a[:, h, NT + 5:NT + 8], func=AF.Copy), reads=[xme], writes=[halo])
        if "l0a" in STAGES:
            return
        for h in range(4):
            for wi, (src, dst) in enumerate(((xcb, qT), (xcb, kT), (xmb, vT))):
                bk = rotP.get()
                mm(bk.a[:, 0:NT], BD.a[:, wi, h, :], src.a[:, h, :], True, True, [BD, src], [bk])
                copy_any(dst.a[:, h, :], bk.a[:, 0:NT], [bk], [dst])
        op("pool", lambda e: e.memset(vtok.a[:, :, :, 128:129], 1.0), reads=[], writes=[vtok])
        for nb in range(NB):
            bk = rotP.get()
            for h in range(4):
                mm(bk.a[:, h * 128:(h + 1) * 128], xmb.a[:, h, nb * 128:(nb + 1) * 128], BD.a[:, 2, h, :], True, True, [xmb, BD], [bk])
            copy_any(vtok.a[:, nb, :, 0:128], bk.a[:, 0:512].rearrange("p (h v) -> p h v", v=128), [bk], [vtok])
        if "l0b" in STAGES:
            return
        bi = rotP.get()
        bf_ = rotP.get()
        for c in range(12):
            src = (qT, kT, vT)[c // 4]
            mm(bi.a[0:4, 0:NT], wif_bf.a[:, c * 8:c * 8 + 4], src.a[:, c % 4, :], c == 0, c == 11, [wif_bf, src], [bi])
        for c in range(12):
            src = (qT, kT, vT)[c // 4]
            mm(bf_.a[0:4, 0:NT], wif_bf.a[:, c * 8 + 4:c * 8 + 8], src.a[:, c % 4, :], c == 0, c == 11, [wif_bf, src], [bf_])
        act(Ig.a[0:4, :], bi.a[0:4, 0:NT], AF.Identity, [bi, prm], [Ig], bias=pc("bif", 0, 1)[0:4, :])
        act(Lf.a[0:4, :], bf_.a[0:4, 0:NT], AF.Exp, [bf_, nbf], [Lf], bias=nbf.a, scale=-1.0)
        act(Lf.a[0:4, :], Lf.a[0:4, :], AF.Ln, [Lf], [Lf], bias=1.0)
        op("dve", lambda e: e.tensor_tensor_scan(out=Bn.a[0:4, :], data0=cc("r128")[0:4, :], data1=Lf.a[0:4, :], initial=0.0,
                                                 op0=ALU.mult, op1=ALU.add), reads=[cst, Lf], writes=[Bn])
        tt("dve", Cm.a[0:4, :], Ig.a[0:4, :], Bn.a[0:4, :], ALU.add, [Ig, Bn], [Cm])
        for c in range(NB):
            cs = slice(c * 128, (c + 1) * 128)
            ts("dve", Am.a[0:4, cs], Cm.a[0:4, cs], Bn.a[0:4, c * 128 + 127:c * 128 + 128], None, ALU.subtract, None, [Cm, Bn], [Am])
        for nb in range(NB):
            cs = slice(nb * 128, (nb + 1) * 128)
            bk = rotP.get()
            tr(bk.a[:, 0:4], Cm.a[0:4, cs], ident_f[0:4, 0:4], [Cm, cst], [bk])
            tr(bk.a[:, 4:8], Am.a[0:4, cs], ident_f[0:4, 0:4], [Am, cst], [bk])
            op("dve", lambda e, bk=bk, nb=nb: e.tensor_copy(out=csT.a[:, nb, :], in_=bk.a[:, 0:4]), reads=[bk], writes=[csT])
            act(eaT.a[:, nb, :], bk.a[:, 4:8], AF.Exp, [bk], [eaT])
        if "l0c" in STAGES:
            return
        for h in range(4):
            bB = rotP.get()
            mm(bB.a[:, 0:NT], cc("selneg", h * 128, h * 128 + 128)[0:4, :], Bn.a[0:4, :], True, True, [cst, Bn], [bB])
            ExpB = scrF.get()
            act(ExpB.a, bB.a[:, 0:NT], AF.Exp, [bB], [ExpB])
            qs = scrB.get()
            tt("dve", qs.a, qT.a[:, h, :], ExpB.a, ALU.mult, [qT, ExpB], [qs])
            bN = accP.get()
            bD = accP.get()
            for c in range(NB):
                cs = slice(c * 128, (c + 1) * 128)
                bE = rotP.get()
                mm(bE.a[:, 0:128], cc("selneg", h * 128, h * 128 + 128)[0:4, :], Bn.a[0:4, cs], True, False, [cst, Bn], [bE])
                mm(bE.a[:, 0:128], ident_f, cc("negmask"), False, True, [cst], [bE])
                E = scrF.get()
                act(E.a[:, 0:128], bE.a[:, 0:128], AF.Exp, [bE, csT], [E], bias=csT.a[:, c, h:h + 1])
                bS = rotP.get()
                mm(bS.a[:, 0:128], kT.a[:, h, cs], qT.a[:, h, cs], True, True, [kT, qT], [bS])
                PT = scrS.get()
                tt("dve", PT.a[:, 0:128], bS.a[:, 0:128], E.a[:, 0:128], ALU.mult, [bS, E], [PT])
                mm(bN.a[:, cs], vtok.a[:, c, h, 0:128], PT.a[:, 0:128], True, False, [vtok, PT], [bN])
                mm(bN.a[:, cs], Cbf.a[:, h, :], qs.a[:, cs], False, True, [Cbf, qs], [bN])
                mm(bD.a[:, cs], ones_bf.a, PT.a[:, 0:128], True, False, [ones_bf, PT], [bD])
                mm(bD.a[:, cs], Nrep.a[:, h, :], qs.a[:, cs], False, True, [Nrep, qs], [bD])
                bK = rotP.get()
                mm(bK.a[:, 0:128], xcb.a[:, h, cs], BD.a[:, 1, h, :], True, True, [xcb, BD], [bK])
                kw = scrS.get()
                act(kw.a[:, 0:128], bK.a[:, 0:128], AF.Copy, [bK, eaT], [kw], scale=eaT.a[:, c, h:h + 1])
                bU = rotP.get()
                mm(bU.a[:, 0:129], kw.a[:, 0:128], vtok.a[:, c, h, :], True, True, [kw, vtok], [bU])
                stt(Cn.a[:, h, :], Cn.a[:, h, :], ExpB.a[:, c * 128 + 127:c * 128 + 128], bU.a[:, 0:129], ALU.mult, ALU.add, [Cn, ExpB, bU], [Cn])
                act(Cbf.a[:, h, :], Cn.a[:, h, 0:128], AF.Copy, [Cn], [Cbf])
                act(Nrep.a[:, h, :], cc("ones"), AF.Copy, [cst, Cn], [Nrep], scale=Cn.a[:, h, 128:129])
            ad = scrF.get()
            act(ad.a, bD.a[:, 0:NT], AF.Abs, [bD], [ad])
            ts("dve", ad.a, ad.a, 1.0, None, ALU.max, None, [ad], [ad])
            recip(ad.a, ad.a, [ad], [ad])
            hb = scrF.get()
            tt("dve", hb.a, bN.a[:, 0:NT], ad.a, ALU.mult, [bN, ad], [hb])
            bM = rotP.get()
            mm(bM.a[:, 0:NT], cc("onesdiv"), hb.a, True, True, [cst, hb], [bM])
            dd = scrF.get()
            tt("dve", dd.a, hb.a, bM.a[:, 0:NT], ALU.subtract, [hb, bM], [dd])
            sq = scrF.get()
            tt("pool", sq.a, dd.a, dd.a, ALU.mult, [dd], [sq])
            bV = rotP.get()
            mm(bV.a[:, 0:NT], cc("onesdiv"), sq.a, True, True, [cst, sq], [bV])
            sd = scrF.get()
            act(sd.a, bV.a[:, 0:NT], AF.Sqrt, [bV], [sd], bias=EPS)
            recip(sd.a, sd.a, [sd], [sd])
            stt(dd.a, dd.a, pc("mhg", h, h + 1), sd.a, ALU.mult, ALU.mult, [dd, prm, sd], [dd])
            stt(dd.a, xc.a[:, h, :], pc("skip", h, h + 1), dd.a, ALU.mult, ALU.add, [xc, prm, dd], [dd])
            tt("dve", cat.a[:, h, :], dd.a, szm.a[:, h, :], ALU.mult, [dd, szm], [cat])
        if "l0d" in STAGES:
            return
        for jj in range(4):
            Ecq = Ec.a[:, jj * 512:(jj + 1) * 512]
            Esq = Es.a[:, jj * 512:(jj + 1) * 512]
            Rq = Rfull.a[:, jj * 512:(jj + 1) * 512]
            for sb in range(NB):
                cs = slice(sb * 128, (sb + 1) * 128)
                bR = rotP.get()
                bI = rotP.get()
                for mi in range(4):
                    m = jj * 4 + mi
                    mm(bR.a[:, mi * 128:(mi + 1) * 128], BBT.a[:, m, 0, :], usb.a[:, jj, cs], True, True, [BBT, usb], [bR])
                for mi in range(4):
                    m = jj * 4 + mi
                    mm(bI.a[:, mi * 128:(mi + 1) * 128], BBT.a[:, m, 1, :], usb.a[:, jj, cs], True, True, [BBT, usb], [bI])
                bre = scrW.get()
                bim = scrW.get()
                act(bre.a, bR.a, AF.Copy, [bR], [bre])
                act(bim.a, bI.a, AF.Copy, [bI], [bim])
                t1 = scrW.get()
                t2 = scrW.get()
                t3 = scrW.get()
                t4 = scrW.get()
                tt("dve", t1.a, Ecq, bre.a, ALU.mult, [Ec, bre], [t1])
                tt("dve", t2.a, Esq, bim.a, ALU.mult, [Es, bim], [t2])
                tt("dve", t1.a, t1.a, t2.a, ALU.add, [t1, t2], [t1])
                tt("pool", t3.a, Ecq, bim.a, ALU.mult, [Ec, bim], [t3])
                tt("pool", t4.a, Esq, bre.a, ALU.mult, [Es, bre], [t4])
                tt("pool", t3.a, t3.a, t4.a, ALU.subtract, [t3, t4], [t3])

                def v3w(t):
                    return t.a.rearrange("p (m t) -> p m t", t=128)

                tt("dve", v3w(t1)[:, :, 0:1], v3w(t1)[:, :, 0:1], rx.a[:, 0, jj * 4:(jj + 1) * 4].unsqueeze(2), ALU.add, [t1, rx], [t1])
                tt("pool", v3w(t3)[:, :, 0:1], v3w(t3)[:, :, 0:1], rx.a[:, 1, jj * 4:(jj + 1) * 4].unsqueeze(2), ALU.add, [t3, rx], [t3])
                zr = scrW.get()
                zi = scrW.get()
                op("dve", lambda e, zr=zr, t1=t1, Rq=Rq: e.tensor_tensor_scan(out=zr.a, data0=Rq, data1=t1.a, initial=0.0, op0=ALU.mult, op1=ALU.add),
                   reads=[Rfull, t1], writes=[zr])
                op("dve", lambda e, zi=zi, t3=t3, Rq=Rq: e.tensor_tensor_scan(out=zi.a, data0=Rq, data1=t3.a, initial=0.0, op0=ALU.mult, op1=ALU.add),
                   reads=[Rfull, t3], writes=[zi])
                u1 = scrW.get()
                u2 = scrW.get()
                u3 = scrW.get()
                u4 = scrW.get()
                tt("dve", u1.a, Ecq, zr.a, ALU.mult, [Ec, zr], [u1])
                tt("pool", u2.a, Esq, zi.a, ALU.mult, [Es, zi], [u2])
                tt("dve", u1.a, u1.a, u2.a, ALU.subtract, [u1, u2], [u1])
                tt("pool", u3.a, Ecq, zi.a, ALU.mult, [Ec, zi], [u3])
                tt("dve", u4.a, Esq, zr.a, ALU.mult, [Es, zr], [u4])
                tt("pool", u3.a, u3.a, u4.a, ALU.add, [u3, u4], [u3])
                tt("dve", rx.a[:, 0, jj * 4:(jj + 1) * 4].unsqueeze(2), v3w(u1)[:, :, 127:128], rcol.a[:, jj * 4:(jj + 1) * 4].unsqueeze(2), ALU.mult, [u1, rcol], [rx])
                tt("pool", rx.a[:, 1, jj * 4:(jj + 1) * 4].unsqueeze(2), v3w(u3)[:, :, 127:128], rcol.a[:, jj * 4:(jj + 1) * 4].unsqueeze(2), ALU.mult, [u3, rcol], [rx])
                xrb = scrWB.get()
                xib = scrWB.get()
                act(xrb.a, u1.a, AF.Copy, [u1], [xrb])
                act(xib.a, u3.a, AF.Copy, [u3], [xib])
                bY = rotP.get()
                for mi in range(4):
                    m = jj * 4 + mi
                    mm(bY.a[:, 0:128], CT.a[:, m, 0, :], xrb.a[:, mi * 128:(mi + 1) * 128], mi == 0, False, [CT, xrb], [bY])
                    mm(bY.a[:, 0:128], CT.a[:, m, 1, :], xib.a[:, mi * 128:(mi + 1) * 128], False, mi == 3, [CT, xib], [bY])
                stt(yf.a[:, jj, cs], usf.a[:, jj, cs], pc("dsk", jj, jj + 1), bY.a[:, 0:128], ALU.mult, ALU.add, [usf, prm, bY], [yf])
        if "l0e" in STAGES:
            return
        for jj in range(4):
            y_ = yf.a[:, jj, :]
            x2 = scrF.get()
            tt("pool", x2.a, y_, y_, ALU.mult, [yf], [x2])
            ts("dve", x2.a, x2.a, 0.044715, 1.0, ALU.mult, ALU.add, [x2], [x2])
            tt("dve", x2.a, x2.a, y_, ALU.mult, [x2, yf], [x2])
            act(x2.a, x2.a, AF.Sigmoid, [x2], [x2], scale=1.5957691216057308)
            tt("dve", y_, y_, x2.a, ALU.mult, [yf, x2], [yf])
            op("act", lambda e, jj=jj: e.activation(out=ygb.a[:, jj, :], in_=yf.a[:, jj, :], func=AF.Copy), reads=[yf], writes=[ygb])

        def c_glu(e_, bk):
            s = scrF.get()
            act(s.a, bk.a[:, 0:NT], AF.Sigmoid, [bk, prm], [s], bias=pc("bglu", e_, e_ + 1))
            tt("dve", cat.a[:, 4 + e_, :], yf.a[:, e_, :], s.a, ALU.mult, [yf, s], [cat])

        proj("w_glu", 4, 512, lambda k: ygb.a[:, k, :], [ygb], c_glu)

        def c_out(e_, bk):
            copy_any(mix.a[:, e_, :], bk.a[:, 0:NT], [bk], [mix])

        proj("w_out0", 8, D, lambda k: cat.a[:, k, :], [cat], c_out)
        postnorm(8)

    def layer1_mixer():
        phase_barrier()
        prenorm(32)

        def c_q(e_, bk):
            act(qsl.a[:, e_, :], bk.a[:, 0:NT], AF.Silu, [bk], [qsl])

        def c_f(e_, bk):
            act(sgf.a[:, e_, :], bk.a[:, 0:NT], AF.Sigmoid, [bk], [sgf])

        def c_g(e_, bk):
            act(sgl.a[:, e_, :], bk.a[:, 0:NT], AF.Silu, [bk], [sgl])

        def c_i(s, c, bk):
            copy_any(vt64.a[0:64, c, s * 512:(s + 1) * 512], bk.a[0:64, 0:512], [bk], [vt64])

        proj("w_q1", 8, 1024, lambda k: hT.a[:, k, :], [hT], c_q)
        proj("w_f1", 8, 1024, lambda k: hT.a[:, k, :], [hT], c_f)
        proj_tok("w_i1", c_i)
        proj("w_g1", 8, 1024, lambda k: hT.a[:, k, :], [hT], c_g)
        for hd in range(8):
            fg = scrF.get()
            kk = scrF.get()
            lf = scrF.get()
            b_ = scrF.get()
            eb = scrF.get()
            enb = scrF.get()
            ed = scrF.get()
            ts("dve", fg.a, sgf.a[:, hd, :], omlt.a[:, hd:hd + 1], lbt.a[:, hd:hd + 1], ALU.mult, ALU.add, [sgf, omlt, lbt], [fg])
            ts("dve", kk.a, fg.a, -1.0, 1.0, ALU.mult, ALU.add, [fg], [kk])
            act(lf.a, fg.a, AF.Ln, [fg], [lf])
            op("dve", lambda e, b_=b_, lf=lf: e.tensor_tensor_scan(out=b_.a, data0=cc("r64"), data1=lf.a, initial=0.0, op0=ALU.mult, op1=ALU.add),
               reads=[cst, lf], writes=[b_])
            act(eb.a, b_.a, AF.Exp, [b_], [eb])
            act(enb.a, b_.a, AF.Exp, [b_], [enb], scale=-1.0)
            for c in range(NC64):
                cs = slice(c * 64, (c + 1) * 64)
                act(ed.a[:, cs], b_.a[:, cs], AF.Exp, [b_], [ed], scale=-1.0, bias=b_.a[:, c * 64 + 63:c * 64 + 64])
            qi = scrB.get()
            k2 = scrB.get()
            kst = scrB.get()
            tt("dve", qi.a, qsl.a[:, hd, :], eb.a, ALU.mult, [qsl, eb], [qi])
            tt("pool", k2.a, kk.a, enb.a, ALU.mult, [kk, enb], [k2])
            tt("pool", kst.a, kk.a, ed.a, ALU.mult, [kk, ed], [kst])
            bO = accP.get()
            for c in range(NC64):
                cs = slice(c * 64, (c + 1) * 64)
                hs = slice(hd * 128, (hd + 1) * 128)
                bS = rotP.get()
                mm(bS.a[0:64, 0:64], k2.a[:, cs], qi.a[:, cs], True, True, [k2, qi], [bS])
                PT = scrS.get()
                tt("dve", PT.a[0:64, 0:64], bS.a[0:64, 0:64], cc("mask64")[0:64, :], ALU.mult, [bS, cst], [PT])
                mm(bO.a[:, cs], vt64.a[0:64, c, hs], PT.a[0:64, 0:64], True, False, [vt64, PT], [bO])
                mm(bO.a[:, cs], Sbf.a[:, hd, :], qi.a[:, cs], False, True, [Sbf, qi], [bO])
                bT = rotP.get()
                tr(bT.a.bitcast(BF16)[0:64, 0:128], kst.a[:, cs], ident_bf.a, [kst, ident_bf], [bT])
                ktok = scrS.get()
                act(ktok.a[0:64, 0:128], bT.a.bitcast(BF16)[0:64, 0:128], AF.Copy, [bT], [ktok])
                bU = rotP.get()
                mm(bU.a[:, 0:128], ktok.a[0:64, 0:128], vt64.a[0:64, c, hs], True, True, [ktok, vt64], [bU])
                stt(S.a[:, hd, :], S.a[:, hd, :], eb.a[:, c * 64 + 63:c * 64 + 64], bU.a[:, 0:128], ALU.mult, ALU.add, [S, eb, bU], [S])
                act(Sbf.a[:, hd, :], S.a[:, hd, :], AF.Copy, [S], [Sbf])
            sq = scrB.get()
            act(sq.a, bO.a[:, 0:NT], AF.Square, [bO], [sq])
            bQ = rotP.get()
            mm(bQ.a[:, 0:NT], ones_bf.a, sq.a, True, True, [ones_bf, sq], [bQ])
            sd = scrF.get()
            act(sd.a, bQ.a[:, 0:NT], AF.Sqrt, [bQ], [sd], bias=EPS, scale=1.0 / 128)
            recip(sd.a, sd.a, [sd], [sd])
            on = scrF.get()
            tt("dve", on.a, bO.a[:, 0:NT], sd.a, ALU.mult, [bO, sd], [on])
            stt(cat.a[:, hd, :], on.a, pc("ggain", hd, hd + 1), sgl.a[:, hd, :], ALU.mult, ALU.mult, [on, prm, sgl], [cat])

        def c_out(e_, bk):
            copy_any(mix.a[:, e_, :], bk.a[:, 0:NT], [bk], [mix])

        proj("w_out1", 8, D, lambda k: cat.a[:, k, :], [cat], c_out)
        postnorm(40)

    stores = []
    for ti in range(n_tiles):
        src = x_d[ti * NT:(ti + 1) * NT, :].rearrange("(n p) d -> p n d", p=128)
        P.dma("pool", lambda e, src=src: e.dma_start(out=xio.a, in_=src), writes=[xio])
        for c in range(8):
            bk = rotP.get()
            for nb in range(NB):
                tr(bk.a[:, nb * 128:(nb + 1) * 128], xio.a[:, nb, c * 128:(c + 1) * 128], ident_f, [xio, cst], [bk])
            copy_any(xT.a[:, c, :], bk.a[:, 0:NT], [bk], [xT])
        if "l0" in STAGES:
            layer0_mixer()
        if "f0" in STAGES:
            ffn(0)
        if "l1" in STAGES:
            layer1_mixer()
        if "f1" in STAGES:
            ffn(1)
        for nb in range(NB):
            for c4 in range(2):
                bk = rotP.get()
                for c in range(4):
                    tr(bk.a[:, c * 128:(c + 1) * 128], xT.a[:, c4 * 4 + c, nb * 128:(nb + 1) * 128], ident_f, [xT, cst], [bk])
                copy_any(xio.a[:, nb, c4 * 512:(c4 + 1) * 512], bk.a[:, 0:512], [bk], [xio])
        dst = y_d[ti * NT:(ti + 1) * NT, :].rearrange("(n p) d -> p n d", p=128)
        stores.append(P.dma("pool", lambda e, dst=dst: e.dma_start(out=dst, in_=xio.a), reads=[xio]))

    stores.append(P.dma("sp", lambda e: e.dma_start(out=st_out[:, 0:516], in_=Cn.a.rearrange("p h v -> p (h v)")), reads=[Cn]))
    stores.append(P.dma("sp", lambda e: e.dma_start(out=st_out[:, 516:548], in_=rx.a.rearrange("p a m -> p (a m)")), reads=[rx]))
    stores.append(P.dma("sp", lambda e: e.dma_start(out=st_out[:, 548:1572], in_=S.a.rearrange("p h v -> p (h v)")), reads=[S]))
    stores.append(P.dma("sp", lambda e: e.dma_start(out=st_out[:, 1572:1584], in_=halo.a.rearrange("p h k -> p (h k)")), reads=[halo]))
    P.emit(final_wait_ops=stores[-12:])
    P.close()
    return nc


N_LAUNCH = 1
_CACHE = {}


def _get_prog(n_tiles):
    if n_tiles not in _CACHE:
        _CACHE[n_tiles] = build_program(n_tiles)
    return _CACHE[n_tiles]


def run_sequence_chunks(inputs, x_seqs, n_launch=1):
    L = x_seqs[0].shape[0]
    per = L // n_launch
    n_tiles = per // NT
    prm = pack_params(inputs)
    cst = make_consts()
    shared = {
        "prm": prm, "cst": cst,
        "ab_w_in": np.ascontiguousarray(inputs["ab_w_in"][0], np.float32),
        "ab_w_glu": np.ascontiguousarray(inputs["ab_w_glu"][0], np.float32),
        "ab_w_out": np.ascontiguousarray(inputs["ab_w_out"][0], np.float32),
        "c_w_in": np.ascontiguousarray(inputs["c_w_in"][0], np.float32),
        "c_w_out": np.ascontiguousarray(inputs["c_w_out"][0], np.float32),
        "ffn_w1": np.ascontiguousarray(inputs["ffn_w1"], np.float32),
        "ffn_w3": np.ascontiguousarray(inputs["ffn_w3"], np.float32),
        "ffn_w2": np.ascontiguousarray(inputs["ffn_w2"], np.float32),
    }
    ncore = len(x_seqs)
    states = [np.zeros((128, NSTATE), np.float32) for _ in range(ncore)]
    outs = [[] for _ in range(ncore)]
    for li in range(n_launch):
        nc = _get_prog(n_tiles)
        in_maps = []
        for c in range(ncore):
            xs = x_seqs[c % len(x_seqs)]
            m = dict(shared)
            m["x"] = np.ascontiguousarray(xs[li * per:(li + 1) * per], np.float32)
            m["st_in"] = states[c]
            in_maps.append(m)
        res = run_bass_kernel_spmd(nc, in_maps, core_ids=list(range(ncore)))
        for c in range(ncore):
            outs[c].append(np.asarray(res.results[c]["y"], np.float32))
            states[c] = np.asarray(res.results[c]["st_out"], np.float32)
    return [np.concatenate(outs[c], axis=0) for c in range(len(x_seqs))]


def kernel(**inputs):
    x = np.asarray(inputs["x"], np.float32)
    seqs = [x[b] for b in range(x.shape[0])]
    ys = run_sequence_chunks(inputs, seqs, n_launch=N_LAUNCH)
    return np.stack(ys, axis=0).astype(np.float32)
```

```python
import contextlib
import numpy as np
import concourse.bass as bass
import concourse.mybir as mybir
from concourse.bass_utils import run_bass_kernel_spmd

F32 = mybir.dt.float32
BF16 = mybir.dt.bfloat16
I32 = mybir.dt.int32
AF = mybir.ActivationFunctionType
ALU = mybir.AluOpType
PI = float(np.pi)
EPS = 1e-6

D = 1024
DFF = 2816
NT = 256
NB = NT // 128
NC64 = NT // 64
SEQ = 16384
BATCH = 2

COMPUTE = ("pe", "act", "dve", "pool")
DEBUG_TAGS = False
STAGES = {"l0", "f0", "l1", "f1"}


class Buf:
    __slots__ = ("last_w", "readers", "serial")

    def __init__(self, serial=False):
        self.last_w = None
        self.readers = {}
        self.serial = serial


class Op:
    __slots__ = ("eng", "fn", "deps", "sig", "is_dma", "sem", "val", "prev_same_sem", "tag")

    def __init__(self, eng, fn, is_dma):
        self.tag = None
        if DEBUG_TAGS:
            import sys as _s
            f = _s._getframe(2)
            ls = []
            while f is not None and len(ls) < 5:
                ls.append(str(f.f_lineno))
                f = f.f_back
            self.tag = ">".join(ls)
        self.eng = eng
        self.fn = fn
        self.is_dma = is_dma
        self.deps = []
        self.sig = False
        self.sem = None
        self.val = 0
        self.prev_same_sem = None


class TL:
    __slots__ = ("a", "b", "cb")

    def __init__(self, a, b, cb=None):
        self.a = a
        self.b = b
        self.cb = cb

    def chunked(self, n):
        self.cb = [Buf() for _ in range(n)]
        return self

    def c(self, i):
        return TL(self.a, self.cb[i])


def _bufs(ts_):
    out = []
    for t in ts_:
        if isinstance(t, TL):
            if t.cb is not None:
                out.extend(t.cb)
            else:
                out.append(t.b)
        else:
            out.append(t)
    return out


class Rot:
    def __init__(self, tiles):
        self.tiles = tiles
        self.i = 0

    def get(self):
        t = self.tiles[self.i % len(self.tiles)]
        self.i += 1
        return t


class Prog:
    def __init__(self, nc, n_dma_sems=8):
        self.nc = nc
        self.ops = {e: [] for e in ("sp", "act", "dve", "pool", "pe")}
        self.n_dma_sems = n_dma_sems
        self.dma_count = {"sp": 0, "act": 0, "pool": 0}
        self.dma_last = {}
        self.stack = contextlib.ExitStack()
        self.nalloc = 0

    def tile(self, shape, dtype, name=None):
        self.nalloc += 1
        t = self.stack.enter_context(self.nc.sbuf_tensor("sb_" + (name or f"t{self.nalloc}"), list(shape), dtype))
        return TL(t[:], Buf())

    def psum_bank(self):
        self.nalloc += 1
        t = self.stack.enter_context(self.nc.psum_tensor(f"ps{self.nalloc}", [128, 512], F32))
        return TL(t[:], Buf(serial=True))

    def _record(self, op, reads, writes):
        raw = {}
        other = {}
        for b in reads:
            if b.last_w is not None:
                raw[id(b.last_w)] = b.last_w
            if b.serial:
                for r in b.readers.values():
                    other[id(r)] = r
        for b in writes:
            if b.last_w is not None:
                other[id(b.last_w)] = b.last_w
            for r in b.readers.values():
                other[id(r)] = r
        for d in raw.values():
            if d is op:
                continue
            if (not d.is_dma) and (not op.is_dma) and d.eng == op.eng and op.eng == "pe":
                continue
            op.deps.append(d)
            d.sig = True
        for k, d in other.items():
            if d is op or k in raw:
                continue
            if (not d.is_dma) and (not op.is_dma) and d.eng == op.eng and op.eng == "pe":
                continue
            op.deps.append(d)
            d.sig = True
        for b in writes:
            b.last_w = op
            b.readers = {}
        for b in reads:
            if b.last_w is op:
                continue
            key = ("dma", id(op)) if op.is_dma else op.eng
            b.readers[key] = op

    def op(self, eng, fn, reads=(), writes=()):
        o = Op(eng, fn, False)
        self._record(o, _bufs(reads), _bufs(writes))
        self.ops[eng].append(o)
        return o

    def dma(self, queue, fn, reads=(), writes=()):
        o = Op(queue, fn, True)
        k = (queue, self.dma_count[queue] % self.n_dma_sems)
        self.dma_count[queue] += 1
        o.sem = k
        o.prev_same_sem = self.dma_last.get(k)
        o.val = (o.prev_same_sem.val if o.prev_same_sem else 0) + 16
        self.dma_last[k] = o
        o.sig = True
        self._record(o, _bufs(reads), _bufs(writes))
        if o.prev_same_sem is not None:
            o.deps.append(o.prev_same_sem)
        self.ops[queue].append(o)
        return o

    def emit(self, final_wait_ops=()):
        nc = self.nc
        st = self.stack
        dma_sems = {(q, i): st.enter_context(nc.semaphore(f"s_dma_{q}{i}")) for q in ("sp", "pool") for i in range(self.n_dma_sems)}
        eng_sems = {}
        SEM_LIMIT = 20000
        self.sig_counts = {}
        for e in COMPUTE:
            cnt = 0
            epoch = 0
            total = 0
            for o in self.ops[e]:
                if o.is_dma:
                    continue
                if o.sig:
                    if cnt >= SEM_LIMIT:
                        epoch += 1
                        cnt = 0
                    cnt += 1
                    total += 1
                    key = (e, epoch)
                    if key not in eng_sems:
                        eng_sems[key] = st.enter_context(nc.semaphore(f"s_{e}{epoch}"))
                    o.sem = key
                    o.val = cnt
            self.sig_counts[e] = total
        block = st.enter_context(nc.Block())

        def run(ename, engobj):
            waited = {}
            for o in self.ops[ename]:
                for d in o.deps:
                    key = ("dma", d.sem) if d.is_dma else d.sem
                    if waited.get(key, 0) >= d.val:
                        continue
                    waited[key] = d.val
                    s = dma_sems[d.sem] if d.is_dma else eng_sems[d.sem]
                    engobj.wait_ge(s, d.val)
                ins = o.fn(engobj)
                if o.tag is not None:
                    ins.annotate(o.tag)
                if o.sig:
                    if o.is_dma:
                        ins.then_inc(dma_sems[o.sem], 16)
                    else:
                        ins.then_inc(eng_sems[o.sem], 1)
            if ename == "sp":
                for d in final_wait_ops:
                    s = dma_sems[d.sem] if d.is_dma else eng_sems[d.sem]
                    engobj.wait_ge(s, d.val)

        @block.sync
        def _(e):
            run("sp", e)

        @block.scalar
        def _(e):
            run("act", e)

        @block.vector
        def _(e):
            run("dve", e)

        @block.gpsimd
        def _(e):
            run("pool", e)

        @block.tensor
        def _(e):
            run("pe", e)

    def close(self):
        self.stack.close()


class Cols:
    def __init__(self):
        self.off = {}
        self.n = 0

    def add(self, name, w):
        self.off[name] = (self.n, w)
        self.n += w


PRM = Cols()
for _n, _w in [("ng", 64), ("convw", 16), ("convb", 4), ("wq", 16), ("wk", 16), ("wv", 16), ("wif", 96), ("bif", 2),
               ("mhg", 4), ("skip", 4), ("are", 16), ("aim", 16), ("ldt", 16), ("bre", 256), ("bim", 256),
               ("cre", 256), ("cim", 256), ("dsk", 4), ("bglu", 4), ("lbraw", 16), ("ggain", 8)]:
    PRM.add(_n, _w)

CST = Cols()
for _n, _w in [("ident", 128), ("ones", 128), ("onesdiv", 128), ("negmask", 128), ("mask64", 64), ("selneg", 512),
               ("r128", NT), ("r64", NT), ("tvec", 128), ("bdmask", 128), ("maskB", 32), ("maskC", 8)]:
    CST.add(_n, _w)

NSTATE = 4 * 129 + 32 + 8 * 128 + 12


def fm(v, nch):
    return np.ascontiguousarray(np.asarray(v, np.float32).reshape(nch, 128).T)


def pack_params(inp):
    prm = np.zeros((128, PRM.n), np.float32)

    def put(name, arr):
        o, w = PRM.off[name]
        arr = np.asarray(arr, np.float32)
        assert arr.shape[1] == w, (name, arr.shape, w)
        prm[:arr.shape[0], o:o + w] = arr

    ng = np.asarray(inp["norm_g"], np.float32)
    put("ng", np.concatenate([fm(ng[l, j], 8) for l in range(2) for j in range(4)], axis=1))
    cw = np.asarray(inp["ab_conv_w"], np.float32)[0]
    put("convw", cw.reshape(4, 4, 128).transpose(2, 1, 0).reshape(128, 16))
    put("convb", fm(inp["ab_conv_b"][0], 4))
    for nm, key in (("wq", "ab_wq"), ("wk", "ab_wk"), ("wv", "ab_wv")):
        w = np.asarray(inp[key], np.float32)[0]
        w = w.reshape(4, 32, 4, 4)
        put(nm, w.transpose(1, 2, 0, 3).reshape(128, 16))
    wif = np.asarray(inp["ab_w_if"], np.float32)[0]
    put("wif", wif.reshape(12, 128, 8).transpose(1, 0, 2).reshape(128, 96))
    bif = np.asarray(inp["ab_b_if"], np.float32)[0]
    put("bif", np.stack([bif[0:4], bif[4:8]], axis=1))
    put("mhg", fm(inp["ab_mh_gain"][0], 4))
    put("skip", fm(inp["ab_skip"][0], 4))

    def sm(a):
        return np.asarray(a, np.float32).reshape(16, 2, 64).transpose(1, 2, 0).reshape(128, 16)

    put("are", sm(inp["ab_a_re"][0]))
    put("aim", sm(inp["ab_a_im"][0]))
    put("ldt", sm(np.repeat(np.asarray(inp["ab_log_dt"], np.float32)[0][:, None], 64, axis=1)))
    for nm, key in (("bre", "ab_b_re"), ("bim", "ab_b_im")):
        b = np.asarray(inp[key], np.float32)[0]
        put(nm, b.reshape(16, 2, 64, 16).transpose(1, 2, 0, 3).reshape(128, 256))
    for nm, key in (("cre", "ab_c_re"), ("cim", "ab_c_im")):
        c = np.asarray(inp[key], np.float32)[0]
        put(nm, c.reshape(4, 128, 64).transpose(1, 0, 2).reshape(128, 256))
    put("dsk", fm(inp["ab_d"][0], 4))
    put("bglu", fm(inp["ab_b_glu"][0], 4))
    lb = np.asarray(inp["c_lb_raw"], np.float32)
    put("lbraw", np.concatenate([fm(lb[0], 8), fm(lb[1], 8)], axis=1))
    put("ggain", fm(inp["c_g_gain"][0], 8))
    return prm


def make_consts():
    c = np.zeros((128, CST.n), np.float32)

    def put(name, arr):
        o, w = CST.off[name]
        arr = np.asarray(arr, np.float32)
        assert arr.shape[1] == w
        c[:arr.shape[0], o:o + w] = arr

    put("ident", np.eye(128))
    put("ones", np.ones((128, 128)))
    put("onesdiv", np.full((128, 128), 1.0 / 128))
    s = np.arange(128)[:, None]
    t = np.arange(128)[None, :]
    put("negmask", np.where(s <= t, 0.0, -30000.0))
    put("mask64", (s[:64] <= t[:, :64]).astype(np.float32))
    sel = np.zeros((4, 4, 128), np.float32)
    for r in range(4):
        sel[r, r, :] = -1.0
    put("selneg", sel.reshape(4, 512))
    tt = np.arange(NT)
    put("r128", np.tile((tt % 128 != 0).astype(np.float32)[None, :], (128, 1)))
    put("r64", np.tile((tt % 64 != 0).astype(np.float32)[None, :], (128, 1)))
    put("tvec", np.tile(np.arange(1, 129, dtype=np.float32)[None, :], (128, 1)))
    p = np.arange(128)
    put("bdmask", (p[:, None] // 4 == p[None, :] // 4).astype(np.float32))
    mb = np.zeros((128, 4, 8), np.float32)
    for e in range(2):
        for v in range(4):
            mb[e * 64:(e + 1) * 64, v, 2 * v + e] = 1.0
    put("maskB", mb.reshape(128, 32))
    mc = np.zeros((128, 4, 2), np.float32)
    for gl in range(8):
        for v in range(4):
            for e in range(2):
                if gl == 2 * v + e:
                    mc[gl * 16:(gl + 1) * 16, v, e] = 1.0
    put("maskC", mc.reshape(128, 8))
    return c


def build_program(n_tiles, with_state=True):
    nc = bass.Bass("TRN2", target_bir_lowering=False)
    T = n_tiles * NT

    def din(name, shape):
        return nc.dram_tensor(name, list(shape), F32, kind="ExternalInput").ap()

    x_d = din("x", [T, D])
    prm_d = din("prm", [128, PRM.n])
    cst_d = din("cst", [128, CST.n])
    st_in = din("st_in", [128, NSTATE])
    w_in0 = din("ab_w_in", [D, 1536])
    w_glu = din("ab_w_glu", [512, 512])
    w_out0 = din("ab_w_out", [D, D])
    w_in1 = din("c_w_in", [D, 4096])
    w_out1 = din("c_w_out", [D, D])
    w1_d = din("ffn_w1", [2, D, DFF])
    w3_d = din("ffn_w3", [2, D, DFF])
    w2_d = din("ffn_w2", [2, DFF, D])
    y_d = nc.dram_tensor("y", [T, D], F32, kind="ExternalOutput").ap()
    st_out = nc.dram_tensor("st_out", [128, NSTATE], F32, kind="ExternalOutput").ap()

    P = Prog(nc)
    op = P.op
    nc_dummy_holder = [P.tile([128, 1], F32, "dummy0")]

    wsc = {}
    conv_ops = []

    def strip_w(K, ncols):
        cw = 512 if K <= 8 else 128
        if ncols % cw:
            cw = 256
        return cw

    def make_scratch(name, w_ap, K, ncols):
        cw = strip_w(K, ncols)
        ns = ncols // cw
        t = nc.dram_tensor("sc_" + name, [ns, 128, K * cw], BF16).ap()
        for s_ in range(ns):
            src = w_ap[:, s_ * cw:(s_ + 1) * cw].rearrange("(k p) n -> p k n", p=128)
            dst = t[s_].rearrange("p (k n) -> p k n", n=cw)
            conv_ops.append(P.dma("pool", lambda e, src=src, dst=dst: e.dma_start(out=dst, in_=src)))
        wsc[name] = (t, K, cw, ns)

    make_scratch("w_in0", w_in0, 8, 1536)
    make_scratch("w_glu", w_glu, 4, 512)
    make_scratch("w_out0", w_out0, 8, D)
    for l in range(2):
        make_scratch(f"w1_{l}", w1_d[l], 8, DFF)
        make_scratch(f"w3_{l}", w3_d[l], 8, DFF)
        make_scratch(f"w2_{l}", w2_d[l], 22, D)
    make_scratch("w_q1", w_in1[:, 0:1024], 8, 1024)
    make_scratch("w_f1", w_in1[:, 1024:2048], 8, 1024)
    make_scratch("w_i1", w_in1[:, 2048:3072], 8, 1024)
    make_scratch("w_g1", w_in1[:, 3072:4096], 8, 1024)
    make_scratch("w_out1", w_out1, 8, D)
    wsc_buf = Buf()
    P.op("pool", lambda e: e.memset(nc_dummy_holder[0].a, 0.0), reads=[], writes=[wsc_buf])
    P.ops["pool"][-1].deps.extend(conv_ops)

    prm = P.tile([128, PRM.n], F32, "prm")
    cst = P.tile([128, CST.n], F32, "cst")

    def pc(name, a=None, b=None):
        o, w = PRM.off[name]
        if a is None:
            return prm.a[:, o:o + w]
        return prm.a[:, o + a:o + b]

    def cc(name, a=None, b=None):
        o, w = CST.off[name]
        if a is None:
            return cst.a[:, o:o + w]
        return cst.a[:, o + a:o + b]

    ident_f = cc("ident")
    ident_bf = P.tile([128, 128], BF16, "ident_bf")
    ones_bf = P.tile([128, 128], BF16, "ones_bf")
    BD = P.tile([128, 3, 4, 128], BF16, "BD")
    wif_bf = P.tile([128, 96], BF16, "wif_bf")
    nbf = P.tile([4, 1], F32, "nbf")
    BBT = P.tile([128, 16, 2, 128], BF16, "BBT")
    CT = P.tile([128, 16, 2, 128], BF16, "CT")
    Ec = P.tile([128, 2048], F32, "Ec")
    Es = P.tile([128, 2048], F32, "Es")
    Rfull = P.tile([128, 2048], F32, "Rfull")
    rcol = P.tile([128, 16], F32, "rcol")
    lbt = P.tile([128, 8], F32, "lbt")
    omlt = P.tile([128, 8], F32, "omlt")
    Cn = P.tile([128, 4, 129], F32, "Cn")
    Cbf = P.tile([128, 4, 128], BF16, "Cbf")
    Nrep = P.tile([128, 4, 128], BF16, "Nrep")
    rx = P.tile([128, 2, 16], F32, "rx")
    S = P.tile([128, 8, 128], F32, "S")
    Sbf = P.tile([128, 8, 128], BF16, "Sbf")
    xio = P.tile([128, NB, D], F32, "xio")
    xT = P.tile([128, 8, NT], F32, "xT")
    hT = P.tile([128, 8, NT], BF16, "hT")
    mix = P.tile([128, 8, NT], F32, "mix")
    cat = P.tile([128, 8, NT], BF16, "cat")
    wbufs = Rot([P.tile([128, 4096], BF16, f"wb{i}") for i in range(3)])
    scrF = Rot([P.tile([128, NT], F32, f"sF{i}") for i in range(9)])
    scrB = Rot([P.tile([128, NT], BF16, f"sB{i}") for i in range(4)])
    scrS = Rot([P.tile([128, 129], BF16, f"sS{i}") for i in range(8)])
    rotP = Rot([P.psum_bank() for _ in range(8)])

    ARENA_BYTES = 72608
    arena = P.tile([128, ARENA_BYTES // 4], F32, "arena")
    phase_buf = Buf()

    class Carver:
        def __init__(self):
            self.off = 0

        def carve(self, shape, dtype):
            n = int(np.prod(shape[1:]))
            isz = 4 if dtype in (F32, I32) else 2
            nbytes = (n * isz + 31) // 32 * 32
            assert self.off + nbytes <= ARENA_BYTES, (self.off, nbytes)
            a = arena.a[:, self.off // 4:(self.off + nbytes) // 4]
            if dtype != F32:
                a = a.bitcast(dtype)
            a = a[:, 0:n]
            if len(shape) == 3:
                a = a.rearrange("p (a b) -> p a b", b=shape[2])
            elif len(shape) == 4:
                a = a.rearrange("p (a b c) -> p a b c", b=shape[2], c=shape[3])
            self.off += nbytes
            return TL(a, Buf())

    cv = Carver()
    xme = cv.carve([128, 4, NT + 8], F32)
    xc = cv.carve([128, 4, NT], F32)
    szm = cv.carve([128, 4, NT], F32)
    usf = cv.carve([128, 4, NT], F32)
    yf = cv.carve([128, 4, NT], F32)
    usb = cv.carve([128, 4, NT], BF16)
    xcb = cv.carve([128, 4, NT], BF16)
    xmb = cv.carve([128, 4, NT], BF16)
    qT = cv.carve([128, 4, NT], BF16)
    kT = cv.carve([128, 4, NT], BF16)
    vT = cv.carve([128, 4, NT], BF16)
    ygb = cv.carve([128, 4, NT], BF16)
    vtok = cv.carve([128, NB, 4, 129], BF16)
    Ig = cv.carve([128, NT], F32)
    Lf = cv.carve([128, NT], F32)
    Bn = cv.carve([128, NT], F32)
    Cm = cv.carve([128, NT], F32)
    Am = cv.carve([128, NT], F32)
    csT = cv.carve([128, NB, 4], F32)
    eaT = cv.carve([128, NB, 4], F32)
    ExpBt = [cv.carve([128, NT], F32) for _ in range(4)]
    hbt = cv.carve([128, 4, NT], F32)
    scrW = Rot([cv.carve([128, 512], F32) for _ in range(8)])
    scrWB = Rot([cv.carve([128, 512], BF16) for _ in range(4)])
    l0_bytes = cv.off
    cv = Carver()
    qsl = cv.carve([128, 8, NT], F32)
    sgf = cv.carve([128, 8, NT], F32)
    vt64 = cv.carve([128, NC64, D], BF16)
    sgl = cv.carve([128, 8, NT], BF16)
    ob = cv.carve([128, 8, NT], F32)
    ebt = [cv.carve([128, NT], F32) for _ in range(8)]
    qit = [cv.carve([128, NT], BF16) for _ in range(8)]
    k2t = [cv.carve([128, NT], BF16) for _ in range(8)]
    kstt = [cv.carve([128, NT], BF16) for _ in range(8)]
    cv = Carver()
    hid = cv.carve([128, 22, NT], BF16)
    halo = P.tile([128, 4, 3], F32, "halo")
    dummy = P.tile([128, 1], F32, "dummy")

    for t_, n_ in ((xT, 8), (hT, 8), (mix, 8), (cat, 8), (hid, 22), (xme, 4), (xc, 4), (szm, 4), (usf, 4), (yf, 4), (usb, 4),
                   (xcb, 4), (xmb, 4), (qT, 4), (kT, 4), (vT, 4), (ygb, 4), (hbt, 4), (Cn, 4), (Cbf, 4), (Nrep, 4),
                   (qsl, 8), (sgf, 8), (sgl, 8), (ob, 8), (S, 8), (Sbf, 8)):
        t_.chunked(n_)

    def AR(t):
        return t

    def phase_barrier():
        op("pool", lambda e: e.memset(dummy.a, 0.0), reads=[], writes=[dummy, phase_buf])

    arena_ids = set()

    def mark(*tls):
        for t in tls:
            arena_ids.add(id(t.b))
            if t.cb is not None:
                for b_ in t.cb:
                    arena_ids.add(id(b_))

    def aop(eng, fn, reads=(), writes=()):
        rs = list(reads)
        if any(id(t.b) in arena_ids for t in list(reads) + list(writes) if isinstance(t, TL)):
            rs.append(phase_buf)
        return P.op(eng, fn, reads=rs, writes=writes)

    op = aop
    mark(ob, *ebt, *qit, *k2t, *kstt, hbt, *ExpBt, xme, xc, szm, usf, yf, usb, xcb, xmb, qT, kT, vT, ygb, vtok, Ig, Lf, Bn, Cm, Am, csT, eaT,
         qsl, sgf, vt64, sgl, hid, *scrW.tiles, *scrWB.tiles)

    cnt = {"evac": 0}

    def copy_any(out_ap, in_ap, reads, writes, engs=("act", "dve")):
        e = engs[cnt["evac"] % len(engs)]
        cnt["evac"] += 1
        if e == "act":
            op("act", lambda en: en.activation(out=out_ap, in_=in_ap, func=AF.Copy), reads=reads, writes=writes)
        elif e == "dve":
            op("dve", lambda en: en.tensor_copy(out=out_ap, in_=in_ap), reads=reads, writes=writes)
        else:
            op("pool", lambda en: en.tensor_copy(out=out_ap, in_=in_ap), reads=reads, writes=writes)

    def act(out_ap, in_ap, func, reads, writes, bias=None, scale=None):
        kw = {}
        if bias is not None:
            kw["bias"] = bias
        if scale is not None:
            kw["scale"] = scale
        op("act", lambda en: en.activation(out=out_ap, in_=in_ap, func=func, **kw), reads=reads, writes=writes)

    def tt(eng, out_ap, a_ap, b_ap, alu, reads, writes):
        op(eng, lambda en: en.tensor_tensor(out=out_ap, in0=a_ap, in1=b_ap, op=alu), reads=reads, writes=writes)

    def ts(eng, out_ap, a_ap, s1, s2, o0, o1, reads, writes):
        if o1 is None:
            op(eng, lambda en: en.tensor_scalar(out=out_ap, in0=a_ap, scalar1=s1, scalar2=None, op0=o0), reads=reads, writes=writes)
        else:
            op(eng, lambda en: en.tensor_scalar(out=out_ap, in0=a_ap, scalar1=s1, scalar2=s2, op0=o0, op1=o1), reads=reads, writes=writes)

    def stt(out_ap, a_ap, sc, b_ap, o0, o1, reads, writes):
        op("dve", lambda en: en.scalar_tensor_tensor(out=out_ap, in0=a_ap, scalar=sc, in1=b_ap, op0=o0, op1=o1), reads=reads, writes=writes)

    def mm(out_ap, lhsT, rhs, start, stop, reads, writes):
        op("pe", lambda en: en.matmul(out_ap, lhsT=lhsT, rhs=rhs, start=start, stop=stop), reads=reads, writes=writes)

    def tr(out_ap, in_ap, ident_ap, reads, writes):
        op("pe", lambda en: en.transpose(out_ap, in_ap, ident_ap), reads=reads, writes=writes)

    def recip(out_ap, in_ap, reads, writes):
        op("dve", lambda en: en.reciprocal(out=out_ap, in_=in_ap), reads=reads, writes=writes)

    P.dma("sp", lambda e: e.dma_start(out=prm.a, in_=prm_d), writes=[prm])
    P.dma("sp", lambda e: e.dma_start(out=cst.a, in_=cst_d), writes=[cst])
    op("act", lambda e: e.activation(out=ident_bf.a, in_=cc("ident"), func=AF.Copy), reads=[cst], writes=[ident_bf])
    op("act", lambda e: e.activation(out=ones_bf.a, in_=cc("ones"), func=AF.Copy), reads=[cst], writes=[ones_bf])
    op("act", lambda e: e.activation(out=wif_bf.a, in_=pc("wif"), func=AF.Copy), reads=[prm], writes=[wif_bf])
    ts("dve", nbf.a, pc("bif", 1, 2)[0:4, :], -1.0, None, ALU.mult, None, [prm], [nbf])
    for wi, nm in enumerate(("wq", "wk", "wv")):
        scale = float(128 ** -0.5) if nm == "wk" else 1.0
        for h in range(4):
            stt(BD.a[:, wi, h, :].rearrange("p (n e) -> p n e", e=4), cc("bdmask").rearrange("p (n e) -> p n e", e=4), scale,
                pc(nm, h * 4, h * 4 + 4).unsqueeze(1).broadcast_to([128, 32, 4]), ALU.mult, ALU.mult, [cst, prm], [BD])

    cv = Carver()
    sp = {n: cv.carve([128, 16], F32) for n in
          ["dt", "ard", "mag", "th", "cos", "sin", "abr", "abi", "den", "inv", "abr1", "t1", "t2", "gre", "gim"]}
    bbr = cv.carve([128, 256], F32)
    bbi = cv.carve([128, 256], F32)
    tmpA = cv.carve([128, 256], F32)
    ki = cv.carve([128, 128], I32)
    kf = cv.carve([128, 128], F32)
    ph = cv.carve([128, 128], F32)
    phi = cv.carve([128, 128], F32)
    Zs = Rot([cv.carve([128, 128], F32) for i in range(2)])
    mark(bbr, bbi, tmpA, ki, kf, ph, phi, *Zs.tiles, *sp.values())

    def sin_of(dst_ap, dst_tl, src_ap, src_tl, add, w):
        p_ = ph.a[:, 0:w]
        k_ = ki.a[:, 0:w]
        f_ = kf.a[:, 0:w]
        ts("dve", p_, src_ap, float(add), None, ALU.add, None, [src_tl], [ph])
        ts("dve", k_, p_, 1.0 / (2 * PI), None, ALU.mult, None, [ph], [ki])
        op("dve", lambda e: e.tensor_copy(out=f_, in_=k_), reads=[ki], writes=[kf])
        stt(p_, f_, -2 * PI, p_, ALU.mult, ALU.add, [kf, ph], [ph])
        ts("dve", f_, p_, PI, -2 * PI, ALU.is_gt, ALU.mult, [ph], [kf])
        tt("dve", p_, p_, f_, ALU.add, [ph, kf], [ph])
        ts("dve", f_, p_, -PI, 2 * PI, ALU.is_lt, ALU.mult, [ph], [kf])
        tt("dve", p_, p_, f_, ALU.add, [ph, kf], [ph])
        act(dst_ap, p_, AF.Sin, [ph], [dst_tl])

    act(sp["dt"].a, pc("ldt"), AF.Exp, [prm], [sp["dt"]])
    tt("dve", sp["ard"].a, pc("are"), sp["dt"].a, ALU.mult, [prm, sp["dt"]], [sp["ard"]])
    act(sp["mag"].a, sp["ard"].a, AF.Exp, [sp["ard"]], [sp["mag"]])
    tt("dve", sp["th"].a, pc("aim"), sp["dt"].a, ALU.mult, [prm, sp["dt"]], [sp["th"]])
    sin_of(sp["cos"].a, sp["cos"], sp["th"].a, sp["th"], PI / 2, 16)
    sin_of(sp["sin"].a, sp["sin"], sp["th"].a, sp["th"], 0.0, 16)
    tt("dve", sp["abr"].a, sp["mag"].a, sp["cos"].a, ALU.mult, [sp["mag"], sp["cos"]], [sp["abr"]])
    tt("dve", sp["abi"].a, sp["mag"].a, sp["sin"].a, ALU.mult, [sp["mag"], sp["sin"]], [sp["abi"]])
    tt("dve", sp["den"].a, pc("are"), pc("are"), ALU.mult, [prm], [sp["den"]])
    tt("dve", sp["t1"].a, pc("aim"), pc("aim"), ALU.mult, [prm], [sp["t1"]])
    tt("dve", sp["den"].a, sp["den"].a, sp["t1"].a, ALU.add, [sp["den"], sp["t1"]], [sp["den"]])
    recip(sp["inv"].a, sp["den"].a, [sp["den"]], [sp["inv"]])
    ts("dve", sp["abr1"].a, sp["abr"].a, -1.0, None, ALU.add, None, [sp["abr"]], [sp["abr1"]])
    tt("dve", sp["t1"].a, sp["abr1"].a, pc("are"), ALU.mult, [sp["abr1"], prm], [sp["t1"]])
    tt("dve", sp["t2"].a, sp["abi"].a, pc("aim"), ALU.mult, [sp["abi"], prm], [sp["t2"]])
    tt("dve", sp["t1"].a, sp["t1"].a, sp["t2"].a, ALU.add, [sp["t1"], sp["t2"]], [sp["t1"]])
    tt("dve", sp["gre"].a, sp["t1"].a, sp["inv"].a, ALU.mult, [sp["t1"], sp["inv"]], [sp["gre"]])
    tt("dve", sp["t1"].a, sp["abi"].a, pc("are"), ALU.mult, [sp["abi"], prm], [sp["t1"]])
    tt("dve", sp["t2"].a, sp["abr1"].a, pc("aim"), ALU.mult, [sp["abr1"], prm], [sp["t2"]])
    tt("dve", sp["t1"].a, sp["t1"].a, sp["t2"].a, ALU.subtract, [sp["t1"], sp["t2"]], [sp["t1"]])
    tt("dve", sp["gim"].a, sp["t1"].a, sp["inv"].a, ALU.mult, [sp["t1"], sp["inv"]], [sp["gim"]])

    def v3(ap2d):
        return ap2d.rearrange("p (m q) -> p m q", q=16)

    def bc_m(tl):
        return tl.a.unsqueeze(2).broadcast_to([128, 16, 16])

    tt("dve", v3(bbr.a), v3(pc("bre")), bc_m(sp["gre"]), ALU.mult, [prm, sp["gre"]], [bbr])
    tt("dve", v3(tmpA.a), v3(pc("bim")), bc_m(sp["gim"]), ALU.mult, [prm, sp["gim"]], [tmpA])
    tt("dve", bbr.a, bbr.a, tmpA.a, ALU.subtract, [bbr, tmpA], [bbr])
    tt("dve", v3(bbi.a), v3(pc("bim")), bc_m(sp["gre"]), ALU.mult, [prm, sp["gre"]], [bbi])
    tt("dve", v3(tmpA.a), v3(pc("bre")), bc_m(sp["gim"]), ALU.mult, [prm, sp["gim"]], [tmpA])
    tt("dve", bbi.a, bbi.a, tmpA.a, ALU.add, [bbi, tmpA], [bbi])
    op("dve", lambda e: e.tensor_copy(out=rcol.a, in_=sp["mag"].a), reads=[sp["mag"]], writes=[rcol])
    for m in range(16):
        ts("dve", phi.a, cc("tvec"), sp["th"].a[:, m:m + 1], None, ALU.mult, None, [cst, sp["th"]], [phi])
        sin_of(Ec.a[:, m * 128:(m + 1) * 128], Ec, phi.a, phi, PI / 2, 128)
        sin_of(Es.a[:, m * 128:(m + 1) * 128], Es, phi.a, phi, 0.0, 128)
        ts("dve", Rfull.a[:, m * 128:(m + 1) * 128], cc("ones"), sp["mag"].a[:, m:m + 1], None, ALU.mult, None, [cst, sp["mag"]], [Rfull])
    op("dve", lambda e: e.memset(Rfull.a.rearrange("p (m t) -> p m t", t=128)[:, :, 0:1], 0.0), reads=[], writes=[Rfull])
    for m in range(16):
        v = m % 4
        jj = m // 4
        for ri, bb in enumerate((bbr, bbi)):
            Z = Zs.get()
            tt("dve", Z.a.rearrange("p (g q) -> p g q", q=16), cc("maskB", v * 8, v * 8 + 8).unsqueeze(2).broadcast_to([128, 8, 16]),
               bb.a[:, m * 16:(m + 1) * 16].unsqueeze(1).broadcast_to([128, 8, 16]), ALU.mult, [cst, bb], [Z])
            bk = rotP.get()
            tr(bk.a[:, 0:128], Z.a, ident_f, [Z, cst], [bk])
            copy_any(BBT.a[:, m, ri, :], bk.a[:, 0:128], [bk], [BBT])
        for ri, cn in enumerate(("cre", "cim")):
            Z = Zs.get()
            sgn = -1.0 if ri == 1 else 1.0
            for e_ in range(2):
                ts("dve", Z.a[:, e_ * 64:(e_ + 1) * 64], pc(cn, jj * 64, jj * 64 + 64), cc("maskC", v * 2 + e_, v * 2 + e_ + 1), sgn,
                   ALU.mult, ALU.mult, [prm, cst], [Z])
            bk = rotP.get()
            tr(bk.a[:, 0:128], Z.a, ident_f, [Z, cst], [bk])
            copy_any(CT.a[:, m, ri, :], bk.a[:, 0:128], [bk], [CT])
    tt("dve", lbt.a, pc("lbraw", 8, 16), pc("lbraw", 0, 8), ALU.subtract, [prm], [lbt])
    act(lbt.a, lbt.a, AF.Sigmoid, [lbt], [lbt])
    ts("dve", omlt.a, lbt.a, -1.0, 1.0, ALU.mult, ALU.add, [lbt], [omlt])
    P.dma("sp", lambda e: e.dma_start(out=Cn.a.rearrange("p h v -> p (h v)"), in_=st_in[:, 0:516]), writes=[Cn])
    P.dma("sp", lambda e: e.dma_start(out=rx.a.rearrange("p a m -> p (a m)"), in_=st_in[:, 516:548]), writes=[rx])
    P.dma("sp", lambda e: e.dma_start(out=S.a.rearrange("p h v -> p (h v)"), in_=st_in[:, 548:1572]), writes=[S])
    P.dma("sp", lambda e: e.dma_start(out=halo.a.rearrange("p h k -> p (h k)"), in_=st_in[:, 1572:1584]), writes=[halo])
    op("act", lambda e: e.activation(out=Sbf.a, in_=S.a, func=AF.Copy), reads=[S], writes=[Sbf])
    op("act", lambda e: e.activation(out=Cbf.a, in_=Cn.a[:, :, 0:128], func=AF.Copy), reads=[Cn], writes=[Cbf])
    for h in range(4):
        act(Nrep.a[:, h, :], cc("ones"), AF.Copy, [cst, Cn], [Nrep], scale=Cn.a[:, h, 128:129])

    def rms_rstd(src_chunks, src_tls, nfeat):
        bk = rotP.get()
        n = len(src_chunks)
        for c, ap_ in enumerate(src_chunks):
            sq = scrB.get()
            act(sq.a, ap_, AF.Square, src_tls, [sq])
            mm(bk.a[:, 0:NT], ones_bf.a, sq.a, c == 0, c == n - 1, [ones_bf, sq], [bk])
        s = scrF.get()
        act(s.a, bk.a[:, 0:NT], AF.Sqrt, [bk], [s], bias=EPS, scale=1.0 / nfeat)
        recip(s.a, s.a, [s], [s])
        return s

    def prenorm(goff):
        r = rms_rstd([xT.a[:, c, :] for c in range(8)], [xT], D)
        for c in range(8):
            stt(hT.a[:, c, :], xT.a[:, c, :], pc("ng", goff + c, goff + c + 1), r.a, ALU.mult, ALU.mult, [xT.c(c), prm, r], [hT.c(c)])

    def postnorm(goff):
        r = rms_rstd([mix.a[:, c, :] for c in range(8)], [mix], D)
        for c in range(8):
            t = scrF.get()
            stt(t.a, mix.a[:, c, :], pc("ng", goff + c, goff + c + 1), r.a, ALU.mult, ALU.mult, [mix.c(c), prm, r], [t])
            tt("pool", xT.a[:, c, :], xT.a[:, c, :], t.a, ALU.add, [xT.c(c), t], [xT.c(c)])

    def proj(wname, K, ncols, rhs_fn, rhs_tls, consume):
        t, K_, cw, ns = wsc[wname]
        assert K_ == K and ns * cw == ncols
        for s in range(ns):
            wb = wbufs.get()
            wv = wb.a[:, 0:K * cw].rearrange("p (k n) -> p k n", n=cw)
            src = t[s]
            P.dma("sp", lambda e, wb=wb, src=src, n=K * cw: e.dma_start(out=wb.a[:, 0:n], in_=src), reads=[wsc_buf], writes=[wb])
            for e_ in range(cw // 128):
                bk = rotP.get()
                for k in range(K):
                    mm(bk.a[:, 0:NT], wv[:, k, e_ * 128:(e_ + 1) * 128], rhs_fn(k), k == 0, k == K - 1, [wb] + (rhs_tls(k) if callable(rhs_tls) else rhs_tls), [bk])
                consume(s * (cw // 128) + e_, bk)

    def proj_tok(wname, consume):
        t, K_, cw, ns = wsc[wname]
        for s in range(ns):
            wb = wbufs.get()
            wv = wb.a[:, 0:8 * cw].rearrange("p (k n) -> p k n", n=cw)
            src = t[s]
            P.dma("sp", lambda e, wb=wb, src=src, n=8 * cw: e.dma_start(out=wb.a[:, 0:n], in_=src), reads=[wsc_buf], writes=[wb])
            for c in range(NC64):
                bk = rotP.get()
                for k in range(8):
                    mm(bk.a[0:64, 0:512], hT.a[:, k, c * 64:(c + 1) * 64], wv[:, k, :], k == 0, k == 7, [wb, hT.c(k)], [bk])
                consume(s, c, bk)

    def ffn(l):
        phase_barrier()
        prenorm((l * 4 + 2) * 8)

        def c1(e_, bk):
            act(hid.a[:, e_, :], bk.a[:, 0:NT], AF.Silu, [bk], [hid.c(e_)])

        proj(f"w1_{l}", 8, DFF, lambda k: hT.a[:, k, :], (lambda k: [hT.c(k)]), c1)

        def c3(e_, bk):
            tt("dve", hid.a[:, e_, :], hid.a[:, e_, :], bk.a[:, 0:NT], ALU.mult, [hid.c(e_), bk], [hid.c(e_)])

        proj(f"w3_{l}", 8, DFF, lambda k: hT.a[:, k, :], (lambda k: [hT.c(k)]), c3)

        def c2(e_, bk):
            copy_any(mix.a[:, e_, :], bk.a[:, 0:NT], [bk], [mix.c(e_)])

        proj(f"w2_{l}", 22, D, lambda k: hid.a[:, k, :], (lambda k: [hid.c(k)]), c2)
        postnorm((l * 4 + 3) * 8)

    def layer0_mixer():
        phase_barrier()
        prenorm(0)
        for h in range(4):
            op("act", lambda e, h=h: e.activation(out=xme.a[:, h, 5:8], in_=halo.a[:, h, :], func=AF.Copy), reads=[halo], writes=[xme.c(h)])

        def c_in(e_, bk):
            if e_ < 4:
                copy_any(xme.a[:, e_, 8:8 + NT], bk.a[:, 0:NT], [bk], [xme.c(e_)])
            elif e_ < 8:
                act(szm.a[:, e_ - 4, :], bk.a[:, 0:NT], AF.Silu, [bk], [szm.c(e_ - 4)])
            else:
                act(usf.a[:, e_ - 8, :], bk.a[:, 0:NT], AF.Copy, [bk], [usf.c(e_ - 8)])
                op("dve", lambda en: en.tensor_copy(out=usb.a[:, e_ - 8, :], in_=bk.a[:, 0:NT]), reads=[bk], writes=[usb.c(e_ - 8)])

        if "l0a0" in STAGES:
            return
        proj("w_in0", 8, 1536, lambda k: hT.a[:, k, :], (lambda k: [hT.c(k)]), c_in)
        if "l0a1" in STAGES:
            return
        for h in range(4):
            acc = scrF.get()
            ts("dve", acc.a, xme.a[:, h, 5:5 + NT], pc("convw", h * 4, h * 4 + 1), None, ALU.mult, None, [xme.c(h), prm], [acc])
            for k in range(1, 4):
                stt(acc.a, xme.a[:, h, 5 + k:5 + k + NT], pc("convw", h * 4 + k, h * 4 + k + 1), acc.a, ALU.mult, ALU.add, [xme.c(h), prm, acc], [acc])
            act(xc.a[:, h, :], acc.a, AF.Silu, [acc, prm], [xc.c(h)], bias=pc("convb", h, h + 1))
            op("act", lambda e, h=h: e.activation(out=xcb.a[:, h, :], in_=xc.a[:, h, :], func=AF.Copy), reads=[xc.c(h)], writes=[xcb.c(h)])
            op("act", lambda e, h=h: e.activation(out=xmb.a[:, h, :], in_=xme.a[:, h, 8:8 + NT], func=AF.Copy), reads=[xme.c(h)], writes=[xmb.c(h)])
            op("act", lambda e, h=h: e.activation(out=halo.a[:, h, :], in_=xme.a[:, h, NT + 5:NT + 8], func=AF.Copy), reads=[xme.c(h)], writes=[halo])
        if "l0a" in STAGES:
            return
        for h in range(4):
            for wi, (src, dst) in enumerate(((xcb, qT), (xcb, kT), (xmb, vT))):
                bk = rotP.get()
                mm(bk.a[:, 0:NT], BD.a[:, wi, h, :], src.a[:, h, :], True, True, [BD, src.c(h)], [bk])
                copy_any(dst.a[:, h, :], bk.a[:, 0:NT], [bk], [dst.c(h)])
        op("pool", lambda e: e.memset(vtok.a[:, :, :, 128:129], 1.0), reads=[], writes=[vtok])
        for nb in range(NB):
            bk = rotP.get()
            for h in range(4):
                mm(bk.a[:, h * 128:(h + 1) * 128], xmb.a[:, h, nb * 128:(nb + 1) * 128], BD.a[:, 2, h, :], True, True, [xmb.c(h), BD], [bk])
            copy_any(vtok.a[:, nb, :, 0:128], bk.a[:, 0:512].rearrange("p (h v) -> p h v", v=128), [bk], [vtok])
        if "l0b" in STAGES:
            return
        bi = rotP.get()
        bf_ = rotP.get()
        for c in range(12):
            src = (qT, kT, vT)[c // 4]
            mm(bi.a[0:4, 0:NT], wif_bf.a[:, c * 8:c * 8 + 4], src.a[:, c % 4, :], c == 0, c == 11, [wif_bf, src.c(c % 4)], [bi])
        for c in range(12):
            src = (qT, kT, vT)[c // 4]
            mm(bf_.a[0:4, 0:NT], wif_bf.a[:, c * 8 + 4:c * 8 + 8], src.a[:, c % 4, :], c == 0, c == 11, [wif_bf, src.c(c % 4)], [bf_])
        act(Ig.a[0:4, :], bi.a[0:4, 0:NT], AF.Identity, [bi, prm], [Ig], bias=pc("bif", 0, 1)[0:4, :])
        act(Lf.a[0:4, :], bf_.a[0:4, 0:NT], AF.Exp, [bf_, nbf], [Lf], bias=nbf.a, scale=-1.0)
        act(Lf.a[0:4, :], Lf.a[0:4, :], AF.Ln, [Lf], [Lf], bias=1.0)
        op("dve", lambda e: e.tensor_tensor_scan(out=Bn.a[0:4, :], data0=cc("r128")[0:4, :], data1=Lf.a[0:4, :], initial=0.0,
                                                 op0=ALU.mult, op1=ALU.add), reads=[cst, Lf], writes=[Bn])
        tt("dve", Cm.a[0:4, :], Ig.a[0:4, :], Bn.a[0:4, :], ALU.add, [Ig, Bn], [Cm])
        for c in range(NB):
            cs = slice(c * 128, (c + 1) * 128)
            ts("dve", Am.a[0:4, cs], Cm.a[0:4, cs], Bn.a[0:4, c * 128 + 127:c * 128 + 128], None, ALU.subtract, None, [Cm, Bn], [Am])
        for nb in range(NB):
            cs = slice(nb * 128, (nb + 1) * 128)
            bk = rotP.get()
            tr(bk.a[:, 0:4], Cm.a[0:4, cs], ident_f[0:4, 0:4], [Cm, cst], [bk])
            tr(bk.a[:, 4:8], Am.a[0:4, cs], ident_f[0:4, 0:4], [Am, cst], [bk])
            op("dve", lambda e, bk=bk, nb=nb: e.tensor_copy(out=csT.a[:, nb, :], in_=bk.a[:, 0:4]), reads=[bk], writes=[csT])
            act(eaT.a[:, nb, :], bk.a[:, 4:8], AF.Exp, [bk], [eaT])
        if "l0c" in STAGES:
            return
        ExpBs, qss = [], []
        for h in range(4):
            bB = rotP.get()
            mm(bB.a[:, 0:NT], cc("selneg", h * 128, h * 128 + 128)[0:4, :], Bn.a[0:4, :], True, True, [cst, Bn], [bB])
            ExpB = ExpBt[h]
            act(ExpB.a, bB.a[:, 0:NT], AF.Exp, [bB], [ExpB])
            ExpBs.append(ExpB)
        for c in range(NB):
            cs = slice(c * 128, (c + 1) * 128)
            for h in range(4):
                ExpB = ExpBs[h]
                qsc = scrS.get()
                tt("dve", qsc.a[:, 0:128], qT.a[:, h, cs], ExpB.a[:, cs], ALU.mult, [qT.c(h), ExpB], [qsc])
                bE = rotP.get()
                mm(bE.a[:, 0:128], cc("selneg", h * 128, h * 128 + 128)[0:4, :], Bn.a[0:4, cs], True, False, [cst, Bn], [bE])
                mm(bE.a[:, 0:128], ident_f, cc("negmask"), False, True, [cst], [bE])
                E = scrF.get()
                act(E.a[:, 0:128], bE.a[:, 0:128], AF.Exp, [bE, csT], [E], bias=csT.a[:, c, h:h + 1])
                bS = rotP.get()
                mm(bS.a[:, 0:128], kT.a[:, h, cs], qT.a[:, h, cs], True, True, [kT.c(h), qT.c(h)], [bS])
                PT = scrS.get()
                tt("dve", PT.a[:, 0:128], bS.a[:, 0:128], E.a[:, 0:128], ALU.mult, [bS, E], [PT])
                bND = rotP.get()
                mm(bND.a[:, 0:128], vtok.a[:, c, h, 0:128], PT.a[:, 0:128], True, False, [vtok, PT], [bND])
                mm(bND.a[:, 0:128], Cbf.a[:, h, :], qsc.a[:, 0:128], False, True, [Cbf.c(h), qsc], [bND])
                mm(bND.a[:, 128:256], ones_bf.a, PT.a[:, 0:128], True, False, [ones_bf, PT], [bND])
                mm(bND.a[:, 128:256], Nrep.a[:, h, :], qsc.a[:, 0:128], False, True, [Nrep.c(h), qsc], [bND])
                bK = rotP.get()
                mm(bK.a[:, 0:128], xcb.a[:, h, cs], BD.a[:, 1, h, :], True, True, [xcb.c(h), BD], [bK])
                kw = scrS.get()
                act(kw.a[:, 0:128], bK.a[:, 0:128], AF.Copy, [bK, eaT], [kw], scale=eaT.a[:, c, h:h + 1])
                bU = rotP.get()
                mm(bU.a[:, 0:129], kw.a[:, 0:128], vtok.a[:, c, h, :], True, True, [kw, vtok], [bU])
                stt(Cn.a[:, h, :], Cn.a[:, h, :], ExpB.a[:, c * 128 + 127:c * 128 + 128], bU.a[:, 0:129], ALU.mult, ALU.add, [Cn.c(h), ExpB, bU], [Cn.c(h)])
                act(Cbf.a[:, h, :], Cn.a[:, h, 0:128], AF.Copy, [Cn.c(h)], [Cbf.c(h)])
                act(Nrep.a[:, h, :], cc("ones"), AF.Copy, [cst, Cn.c(h)], [Nrep.c(h)], scale=Cn.a[:, h, 128:129])
                ad = scrF.get()
                act(ad.a[:, 0:128], bND.a[:, 128:256], AF.Abs, [bND], [ad])
                ts("dve", ad.a[:, 0:128], ad.a[:, 0:128], 1.0, None, ALU.max, None, [ad], [ad])
                recip(ad.a[:, 0:128], ad.a[:, 0:128], [ad], [ad])
                tt("dve", hbt.a[:, h, cs], bND.a[:, 0:128], ad.a[:, 0:128], ALU.mult, [bND, ad], [hbt.c(h)])
        for h in range(4):
            hb_a = hbt.a[:, h, :]
            bM = rotP.get()
            mm(bM.a[:, 0:NT], cc("onesdiv"), hb_a, True, True, [cst, hbt.c(h)], [bM])
            dd = scrF.get()
            tt("dve", dd.a, hb_a, bM.a[:, 0:NT], ALU.subtract, [hbt.c(h), bM], [dd])
            sq = scrF.get()
            tt("pool", sq.a, dd.a, dd.a, ALU.mult, [dd], [sq])
            bV = rotP.get()
            mm(bV.a[:, 0:NT], cc("onesdiv"), sq.a, True, True, [cst, sq], [bV])
            sd = scrF.get()
            act(sd.a, bV.a[:, 0:NT], AF.Sqrt, [bV], [sd], bias=EPS)
            recip(sd.a, sd.a, [sd], [sd])
            stt(dd.a, dd.a, pc("mhg", h, h + 1), sd.a, ALU.mult, ALU.mult, [dd, prm, sd], [dd])
            stt(dd.a, xc.a[:, h, :], pc("skip", h, h + 1), dd.a, ALU.mult, ALU.add, [xc.c(h), prm, dd], [dd])
            tt("dve", cat.a[:, h, :], dd.a, szm.a[:, h, :], ALU.mult, [dd, szm.c(h)], [cat.c(h)])
        if "l0d" in STAGES:
            return
        for sb in range(NB):
            for jj in range(4):
                Ecq = Ec.a[:, jj * 512:(jj + 1) * 512]
                Esq = Es.a[:, jj * 512:(jj + 1) * 512]
                Rq = Rfull.a[:, jj * 512:(jj + 1) * 512]
                cs = slice(sb * 128, (sb + 1) * 128)
                bR = rotP.get()
                bI = rotP.get()
                for mi in range(4):
                    m = jj * 4 + mi
                    mm(bR.a[:, mi * 128:(mi + 1) * 128], BBT.a[:, m, 0, :], usb.a[:, jj, cs], True, True, [BBT, usb.c(jj)], [bR])
                for mi in range(4):
                    m = jj * 4 + mi
                    mm(bI.a[:, mi * 128:(mi + 1) * 128], BBT.a[:, m, 1, :], usb.a[:, jj, cs], True, True, [BBT, usb.c(jj)], [bI])
                bre = scrW.get()
                bim = scrW.get()
                act(bre.a, bR.a, AF.Copy, [bR], [bre])
                act(bim.a, bI.a, AF.Copy, [bI], [bim])
                t1 = scrW.get()
                t2 = scrW.get()
                t3 = scrW.get()
                t4 = scrW.get()
                tt("dve", t1.a, Ecq, bre.a, ALU.mult, [Ec, bre], [t1])
                tt("dve", t2.a, Esq, bim.a, ALU.mult, [Es, bim], [t2])
                tt("dve", t1.a, t1.a, t2.a, ALU.add, [t1, t2], [t1])
                tt("pool", t3.a, Ecq, bim.a, ALU.mult, [Ec, bim], [t3])
                tt("pool", t4.a, Esq, bre.a, ALU.mult, [Es, bre], [t4])
                tt("pool", t3.a, t3.a, t4.a, ALU.subtract, [t3, t4], [t3])

                def v3w(t):
                    return t.a.rearrange("p (m t) -> p m t", t=128)

                tt("dve", v3w(t1)[:, :, 0:1], v3w(t1)[:, :, 0:1], rx.a[:, 0, jj * 4:(jj + 1) * 4].unsqueeze(2), ALU.add, [t1, rx], [t1])
                tt("pool", v3w(t3)[:, :, 0:1], v3w(t3)[:, :, 0:1], rx.a[:, 1, jj * 4:(jj + 1) * 4].unsqueeze(2), ALU.add, [t3, rx], [t3])
                zr = scrW.get()
                zi = scrW.get()
                op("dve", lambda e, zr=zr, t1=t1, Rq=Rq: e.tensor_tensor_scan(out=zr.a, data0=Rq, data1=t1.a, initial=0.0, op0=ALU.mult, op1=ALU.add),
                   reads=[Rfull, t1], writes=[zr])
                op("dve", lambda e, zi=zi, t3=t3, Rq=Rq: e.tensor_tensor_scan(out=zi.a, data0=Rq, data1=t3.a, initial=0.0, op0=ALU.mult, op1=ALU.add),
                   reads=[Rfull, t3], writes=[zi])
                u1 = scrW.get()
                u2 = scrW.get()
                u3 = scrW.get()
                u4 = scrW.get()
                tt("dve", u1.a, Ecq, zr.a, ALU.mult, [Ec, zr], [u1])
                tt("pool", u2.a, Esq, zi.a, ALU.mult, [Es, zi], [u2])
                tt("dve", u1.a, u1.a, u2.a, ALU.subtract, [u1, u2], [u1])
                tt("pool", u3.a, Ecq, zi.a, ALU.mult, [Ec, zi], [u3])
                tt("dve", u4.a, Esq, zr.a, ALU.mult, [Es, zr], [u4])
                tt("pool", u3.a, u3.a, u4.a, ALU.add, [u3, u4], [u3])
                tt("dve", rx.a[:, 0, jj * 4:(jj + 1) * 4].unsqueeze(2), v3w(u1)[:, :, 127:128], rcol.a[:, jj * 4:(jj + 1) * 4].unsqueeze(2), ALU.mult, [u1, rcol], [rx])
                tt("pool", rx.a[:, 1, jj * 4:(jj + 1) * 4].unsqueeze(2), v3w(u3)[:, :, 127:128], rcol.a[:, jj * 4:(jj + 1) * 4].unsqueeze(2), ALU.mult, [u3, rcol], [rx])
                xrb = scrWB.get()
                xib = scrWB.get()
                act(xrb.a, u1.a, AF.Copy, [u1], [xrb])
                act(xib.a, u3.a, AF.Copy, [u3], [xib])
                bY = rotP.get()
                for mi in range(4):
                    m = jj * 4 + mi
                    mm(bY.a[:, 0:128], CT.a[:, m, 0, :], xrb.a[:, mi * 128:(mi + 1) * 128], mi == 0, False, [CT, xrb], [bY])
                    mm(bY.a[:, 0:128], CT.a[:, m, 1, :], xib.a[:, mi * 128:(mi + 1) * 128], False, mi == 3, [CT, xib], [bY])
                stt(yf.a[:, jj, cs], usf.a[:, jj, cs], pc("dsk", jj, jj + 1), bY.a[:, 0:128], ALU.mult, ALU.add, [usf.c(jj), prm, bY], [yf.c(jj)])
        if "l0e" in STAGES:
            return
        for jj in range(4):
            y_ = yf.a[:, jj, :]
            x2 = scrF.get()
            tt("pool", x2.a, y_, y_, ALU.mult, [yf.c(jj)], [x2])
            ts("dve", x2.a, x2.a, 0.044715, 1.0, ALU.mult, ALU.add, [x2], [x2])
            tt("dve", x2.a, x2.a, y_, ALU.mult, [x2, yf.c(jj)], [x2])
            act(x2.a, x2.a, AF.Sigmoid, [x2], [x2], scale=1.5957691216057308)
            tt("dve", y_, y_, x2.a, ALU.mult, [yf.c(jj), x2], [yf.c(jj)])
            op("act", lambda e, jj=jj: e.activation(out=ygb.a[:, jj, :], in_=yf.a[:, jj, :], func=AF.Copy), reads=[yf.c(jj)], writes=[ygb.c(jj)])

        def c_glu(e_, bk):
            s = scrF.get()
            act(s.a, bk.a[:, 0:NT], AF.Sigmoid, [bk, prm], [s], bias=pc("bglu", e_, e_ + 1))
            tt("dve", cat.a[:, 4 + e_, :], yf.a[:, e_, :], s.a, ALU.mult, [yf.c(e_), s], [cat.c(4 + e_)])

        proj("w_glu", 4, 512, lambda k: ygb.a[:, k, :], (lambda k: [ygb.c(k)]), c_glu)

        def c_out(e_, bk):
            copy_any(mix.a[:, e_, :], bk.a[:, 0:NT], [bk], [mix.c(e_)])

        proj("w_out0", 8, D, lambda k: cat.a[:, k, :], (lambda k: [cat.c(k)]), c_out)
        postnorm(8)

    def layer1_mixer():
        phase_barrier()
        prenorm(32)

        def c_q(e_, bk):
            act(qsl.a[:, e_, :], bk.a[:, 0:NT], AF.Silu, [bk], [qsl.c(e_)])

        def c_f(e_, bk):
            act(sgf.a[:, e_, :], bk.a[:, 0:NT], AF.Sigmoid, [bk], [sgf.c(e_)])

        def c_g(e_, bk):
            act(sgl.a[:, e_, :], bk.a[:, 0:NT], AF.Silu, [bk], [sgl.c(e_)])

        def c_i(s, c, bk):
            copy_any(vt64.a[0:64, c, s * 512:(s + 1) * 512], bk.a[0:64, 0:512], [bk], [vt64])

        proj("w_q1", 8, 1024, lambda k: hT.a[:, k, :], (lambda k: [hT.c(k)]), c_q)
        proj("w_f1", 8, 1024, lambda k: hT.a[:, k, :], (lambda k: [hT.c(k)]), c_f)
        proj_tok("w_i1", c_i)
        proj("w_g1", 8, 1024, lambda k: hT.a[:, k, :], (lambda k: [hT.c(k)]), c_g)
        for hd in range(8):
            fg = scrF.get()
            kk = scrF.get()
            lf = scrF.get()
            b_ = scrF.get()
            enb = scrF.get()
            ed = scrF.get()
            eb = ebt[hd]
            ts("dve", fg.a, sgf.a[:, hd, :], omlt.a[:, hd:hd + 1], lbt.a[:, hd:hd + 1], ALU.mult, ALU.add, [sgf.c(hd), omlt, lbt], [fg])
            ts("dve", kk.a, fg.a, -1.0, 1.0, ALU.mult, ALU.add, [fg], [kk])
            act(lf.a, fg.a, AF.Ln, [fg], [lf])
            op("dve", lambda e, b_=b_, lf=lf: e.tensor_tensor_scan(out=b_.a, data0=cc("r64"), data1=lf.a, initial=0.0, op0=ALU.mult, op1=ALU.add),
               reads=[cst, lf], writes=[b_])
            act(eb.a, b_.a, AF.Exp, [b_], [eb])
            act(enb.a, b_.a, AF.Exp, [b_], [enb], scale=-1.0)
            for c in range(NC64):
                cs = slice(c * 64, (c + 1) * 64)
                act(ed.a[:, cs], b_.a[:, cs], AF.Exp, [b_], [ed], scale=-1.0, bias=b_.a[:, c * 64 + 63:c * 64 + 64])
            tt("dve", qit[hd].a, qsl.a[:, hd, :], eb.a, ALU.mult, [qsl.c(hd), eb], [qit[hd]])
            tt("pool", k2t[hd].a, kk.a, enb.a, ALU.mult, [kk, enb], [k2t[hd]])
            tt("pool", kstt[hd].a, kk.a, ed.a, ALU.mult, [kk, ed], [kstt[hd]])
        for c in range(NC64):
            cs = slice(c * 64, (c + 1) * 64)
            for hd in range(8):
                qi, k2, kst, eb = qit[hd], k2t[hd], kstt[hd], ebt[hd]
                hs = slice(hd * 128, (hd + 1) * 128)
                bS = rotP.get()
                mm(bS.a[0:64, 0:64], k2.a[:, cs], qi.a[:, cs], True, True, [k2, qi], [bS])
                PT = scrS.get()
                tt("dve", PT.a[0:64, 0:64], bS.a[0:64, 0:64], cc("mask64")[0:64, :], ALU.mult, [bS, cst], [PT])
                bO = rotP.get()
                mm(bO.a[:, 0:64], vt64.a[0:64, c, hs], PT.a[0:64, 0:64], True, False, [vt64, PT], [bO])
                mm(bO.a[:, 0:64], Sbf.a[:, hd, :], qi.a[:, cs], False, True, [Sbf.c(hd), qi], [bO])
                copy_any(ob.a[:, hd, cs], bO.a[:, 0:64], [bO], [ob.c(hd)])
                bT = rotP.get()
                tr(bT.a.bitcast(BF16)[0:64, 0:128], kst.a[:, cs], ident_bf.a, [kst, ident_bf], [bT])
                ktok = scrS.get()
                act(ktok.a[0:64, 0:128], bT.a.bitcast(BF16)[0:64, 0:128], AF.Copy, [bT], [ktok])
                bU = rotP.get()
                mm(bU.a[:, 0:128], ktok.a[0:64, 0:128], vt64.a[0:64, c, hs], True, True, [ktok, vt64], [bU])
                stt(S.a[:, hd, :], S.a[:, hd, :], eb.a[:, c * 64 + 63:c * 64 + 64], bU.a[:, 0:128], ALU.mult, ALU.add, [S.c(hd), eb, bU], [S.c(hd)])
                act(Sbf.a[:, hd, :], S.a[:, hd, :], AF.Copy, [S.c(hd)], [Sbf.c(hd)])
        for hd in range(8):
            sq = scrB.get()
            act(sq.a, ob.a[:, hd, :], AF.Square, [ob.c(hd)], [sq])
            bQ = rotP.get()
            mm(bQ.a[:, 0:NT], ones_bf.a, sq.a, True, True, [ones_bf, sq], [bQ])
            sd = scrF.get()
            act(sd.a, bQ.a[:, 0:NT], AF.Sqrt, [bQ], [sd], bias=EPS, scale=1.0 / 128)
            recip(sd.a, sd.a, [sd], [sd])
            on = scrF.get()
            tt("dve", on.a, ob.a[:, hd, :], sd.a, ALU.mult, [ob.c(hd), sd], [on])
            stt(cat.a[:, hd, :], on.a, pc("ggain", hd, hd + 1), sgl.a[:, hd, :], ALU.mult, ALU.mult, [on, prm, sgl.c(hd)], [cat.c(hd)])

        def c_out(e_, bk):
            copy_any(mix.a[:, e_, :], bk.a[:, 0:NT], [bk], [mix.c(e_)])

        proj("w_out1", 8, D, lambda k: cat.a[:, k, :], (lambda k: [cat.c(k)]), c_out)
        postnorm(40)

    stores = []
    for ti in range(n_tiles):
        src = x_d[ti * NT:(ti + 1) * NT, :].rearrange("(n p) d -> p n d", p=128)
        P.dma("pool", lambda e, src=src: e.dma_start(out=xio.a, in_=src), writes=[xio])
        for c in range(8):
            bk = rotP.get()
            for nb in range(NB):
                tr(bk.a[:, nb * 128:(nb + 1) * 128], xio.a[:, nb, c * 128:(c + 1) * 128], ident_f, [xio, cst], [bk])
            copy_any(xT.a[:, c, :], bk.a[:, 0:NT], [bk], [xT.c(c)])
        if "l0" in STAGES:
            layer0_mixer()
        if "f0" in STAGES:
            ffn(0)
        if "l1" in STAGES:
            layer1_mixer()
        if "f1" in STAGES:
            ffn(1)
        for nb in range(NB):
            for c4 in range(2):
                bk = rotP.get()
                for c in range(4):
                    tr(bk.a[:, c * 128:(c + 1) * 128], xT.a[:, c4 * 4 + c, nb * 128:(nb + 1) * 128], ident_f, [xT.c(c4 * 4 + c), cst], [bk])
                copy_any(xio.a[:, nb, c4 * 512:(c4 + 1) * 512], bk.a[:, 0:512], [bk], [xio])
        dst = y_d[ti * NT:(ti + 1) * NT, :].rearrange("(n p) d -> p n d", p=128)
        stores.append(P.dma("pool", lambda e, dst=dst: e.dma_start(out=dst, in_=xio.a), reads=[xio]))

    stores.append(P.dma("sp", lambda e: e.dma_start(out=st_out[:, 0:516], in_=Cn.a.rearrange("p h v -> p (h v)")), reads=[Cn]))
    stores.append(P.dma("sp", lambda e: e.dma_start(out=st_out[:, 516:548], in_=rx.a.rearrange("p a m -> p (a m)")), reads=[rx]))
    stores.append(P.dma("sp", lambda e: e.dma_start(out=st_out[:, 548:1572], in_=S.a.rearrange("p h v -> p (h v)")), reads=[S]))
    stores.append(P.dma("sp", lambda e: e.dma_start(out=st_out[:, 1572:1584], in_=halo.a.rearrange("p h k -> p (h k)")), reads=[halo]))
    P.emit(final_wait_ops=stores[-12:])
    P.close()
    return nc


N_LAUNCH = 1
_CACHE = {}


def _get_prog(n_tiles):
    if n_tiles not in _CACHE:
        _CACHE[n_tiles] = build_program(n_tiles)
    return _CACHE[n_tiles]


def run_sequence_chunks(inputs, x_seqs, n_launch=1):
    L = x_seqs[0].shape[0]
    per = L // n_launch
    n_tiles = per // NT
    prm = pack_params(inputs)
    cst = make_consts()
    shared = {
        "prm": prm, "cst": cst,
        "ab_w_in": np.ascontiguousarray(inputs["ab_w_in"][0], np.float32),
        "ab_w_glu": np.ascontiguousarray(inputs["ab_w_glu"][0], np.float32),
        "ab_w_out": np.ascontiguousarray(inputs["ab_w_out"][0], np.float32),
        "c_w_in": np.ascontiguousarray(inputs["c_w_in"][0], np.float32),
        "c_w_out": np.ascontiguousarray(inputs["c_w_out"][0], np.float32),
        "ffn_w1": np.ascontiguousarray(inputs["ffn_w1"], np.float32),
        "ffn_w3": np.ascontiguousarray(inputs["ffn_w3"], np.float32),
        "ffn_w2": np.ascontiguousarray(inputs["ffn_w2"], np.float32),
    }
    ncore = len(x_seqs)
    states = [np.zeros((128, NSTATE), np.float32) for _ in range(ncore)]
    outs = [[] for _ in range(ncore)]
    for li in range(n_launch):
        nc = _get_prog(n_tiles)
        in_maps = []
        for c in range(ncore):
            xs = x_seqs[c % len(x_seqs)]
            m = dict(shared)
            m["x"] = np.ascontiguousarray(xs[li * per:(li + 1) * per], np.float32)
            m["st_in"] = states[c]
            in_maps.append(m)
        res = run_bass_kernel_spmd(nc, in_maps, core_ids=list(range(ncore)))
        for c in range(ncore):
            outs[c].append(np.asarray(res.results[c]["y"], np.float32))
            states[c] = np.asarray(res.results[c]["st_out"], np.float32)
    return [np.concatenate(outs[c], axis=0) for c in range(len(x_seqs))]


def kernel(**inputs):
    x = np.asarray(inputs["x"], np.float32)
    seqs = [x[b] for b in range(x.shape[0])]
    ys = run_sequence_chunks(inputs, seqs, n_launch=N_LAUNCH)
    return np.stack(ys, axis=0).astype(np.float32)
```
